# Optimizing a Trainium2 kernel written in Bass

```python
import math
import jax, jax.numpy as jnp
from jax import lax
import numpy as np

D_MODEL = 1024
BATCH = 2
SEQ = 8192
DEPTH = 4

PLE_DIM = 256
SB_HEADS = 8
SB_HEAD_DIM = 64
SB_WIDTH = SB_HEADS * SB_HEAD_DIM
RW_HEADS = 8
RW_HEAD_DIM = 64
RW_WIDTH = RW_HEADS * RW_HEAD_DIM
RW_DECAY_LORA = 64
RW_AAA_LORA = 64
RW_GATE_LORA = 160
RW_GN_EPS = 64e-5
RW_IN = 3 * RW_WIDTH + RW_DECAY_LORA + RW_AAA_LORA + RW_GATE_LORA
HYB_IN = 3 * SB_WIDTH + RW_IN
HYB_OUT = SB_WIDTH + RW_WIDTH
MLA_HEADS = 16
MLA_NOPE = 64
MLA_ROPE = 32
MLA_V = 64
MLA_Q_RANK = 384
MLA_KV_RANK = 256
MLA_DOWN = MLA_Q_RANK + MLA_KV_RANK + MLA_ROPE
ROPE_THETA = 10000.0
FFN_DIM = 2816
CONV_WIDTH = 3
Q_BLOCK = 128
NORM_EPS = 1e-6
N_EVEN = (DEPTH + 1) // 2
N_ODD = DEPTH // 2

kernel_name = "hybrid_sb_rwkv7_mla_convglu"

F32 = jnp.float32


def rmsnorm(x, g):
    xf = x.astype(F32)
    y = xf * lax.rsqrt(jnp.mean(xf * xf, axis=-1, keepdims=True) + NORM_EPS)
    return (y * g.astype(F32)).astype(x.dtype)


def shift_right(x, n):
    return jnp.pad(x, ((0, 0), (n, 0), (0, 0)))[:, : x.shape[1]]


def sweep_query_blocks(fn, q):
    b, s, h, e = q.shape
    nb = s // Q_BLOCK
    qb = q.reshape(b, nb, Q_BLOCK, h, e).transpose(1, 0, 2, 3, 4)
    starts = jnp.arange(nb, dtype=jnp.int32) * Q_BLOCK
    out = lax.map(lambda args: fn(args[0], args[1]), (qb, starts))
    return out.transpose(1, 0, 2, 3, 4).reshape(b, s, h, out.shape[-1])


def stick_breaking_attention(q, k, v):
    s_len = k.shape[1]
    scale = 1.0 / math.sqrt(q.shape[-1])
    key_pos = jnp.arange(s_len, dtype=jnp.int32)
    kf = k.astype(F32)
    vf = v.astype(F32)

    def block(qb, t0):
        z = jnp.einsum('bqhe,bshe->bhqs', qb.astype(F32), kf) * scale
        qpos = t0 + jnp.arange(Q_BLOCK, dtype=jnp.int32)
        mask = key_pos[None, :] < qpos[:, None]
        log_one_minus = jnp.where(mask, -jax.nn.softplus(z), 0.0)
        rev = lax.cumsum(log_one_minus, axis=3, reverse=True)
        after = jnp.concatenate([rev[..., 1:], jnp.zeros_like(rev[..., :1])], axis=3)
        w = jnp.where(mask, jnp.exp(jax.nn.log_sigmoid(z) + after), 0.0)
        return jnp.einsum('bhqs,bshe->bqhe', w, vf)

    return sweep_query_blocks(block, q).astype(v.dtype)


def causal_softmax_attention(q, k, v, scale):
    s_len = k.shape[1]
    key_pos = jnp.arange(s_len, dtype=jnp.int32)
    kf = k.astype(F32)
    vf = v.astype(F32)

    def block(qb, t0):
        sc = jnp.einsum('bqhe,bshe->bhqs', qb.astype(F32), kf) * scale
        qpos = t0 + jnp.arange(Q_BLOCK, dtype=jnp.int32)
        mask = key_pos[None, :] <= qpos[:, None]
        w = jax.nn.softmax(jnp.where(mask, sc, -jnp.inf), axis=-1)
        return jnp.einsum('bhqs,bshe->bqhe', w, vf)

    return sweep_query_blocks(block, q).astype(v.dtype)


def apply_rope(x, positions):
    half = x.shape[-1] // 2
    inv_freq = ROPE_THETA ** (-jnp.arange(half, dtype=F32) / half)
    ang = positions.astype(F32)[:, :, None, None] * inv_freq
    cos, sin = jnp.cos(ang), jnp.sin(ang)
    xf = x.astype(F32)
    x1, x2 = xf[..., :half], xf[..., half:]
    return jnp.concatenate([x1 * cos - x2 * sin, x1 * sin + x2 * cos], axis=-1).astype(x.dtype)


def rwkv7_time_mix(proj, mu, w0, w2, a0, a2, g2, k_k, k_a, r_k, ln_w, ln_b):
    b, s, _ = proj.shape
    pf = proj.astype(F32)
    xm = pf + (shift_right(pf, 1) - pf) * mu
    r, k, v, wl, al, gl = jnp.split(
        xm, [RW_WIDTH, 2 * RW_WIDTH, 3 * RW_WIDTH, 3 * RW_WIDTH + RW_DECAY_LORA,
             3 * RW_WIDTH + RW_DECAY_LORA + RW_AAA_LORA], axis=-1)
    w = -jax.nn.softplus(-(w0 + jnp.tanh(wl) @ w2)) - 0.5
    decay = jnp.exp(-jnp.exp(w))
    a = jax.nn.sigmoid(a0 + al @ a2)
    g = jax.nn.sigmoid(gl) @ g2

    hd = lambda t: t.reshape(b, s, RW_HEADS, RW_HEAD_DIM)
    ph = lambda t: t.astype(F32).reshape(RW_HEADS, RW_HEAD_DIM)
    r, k, v, decay, a = hd(r), hd(k), hd(v), hd(decay), hd(a)
    kk = k * ph(k_k)
    kk = kk * lax.rsqrt(jnp.maximum(jnp.sum(kk * kk, axis=-1, keepdims=True), 1e-24))
    k = k * (1.0 + (a - 1.0) * ph(k_a))

    def step(state, inp):
        r_t, w_t, k_t, v_t, kk_t, a_t = inp
        sa = jnp.einsum('bhij,bhj->bhi', state, -kk_t)
        state = (state * w_t[:, :, None, :] + sa[..., None] * (kk_t * a_t)[:, :, None, :]
                 + v_t[..., None] * k_t[:, :, None, :])
        return state, jnp.einsum('bhij,bhj->bhi', state, r_t)

    tm = lambda t: jnp.moveaxis(t, 1, 0)
    state0 = jnp.zeros((b, RW_HEADS, RW_HEAD_DIM, RW_HEAD_DIM), F32)
    _, y = lax.scan(step, state0, (tm(r), tm(decay), tm(k), tm(v), tm(kk), tm(a)))
    y = jnp.moveaxis(y, 0, 1)

    mean = jnp.mean(y, axis=-1, keepdims=True)
    var = jnp.mean(jnp.square(y - mean), axis=-1, keepdims=True)
    y = (y - mean) * lax.rsqrt(var + RW_GN_EPS)
    y = y * ph(ln_w) + ph(ln_b)
    y = y + jnp.sum(r * k * r_k.astype(F32), axis=-1, keepdims=True) * v
    return (y.reshape(b, s, RW_WIDTH) * g).astype(proj.dtype)


def sb_rwkv_mixer(hn, w_in, w_out, mu, w0, w2, a0, a2, g2, k_k, k_a, r_k, ln_w, ln_b):
    b, s, _ = hn.shape
    proj = hn @ w_in
    qa, ka, va, rw = jnp.split(proj, [SB_WIDTH, 2 * SB_WIDTH, 3 * SB_WIDTH], axis=-1)
    hd = lambda t: t.reshape(b, s, SB_HEADS, SB_HEAD_DIM)
    o_a = stick_breaking_attention(hd(qa), hd(ka), hd(va)).reshape(b, s, SB_WIDTH)
    o_b = rwkv7_time_mix(rw, mu, w0, w2, a0, a2, g2, k_k, k_a, r_k, ln_w, ln_b)
    return jnp.concatenate([o_a, o_b.astype(o_a.dtype)], axis=-1) @ w_out


def mla_mixer(hn, positions, w_down, q_norm, kv_norm, w_uq, w_ukv, w_o):
    b, s, _ = hn.shape
    c = hn @ w_down
    cq, ckv, kr = jnp.split(c, [MLA_Q_RANK, MLA_Q_RANK + MLA_KV_RANK], axis=-1)
    q = (rmsnorm(cq, q_norm) @ w_uq).reshape(b, s, MLA_HEADS, MLA_NOPE + MLA_ROPE)
    kv = (rmsnorm(ckv, kv_norm) @ w_ukv).reshape(b, s, MLA_HEADS, MLA_NOPE + MLA_V)
    q_nope, q_rope = jnp.split(q, [MLA_NOPE], axis=-1)
    k_nope, v = jnp.split(kv, [MLA_NOPE], axis=-1)
    q = jnp.concatenate([q_nope, apply_rope(q_rope, positions)], axis=-1)
    k_rope = jnp.broadcast_to(apply_rope(kr[:, :, None, :], positions), (b, s, MLA_HEADS, MLA_ROPE))
    k = jnp.concatenate([k_nope, k_rope], axis=-1)
    o = causal_softmax_attention(q, k, v, 1.0 / math.sqrt(MLA_NOPE + MLA_ROPE))
    return o.reshape(b, s, MLA_HEADS * MLA_V) @ w_o


def conv_glu(hn, w_in, conv_w, conv_b, w_out):
    u, gate_in = jnp.split(hn @ w_in, 2, axis=-1)
    c = (conv_w[0] * shift_right(gate_in, 2) + conv_w[1] * shift_right(gate_in, 1)
         + conv_w[2] * gate_in + conv_b)
    return (jax.nn.gelu(c, approximate=False) * u) @ w_out


def setup_inputs(seed: int = 0) -> dict:
    key = jax.random.key(seed)
    keys = jax.random.split(key, 48)
    counter = [0]

    def nk():
        kk = keys[counter[0]]
        counter[0] += 1
        return kk

    def nrm(shape, scale):
        return jax.random.normal(nk(), shape, F32) * scale

    def gain(shape):
        return 1.0 + nrm(shape, 0.05)

    x = nrm((BATCH, SEQ, D_MODEL), 1.0)
    p = nrm((DEPTH, BATCH, SEQ, PLE_DIM), 1.0)
    offset = jax.random.randint(nk(), (BATCH, 1), 0, 4096, dtype=jnp.int32)
    positions = offset + jnp.arange(SEQ, dtype=jnp.int32)[None, :]
    return {
        "x": x,
        "p": p,
        "positions": positions,
        "attn_norm": gain((DEPTH, D_MODEL)),
        "ffn_norm": gain((DEPTH, D_MODEL)),
        "ffn_w_in": nrm((DEPTH, D_MODEL, 2 * FFN_DIM), D_MODEL ** -0.5),
        "ffn_conv_w": nrm((DEPTH, CONV_WIDTH, FFN_DIM), CONV_WIDTH ** -0.5),
        "ffn_conv_b": nrm((DEPTH, FFN_DIM), 0.01),
        "ffn_w_out": nrm((DEPTH, FFN_DIM, D_MODEL), FFN_DIM ** -0.5),
        "ple_w_proj": nrm((DEPTH, PLE_DIM, D_MODEL), PLE_DIM ** -0.5),
        "ple_norm": gain((DEPTH, D_MODEL)),
        "ple_gate_norm": gain((DEPTH, D_MODEL)),
        "ple_w_gate": nrm((DEPTH, D_MODEL, D_MODEL), D_MODEL ** -0.5),
        "hyb_w_in": nrm((N_EVEN, D_MODEL, HYB_IN), D_MODEL ** -0.5),
        "hyb_w_out": nrm((N_EVEN, HYB_OUT, D_MODEL), HYB_OUT ** -0.5),
        "rw_mu": jax.random.uniform(nk(), (N_EVEN, RW_IN), F32),
        "rw_w0": jax.random.uniform(nk(), (N_EVEN, RW_WIDTH), F32, -6.0, 1.0),
        "rw_w2": nrm((N_EVEN, RW_DECAY_LORA, RW_WIDTH), RW_DECAY_LORA ** -0.5),
        "rw_a0": nrm((N_EVEN, RW_WIDTH), 0.1),
        "rw_a2": nrm((N_EVEN, RW_AAA_LORA, RW_WIDTH), RW_AAA_LORA ** -0.5),
        "rw_g2": nrm((N_EVEN, RW_GATE_LORA, RW_WIDTH), RW_GATE_LORA ** -0.5),
        "rw_k_k": 0.85 + nrm((N_EVEN, RW_WIDTH), 0.05),
        "rw_k_a": gain((N_EVEN, RW_WIDTH)),
        "rw_r_k": nrm((N_EVEN, RW_HEADS, RW_HEAD_DIM), 0.1),
        "rw_ln_w": gain((N_EVEN, RW_WIDTH)),
        "rw_ln_b": nrm((N_EVEN, RW_WIDTH), 0.01),
        "mla_w_down": nrm((N_ODD, D_MODEL, MLA_DOWN), D_MODEL ** -0.5),
        "mla_q_norm": gain((N_ODD, MLA_Q_RANK)),
        "mla_kv_norm": gain((N_ODD, MLA_KV_RANK)),
        "mla_w_uq": nrm((N_ODD, MLA_Q_RANK, MLA_HEADS * (MLA_NOPE + MLA_ROPE)), MLA_Q_RANK ** -0.5),
        "mla_w_ukv": nrm((N_ODD, MLA_KV_RANK, MLA_HEADS * (MLA_NOPE + MLA_V)), MLA_KV_RANK ** -0.5),
        "mla_w_o": nrm((N_ODD, MLA_HEADS * MLA_V, D_MODEL), (MLA_HEADS * MLA_V) ** -0.5),
        "final_norm": gain((D_MODEL,)),
    }


def reference(x, p, positions, attn_norm, ffn_norm, ffn_w_in, ffn_conv_w, ffn_conv_b, ffn_w_out,
              ple_w_proj, ple_norm, ple_gate_norm, ple_w_gate,
              hyb_w_in, hyb_w_out, rw_mu, rw_w0, rw_w2, rw_a0, rw_a2, rw_g2, rw_k_k, rw_k_a,
              rw_r_k, rw_ln_w, rw_ln_b,
              mla_w_down, mla_q_norm, mla_kv_norm, mla_w_uq, mla_w_ukv, mla_w_o, final_norm):
    h = x
    for i in range(DEPTH):
        j = i // 2
        hn = rmsnorm(h, attn_norm[i])
        if i % 2 == 0:
            mix = sb_rwkv_mixer(hn, hyb_w_in[j], hyb_w_out[j], rw_mu[j], rw_w0[j], rw_w2[j],
                                rw_a0[j], rw_a2[j], rw_g2[j], rw_k_k[j], rw_k_a[j], rw_r_k[j],
                                rw_ln_w[j], rw_ln_b[j])
        else:
            mix = mla_mixer(hn, positions, mla_w_down[j], mla_q_norm[j], mla_kv_norm[j],
                            mla_w_uq[j], mla_w_ukv[j], mla_w_o[j])
        h = h + mix
        h = h + conv_glu(rmsnorm(h, ffn_norm[i]), ffn_w_in[i], ffn_conv_w[i], ffn_conv_b[i], ffn_w_out[i])
        e = rmsnorm(p[i] @ ple_w_proj[i], ple_norm[i])
        gate = jax.nn.sigmoid(rmsnorm(h, ple_gate_norm[i]) @ ple_w_gate[i])
        h = h + gate * e
    return rmsnorm(h, final_norm)
```

```python
import numpy as np
from contextlib import ExitStack
import concourse.bass as bass
import concourse.mybir as mybir
from concourse.bass_utils import run_bass_kernel_spmd

F32, BF16, I32 = mybir.dt.float32, mybir.dt.bfloat16, mybir.dt.int32
AF = mybir.ActivationFunctionType
ALU = mybir.AluOpType
SAME_ENG_SYNC = True


class Tk:
    __slots__ = ("h", "w", "r", "name")

    def __init__(self, h, name=""):
        self.h = h
        self.w = None
        self.r = {}
        self.name = name

    def __getitem__(self, idx):
        return self.h[idx]


class SubTk:
    __slots__ = ("h", "p", "name")

    def __init__(self, parent, ap, name=""):
        self.h = ap
        self.p = parent
        self.name = name

    def __getitem__(self, idx):
        return self.h[idx]

    @property
    def w(self):
        return self.p.w

    @w.setter
    def w(self, v):
        self.p.w = v

    @property
    def r(self):
        return self.p.r

    @r.setter
    def r(self, v):
        self.p.r = v


class Ctx:
    def __init__(self, nc, stack, n_dma_sems=48):
        self.nc = nc
        self.engs = {"pe": nc.tensor, "act": nc.scalar, "dve": nc.vector, "pool": nc.gpsimd, "sp": nc.sync}
        self.esem = {k: stack.enter_context(nc.semaphore("s_" + k)) for k in ("pe", "act", "dve", "pool")}
        self.ecnt = {k: 0 for k in self.esem}
        self.dsem = [stack.enter_context(nc.semaphore(f"d{i}")) for i in range(n_dma_sems)]
        self.dcnt = [0] * n_dma_sems
        self.dnext = 0
        self.known = {k: {} for k in self.engs}
        self.nid = 0

    def _sem(self, k):
        return self.esem[k[1]] if k[0] == "e" else self.dsem[k[1]]

    def _wait(self, eng, events):
        need = {}
        for ev in events:
            if ev is None:
                continue
            k, v = ev
            if k[0] == "e" and k[1] == eng and (eng == "pe" or not SAME_ENG_SYNC):
                continue
            if need.get(k, 0) < v:
                need[k] = v
        kn = self.known[eng]
        for k, v in need.items():
            if kn.get(k, 0) >= v:
                continue
            self.engs[eng].wait_ge(self._sem(k), v)
            kn[k] = v

    @staticmethod
    def _deps(r, w):
        ev = []
        for t in r:
            ev.append(t.w)
        for t in w:
            ev.append(t.w)
            ev.extend(t.r.items())
        return ev

    @staticmethod
    def _mark(ev, r, w):
        k, v = ev
        for t in r:
            if t.r.get(k, 0) < v:
                t.r[k] = v
        for t in w:
            t.w = ev
            t.r = {}

    def op(self, eng, fn, r=(), w=()):
        self._wait(eng, self._deps(r, w))
        ins = fn(self.engs[eng])
        self.ecnt[eng] += 1
        ins.then_inc(self.esem[eng], 1)
        ev = (("e", eng), self.ecnt[eng])
        self._mark(ev, r, w)
        return ev

    def dma(self, eng, out, in_, r=(), w=()):
        i = self.dnext
        self.dnext = (i + 1) % len(self.dsem)
        deps = self._deps(r, w)
        if self.dcnt[i] > 0:
            deps.append((("d", i), self.dcnt[i]))
        self._wait(eng, deps)
        ins = self.engs[eng].dma_start(out=out, in_=in_)
        self.dcnt[i] += 16
        ins.then_inc(self.dsem[i], 16)
        ev = (("d", i), self.dcnt[i])
        self._mark(ev, r, w)
        return ev

    def finish(self, eng="sp"):
        evs = [(("d", i), c) for i, c in enumerate(self.dcnt) if c > 0]
        evs += [(("e", k), c) for k, c in self.ecnt.items() if c > 0]
        self._wait(eng, evs)

    def sb(self, shape, dt, name=None):
        self.nid += 1
        name = f"{name or 't'}_{self.nid}"
        return Tk(self.nc.alloc_sbuf_tensor(name, list(shape), dt), name)

    def ps(self, shape, dt=F32, name=None):
        self.nid += 1
        name = f"{name or 'p'}_{self.nid}"
        return Tk(self.nc.alloc_psum_tensor(name, list(shape), dt), name)

    def dram(self, name, shape, dt, kind="Internal"):
        return Tk(self.nc.dram_tensor(name, list(shape), dt, kind=kind).ap(), name)


def perm_down(w):
    out = np.zeros((w.shape[0], 7 * 128), w.dtype)
    out[:, :640] = w[:, :640]
    out[:, 640:656] = w[:, 640:656]
    out[:, 768:784] = w[:, 656:672]
    return out
def perm_uq(w):
    w3 = w.reshape(w.shape[0], 16, 96)
    return np.concatenate([w3[:, :, :64].reshape(-1, 1024), w3[:, :, 64:80].reshape(-1, 256), w3[:, :, 80:96].reshape(-1, 256)], axis=1)
_f = (10000.0 ** (-np.arange(16, dtype=np.float32) / 16)).astype(np.float32)
INVF = np.tile(_f, 8).reshape(128, 1).astype(np.float32)


NORM_EPS = 1e-6
D_MODEL = 1024
FFN = 2816
NFC = FFN // 128


def blocked(w):
    K, N = w.shape
    nb = (N + 127) // 128
    if N % 128:
        w = np.concatenate([w, np.zeros((K, nb * 128 - N), w.dtype)], axis=1)
    kc = K // 128
    return np.ascontiguousarray(w.reshape(kc, 128, nb, 128).transpose(2, 1, 0, 3))


def chunked_vec(v):
    return np.ascontiguousarray(v.reshape(-1, 128).T)


class TokEnv:
    def __init__(self, ctx, T):
        self.c = ctx
        self.T = T
        c = ctx
        self.ones = c.sb([128, 128], BF16, "ones")
        c.op("dve", lambda e: e.memset(self.ones[:], 1.0), w=[self.ones])
        self.ps_ss = c.ps([128, 512], F32, "ps_ss")
        self.ps_mm = [c.ps([128, 512], F32, "ps_mm") for _ in range(4)]
        self.mm_i = 0
        self.wbuf = {}
        self.sq = [c.sb([128, 512], BF16, "sq") for _ in range(2)]
        self.sq_i = 0
        self.rs = c.sb([128, 512], F32, "rs")

    def next_ps(self):
        p = self.ps_mm[self.mm_i % 4]
        self.mm_i += 1
        return p

    def wtile(self, KC):
        if KC not in self.wbuf:
            self.wbuf[KC] = [[self.c.sb([128, KC, 128], BF16, f"wb{KC}") for _ in range(3)], 0]
        ent = self.wbuf[KC]
        t = ent[0][ent[1] % 3]
        ent[1] += 1
        return t

    def load_w(self, Wd, n, KC):
        wt = self.wtile(KC)
        self.c.dma("pool", wt[:], Wd[n], r=[Wd], w=[wt])
        return wt

    def rstd(self, xs, D, T):
        c = self.c
        KC = len(xs)
        for k, (xt, xap) in enumerate(xs):
            sq = self.sq[self.sq_i % 2]
            self.sq_i += 1
            c.op("act", lambda e, sq=sq, xap=xap: e.activation(out=sq[:, :T], in_=xap, func=AF.Square), r=[xt], w=[sq])
            c.op("pe", lambda e, sq=sq, k=k: e.matmul(self.ps_ss[:, :T], lhsT=self.ones[:], rhs=sq[:, :T], start=(k == 0), stop=(k == KC - 1)),
                 r=[sq, self.ones], w=[self.ps_ss])
        c.op("act", lambda e: e.activation(out=self.rs[:, :T], in_=self.ps_ss[:, :T], func=AF.Ln, bias=float(D * NORM_EPS), scale=1.0), r=[self.ps_ss], w=[self.rs])
        c.op("act", lambda e: e.activation(out=self.rs[:, :T], in_=self.rs[:, :T], func=AF.Exp, scale=-0.5), r=[self.rs], w=[self.rs])
        return self.rs

    def normed(self, xs, D, g, xn, T):
        c = self.c
        rs = self.rstd(xs, D, T)
        for k, (xt, xap) in enumerate(xs):
            c.op("dve", lambda e, k=k, xap=xap: e.scalar_tensor_tensor(out=xn[:, k, :T], in0=xap, scalar=g[:, k:k + 1], in1=rs[:, :T], op0=ALU.mult, op1=ALU.mult),
                 r=[xt, g, rs], w=[xn])

    def linear(self, Wd, KC, xn, T, nblocks, consume, xn_r=None):
        c = self.c
        for n in nblocks:
            wt = self.load_w(Wd, n, KC)
            ps = self.next_ps()
            for k in range(KC):
                c.op("pe", lambda e, k=k, wt=wt, ps=ps: e.matmul(ps[:, :T], lhsT=wt[:, k, :], rhs=xn[:, k, :T], start=(k == 0), stop=(k == KC - 1)),
                     r=[wt, xn], w=[ps])
            consume(n, ps)


def prep_gain(c, gd, KC, D, name):
    g = c.sb([128, KC], F32, name)
    c.dma("sp", g[:], gd[:, :], r=[gd], w=[g])
    c.op("dve", lambda e: e.tensor_scalar(out=g[:], in0=g[:], scalar1=float(np.sqrt(D)), scalar2=None, op0=ALU.mult), r=[g], w=[g])
    return g


def build_token_program(mix_in, nxt, NT=4, T=512, parts=("halo", "mo", "ffn", "ple")):
    nc = bass.Bass("TRN2", target_bir_lowering=False)
    st = ExitStack()
    c = Ctx(nc, st)
    HALO = 32
    TT = HALO + NT * T
    NR = NT * T
    EI, EO = "ExternalInput", "ExternalOutput"
    d = {}
    d["hT"] = c.dram("hT", [8, 128, TT], F32, EI)
    hTv = d["hT"].h.rearrange("k p t -> p k t")
    env = TokEnv(c, T)
    h = c.sb([128, 8, T], F32, "h")
    xn = c.sb([128, 8, T], BF16, "xn")
    stage = [c.sb([128, T], F32, "stage") for _ in range(3)]
    stage_i = [0]

    def next_stage():
        s = stage[stage_i[0] % 3]
        stage_i[0] += 1
        return s

    hch = lambda TT_: [(h, h[:, k, :TT_]) for k in range(8)]

    if mix_in:
        d["oT"] = c.dram("oT", [8, 128, TT], F32, EI)
        oTv = d["oT"].h.rearrange("k p t -> p k t")
        d["pT"] = c.dram("pT", [2, 128, TT], F32, EI)
        pTv = d["pT"].h.rearrange("k p t -> p k t")
        d["Wmo"] = c.dram("Wmo", [8, 128, 8, 128], F32, EI)
        d["Wfi"] = c.dram("Wfi", [44, 128, 8, 128], F32, EI)
        d["Wfo"] = c.dram("Wfo", [8, 128, 22, 128], F32, EI)
        d["Wpp"] = c.dram("Wpp", [8, 128, 2, 128], F32, EI)
        d["Wpg"] = c.dram("Wpg", [8, 128, 8, 128], F32, EI)
        d["g_ffn"] = c.dram("g_ffn", [128, 8], F32, EI)
        d["g_pe"] = c.dram("g_pe", [128, 8], F32, EI)
        d["g_pg"] = c.dram("g_pg", [128, 8], F32, EI)
        d["convw"] = c.dram("convw", [128, 3, NFC], F32, EI)
        d["convb"] = c.dram("convb", [128, NFC], F32, EI)
        d["hT_out"] = c.dram("hT_out", [8, 128, NR], F32, EO)
        hOv = d["hT_out"].h.rearrange("k p t -> p k t")
        g_ffn = prep_gain(c, d["g_ffn"], 8, 1024, "g_ffn")
        g_pe = prep_gain(c, d["g_pe"], 8, 1024, "g_pe")
        g_pg = prep_gain(c, d["g_pg"], 8, 1024, "g_pg")
        convw = c.sb([128, 3, NFC], F32, "convw")
        convb = c.sb([128, NFC], F32, "convb")
        c.dma("sp", convw[:], d["convw"][:, :, :], r=[d["convw"]], w=[convw])
        c.dma("sp", convb[:], d["convb"][:, :], r=[d["convb"]], w=[convb])
        o_bf = c.sb([128, 8, T], BF16, "o_bf")
        p_bf = c.sb([128, 2, T], BF16, "p_bf")
        act = c.sb([128, NFC, T], BF16, "act")
        e_sb = c.sb([128, 8, T], F32, "e_sb")
        rs_e = c.sb([128, T], F32, "rs_e")
        carry = c.sb([128, NFC, 2], F32, "carry")
        gate_sb = [c.sb([128, T + 2], F32, "gate_sb") for _ in range(2)]
        ctmp = [c.sb([128, T], F32, "ctmp") for _ in range(2)]
        ge = [c.sb([128, T], F32, "ge") for _ in range(2)]
        sg = [c.sb([128, T], F32, "sg") for _ in range(2)]
        t1 = [c.sb([128, T], F32, "t1") for _ in range(2)]

    if nxt == "even":
        d["Whyb"] = c.dram("Whyb", [27, 128, 8, 128], F32, EI)
        d["g_attn"] = c.dram("g_attn", [128, 8], F32, EI)
        d["projT"] = c.dram("projT", [27, 128, NR], F32, EO)
        g_attn = prep_gain(c, d["g_attn"], 8, 1024, "g_attn")
    elif nxt == "odd":
        d["Wdn"] = c.dram("Wdn", [7, 128, 8, 128], F32, EI)
        d["Wuq"] = c.dram("Wuq", [12, 128, 3, 128], F32, EI)
        d["Wukv"] = c.dram("Wukv", [16, 128, 2, 128], F32, EI)
        d["g_attn"] = c.dram("g_attn", [128, 8], F32, EI)
        d["g_q"] = c.dram("g_q", [128, 3], F32, EI)
        d["g_kv"] = c.dram("g_kv", [128, 2], F32, EI)
        d["pos"] = c.dram("pos", [128, NR], I32, EI)
        d["invf"] = c.dram("invf", [128, 1], F32, EI)
        d["qT"] = c.dram("qT", [12, 128, NR], F32, EO)
        d["kvT"] = c.dram("kvT", [16, 128, NR], F32, EO)
        d["krT"] = c.dram("krT", [2, 16, NR], F32, EO)
        g_attn = prep_gain(c, d["g_attn"], 8, 1024, "g_attn")
        g_q = prep_gain(c, d["g_q"], 3, 384, "g_q")
        g_kv = prep_gain(c, d["g_kv"], 2, 256, "g_kv")
        invf = c.sb([128, 1], F32, "invf")
        c.dma("sp", invf[:], d["invf"][:, :], r=[d["invf"]], w=[invf])
        c_sb = c.sb([128, 7, T], F32, "c_sb")
        cn = c.sb([128, 3, T], BF16, "cn")
        pos_i = c.sb([128, T], I32, "pos_i")
        ang = c.sb([128, T], F32, "ang")
        tmpr = c.sb([128, T], F32, "tmpr")
        cosT = c.sb([128, T], F32, "cosT")
        sinT = c.sb([128, T], F32, "sinT")
        qr = c.sb([128, 4, T], F32, "qr")
        negpi = c.sb([128, 1], F32, "negpi")
        c.op("dve", lambda e: e.memset(negpi[:], -float(np.pi)), w=[negpi])
    elif nxt == "final":
        d["g_fin"] = c.dram("g_fin", [128, 8], F32, EI)
        d["yT"] = c.dram("yT", [8, 128, NR], F32, EO)
        g_fin = prep_gain(c, d["g_fin"], 8, 1024, "g_fin")

    def add_into_h(TT_):
        def f(n, ps):
            c.op("dve", lambda e: e.tensor_tensor(out=h[:, n, :TT_], in0=h[:, n, :TT_], in1=ps[:, :TT_], op=ALU.add), r=[ps, h], w=[h])
        return f

    def mixer_out(col0, TT_):
        c.dma("pool", o_bf[:, :, :TT_], oTv[:, :, col0:col0 + TT_], r=[d["oT"]], w=[o_bf])
        env.linear(d["Wmo"], 8, o_bf, TT_, range(8), add_into_h(TT_))

    def ffn_in(TT_, halo):
        env.normed(hch(TT_), 1024, g_ffn, xn, TT_)
        for fc in range(NFC):
            wg = env.load_w(d["Wfi"], NFC + fc, 8)
            ps_g = env.next_ps()
            for k in range(8):
                c.op("pe", lambda e: e.matmul(ps_g[:, :TT_], lhsT=wg[:, k, :], rhs=xn[:, k, :TT_], start=(k == 0), stop=(k == 7)), r=[wg, xn], w=[ps_g])
            if halo:
                c.op("act", lambda e: e.activation(out=carry[:, fc, :], in_=ps_g[:, TT_ - 2:TT_], func=AF.Copy), r=[ps_g], w=[carry])
                continue
            wu = env.load_w(d["Wfi"], fc, 8)
            ps_u = env.next_ps()
            for k in range(8):
                c.op("pe", lambda e: e.matmul(ps_u[:, :TT_], lhsT=wu[:, k, :], rhs=xn[:, k, :TT_], start=(k == 0), stop=(k == 7)), r=[wu, xn], w=[ps_u])
            gs = gate_sb[fc % 2]
            ct = ctmp[fc % 2]
            gg = ge[fc % 2]
            c.op("act", lambda e: e.activation(out=gs[:, 0:2], in_=carry[:, fc, :], func=AF.Copy), r=[carry], w=[gs])
            c.op("act", lambda e: e.activation(out=gs[:, 2:2 + TT_], in_=ps_g[:, :TT_], func=AF.Copy), r=[ps_g], w=[gs])
            c.op("act", lambda e: e.activation(out=carry[:, fc, :], in_=gs[:, TT_:TT_ + 2], func=AF.Copy), r=[gs], w=[carry])
            c.op("dve", lambda e: e.tensor_scalar(out=ct[:, :TT_], in0=gs[:, 2:2 + TT_], scalar1=convw[:, 2, fc:fc + 1], scalar2=convb[:, fc:fc + 1], op0=ALU.mult, op1=ALU.add),
                 r=[gs, convw, convb], w=[ct])
            c.op("dve", lambda e: e.scalar_tensor_tensor(out=ct[:, :TT_], in0=gs[:, 1:1 + TT_], scalar=convw[:, 1, fc:fc + 1], in1=ct[:, :TT_], op0=ALU.mult, op1=ALU.add),
                 r=[gs, convw, ct], w=[ct])
            c.op("dve", lambda e: e.scalar_tensor_tensor(out=ct[:, :TT_], in0=gs[:, 0:TT_], scalar=convw[:, 0, fc:fc + 1], in1=ct[:, :TT_], op0=ALU.mult, op1=ALU.add),
                 r=[gs, convw, ct], w=[ct])
            c.op("act", lambda e: e.activation(out=gg[:, :TT_], in_=ct[:, :TT_], func=AF.Gelu), r=[ct], w=[gg])
            c.op("dve", lambda e: e.tensor_tensor(out=act[:, fc, :TT_], in0=gg[:, :TT_], in1=ps_u[:, :TT_], op=ALU.mult), r=[gg, ps_u], w=[act])

    def ple(col0, TT_):
        c.dma("pool", p_bf[:, :, :TT_], pTv[:, :, col0:col0 + TT_], r=[d["pT"]], w=[p_bf])

        def ev_e(n, ps):
            c.op("act", lambda e: e.activation(out=e_sb[:, n, :TT_], in_=ps[:, :TT_], func=AF.Copy), r=[ps], w=[e_sb])
        env.linear(d["Wpp"], 2, p_bf, TT_, range(8), ev_e)
        rs = env.rstd([(e_sb, e_sb[:, k, :TT_]) for k in range(8)], 1024, TT_)
        c.op("act", lambda e: e.activation(out=rs_e[:, :TT_], in_=rs[:, :TT_], func=AF.Copy), r=[rs], w=[rs_e])
        env.normed(hch(TT_), 1024, g_pg, xn, TT_)

        def ev_g(n, ps):
            s_ = sg[n % 2]
            t_ = t1[n % 2]
            c.op("act", lambda e: e.activation(out=s_[:, :TT_], in_=ps[:, :TT_], func=AF.Sigmoid), r=[ps], w=[s_])
            c.op("dve", lambda e: e.scalar_tensor_tensor(out=t_[:, :TT_], in0=e_sb[:, n, :TT_], scalar=g_pe[:, n:n + 1], in1=rs_e[:, :TT_], op0=ALU.mult, op1=ALU.mult),
                 r=[e_sb, g_pe, rs_e], w=[t_])
            c.op("dve", lambda e: e.tensor_tensor(out=t_[:, :TT_], in0=t_[:, :TT_], in1=s_[:, :TT_], op=ALU.mult), r=[t_, s_], w=[t_])
            c.op("dve", lambda e: e.tensor_tensor(out=h[:, n, :TT_], in0=h[:, n, :TT_], in1=t_[:, :TT_], op=ALU.add), r=[t_, h], w=[h])
        env.linear(d["Wpg"], 8, xn, TT_, range(8), ev_g)

    def store_rows(dst_tk, dst_ap, n_part=128):
        def f(n, ps):
            s = next_stage()
            c.op("act", lambda e: e.activation(out=s[:n_part, :T], in_=ps[:n_part, :T], func=AF.Copy), r=[ps], w=[s])
            c.dma("sp", dst_ap(n), s[:n_part, :T], r=[s], w=[dst_tk])
        return f

    def rope_tables(r0):
        c.dma("sp", pos_i[:], d["pos"][:, r0:r0 + T], r=[d["pos"]], w=[pos_i])
        c.op("dve", lambda e: e.tensor_copy(out=ang[:], in_=pos_i[:]), r=[pos_i], w=[ang])
        c.op("dve", lambda e: e.tensor_scalar(out=ang[:], in0=ang[:], scalar1=invf[:, 0:1], scalar2=None, op0=ALU.mult), r=[ang, invf], w=[ang])
        for (off, dst) in ((0.0, sinT), (float(np.pi / 2), cosT)):
            if off:
                c.op("dve", lambda e: e.tensor_scalar(out=ang[:], in0=ang[:], scalar1=off, scalar2=None, op0=ALU.add), r=[ang], w=[ang])
            c.op("dve", lambda e: e.tensor_scalar(out=pos_i[:], in0=ang[:], scalar1=float(1.0 / (2 * np.pi)), scalar2=None, op0=ALU.mult), r=[ang], w=[pos_i])
            c.op("dve", lambda e: e.tensor_copy(out=tmpr[:], in_=pos_i[:]), r=[pos_i], w=[tmpr])
            c.op("dve", lambda e: e.scalar_tensor_tensor(out=dst[:], in0=tmpr[:], scalar=-6.28125, in1=ang[:], op0=ALU.mult, op1=ALU.add), r=[tmpr, ang], w=[dst])
            c.op("dve", lambda e: e.scalar_tensor_tensor(out=dst[:], in0=tmpr[:], scalar=-0.0019353071795864769, in1=dst[:], op0=ALU.mult, op1=ALU.add), r=[tmpr, dst], w=[dst])
            c.op("act", lambda e: e.activation(out=dst[:], in_=dst[:], func=AF.Sin), r=[dst], w=[dst])

    def rope_apply(x1_tk, x1, x2, P, out1_ap, out2_ap, dst_tk):
        s1 = next_stage()
        s2 = next_stage()
        tm = next_stage()
        c.op("dve", lambda e: e.tensor_tensor(out=s1[:P, :], in0=x1, in1=cosT[:P, :], op=ALU.mult), r=[x1_tk, cosT], w=[s1])
        c.op("dve", lambda e: e.tensor_tensor(out=tm[:P, :], in0=x2, in1=sinT[:P, :], op=ALU.mult), r=[x1_tk, sinT], w=[tm])
        c.op("dve", lambda e: e.tensor_tensor(out=s1[:P, :], in0=s1[:P, :], in1=tm[:P, :], op=ALU.subtract), r=[s1, tm], w=[s1])
        c.op("dve", lambda e: e.tensor_tensor(out=s2[:P, :], in0=x1, in1=sinT[:P, :], op=ALU.mult), r=[x1_tk, sinT], w=[s2])
        c.op("dve", lambda e: e.tensor_tensor(out=tm[:P, :], in0=x2, in1=cosT[:P, :], op=ALU.mult), r=[x1_tk, cosT, s1], w=[tm])
        c.op("dve", lambda e: e.tensor_tensor(out=s2[:P, :], in0=s2[:P, :], in1=tm[:P, :], op=ALU.add), r=[s2, tm], w=[s2])
        c.dma("sp", out1_ap, s1[:P, :], r=[s1], w=[dst_tk])
        c.dma("sp", out2_ap, s2[:P, :], r=[s2], w=[dst_tk])

    if mix_in:
        if "halo" in parts:
            c.dma("sp", h[:, :, :HALO], hTv[:, :, 0:HALO], r=[d["hT"]], w=[h])
            mixer_out(0, HALO)
            ffn_in(HALO, True)
        else:
            c.op("dve", lambda e: e.memset(carry[:], 0.0), w=[carry])
    for it in range(NT):
        col0 = HALO + it * T
        r0 = it * T
        c.dma("sp", h[:, :, :T], hTv[:, :, col0:col0 + T], r=[d["hT"]], w=[h])
        if mix_in:
            if "mo" in parts:
                mixer_out(col0, T)
            if "ffn" in parts:
                ffn_in(T, False)
                env.linear(d["Wfo"], NFC, act, T, range(8), add_into_h(T))
            if "ple" in parts:
                ple(col0, T)
            c.dma("sp", hOv[:, :, r0:r0 + T], h[:, :, :T], r=[h], w=[d["hT_out"]])
        if nxt == "even":
            env.normed(hch(T), 1024, g_attn, xn, T)
            env.linear(d["Whyb"], 8, xn, T, range(27), store_rows(d["projT"], lambda n: d["projT"][n, :, r0:r0 + T]))
        elif nxt == "odd":
            env.normed(hch(T), 1024, g_attn, xn, T)

            def ev_c(n, ps):
                c.op("act", lambda e: e.activation(out=c_sb[:, n, :], in_=ps[:, :T], func=AF.Copy), r=[ps], w=[c_sb])
            env.linear(d["Wdn"], 8, xn, T, range(7), ev_c)
            rope_tables(r0)
            rope_apply(c_sb, c_sb[0:16, 5, :], c_sb[0:16, 6, :], 16, d["krT"][0, :, r0:r0 + T], d["krT"][1, :, r0:r0 + T], d["krT"])
            env.normed([(c_sb, c_sb[:, k, :]) for k in range(3)], 384, g_q, cn, T)

            def ev_q(n, ps):
                if n < 8:
                    store_rows(d["qT"], lambda n_: d["qT"][n_, :, r0:r0 + T])(n, ps)
                else:
                    c.op("act", lambda e: e.activation(out=qr[:, n - 8, :], in_=ps[:, :T], func=AF.Copy), r=[ps], w=[qr])
            env.linear(d["Wuq"], 3, cn, T, range(12), ev_q)
            for j in range(2):
                rope_apply(qr, qr[:, j, :], qr[:, 2 + j, :], 128, d["qT"][8 + j, :, r0:r0 + T], d["qT"][10 + j, :, r0:r0 + T], d["qT"])
            env.normed([(c_sb, c_sb[:, 3 + k, :]) for k in range(2)], 256, g_kv, cn, T)
            env.linear(d["Wukv"], 2, cn, T, range(16), store_rows(d["kvT"], lambda n: d["kvT"][n, :, r0:r0 + T]))
        elif nxt == "final":
            rs = env.rstd(hch(T), 1024, T)
            for k in range(8):
                s = next_stage()
                c.op("dve", lambda e: e.scalar_tensor_tensor(out=s[:, :T], in0=h[:, k, :T], scalar=g_fin[:, k:k + 1], in1=rs[:, :T], op0=ALU.mult, op1=ALU.mult),
                     r=[h, g_fin, rs], w=[s])
                c.dma("sp", d["yT"][k, :, r0:r0 + T], s[:, :T], r=[s], w=[d["yT"]])
    c.finish()
    return nc, st


def make_masks(c, strict):
    masks = []
    for dl in range(4):
        mf = c.sb([128, 512], F32, "maskf")
        c.op("pool", lambda e: e.memset(mf[:], 1.0), w=[mf])
        c.op("pool", lambda e: e.affine_select(out=mf[:], in_=mf[:], pattern=[[1, 512]], compare_op=(ALU.is_gt if strict else ALU.is_ge),
                                                fill=0.0, base=-128 * dl, channel_multiplier=-1), r=[mf], w=[mf])
        mb = c.sb([128, 512], BF16, "maskb")
        c.op("pool", lambda e: e.tensor_copy(out=mb[:], in_=mf[:]), r=[mf], w=[mb])
        masks.append(mb)
    return masks


def make_ident(c, dt=BF16, n=128):
    f = c.sb([128, n], F32, "identf")
    c.op("pool", lambda e: e.memset(f[:], 1.0), w=[f])
    c.op("pool", lambda e: e.affine_select(out=f[:], in_=f[:], pattern=[[-1, n]], compare_op=ALU.is_equal, fill=0.0, base=0, channel_multiplier=1), r=[f], w=[f])
    if dt == F32:
        return f
    b = c.sb([128, n], dt, "identb")
    c.op("pool", lambda e: e.tensor_copy(out=b[:], in_=f[:]), r=[f], w=[b])
    return b


def build_mla_program(S=8192, NH=4):
    nc = bass.Bass("TRN2", target_bir_lowering=False)
    st = ExitStack()
    c = Ctx(nc, st)
    EI, EO = "ExternalInput", "ExternalOutput"
    qd = c.dram("q", [NH, 96, S], F32, EI)
    kd = c.dram("k", [NH, 96, S], F32, EI)
    vd = c.dram("v", [NH, 64, S], F32, EI)
    od = c.dram("oT", [NH, 64, S], F32, EO)
    NB = S // 128
    NQ = S // 512
    scale = 1.0 / float(np.sqrt(96.0))
    masks = make_masks(c, strict=False)
    ident = make_ident(c, BF16)
    Q = [c.sb([96, S], BF16, "Q") for _ in range(2)]
    K = [c.sb([96, S], BF16, "K") for _ in range(2)]
    VT = [c.sb([64, S], BF16, "VT") for _ in range(2)]
    Va = [c.sb([128, NB, 128], BF16, "Va") for _ in range(2)]
    for va in Va:
        c.op("pool", lambda e: e.memset(va[:], 1.0), w=[va])
    ps_s = [c.ps([128, 512], F32, "ps_s") for _ in range(3)]
    ps_o = [c.ps([128, 512], F32, "ps_o") for _ in range(2)]
    ps_t = c.ps([128, 64], F32, "ps_t")
    P = [c.sb([128, 512], BF16, "P") for _ in range(3)]
    rec = [c.sb([64, 512], F32, "rec") for _ in range(2)]
    ob = [c.sb([64, 512], F32, "ob") for _ in range(2)]
    it = 0
    for h in range(NH):
        q, k, vt, va = Q[h % 2], K[h % 2], VT[h % 2], Va[h % 2]
        c.dma("pool", q[:], qd[h], r=[qd], w=[q])
        c.dma("pool", k[:], kd[h], r=[kd], w=[k])
        c.dma("pool", vt[:], vd[h], r=[vd], w=[vt])
        for jb in range(NB):
            c.op("pe", lambda e: e.matmul(ps_t[:, :], lhsT=vt[:, jb * 128:(jb + 1) * 128], rhs=ident[0:64, 0:64], start=True, stop=True), r=[vt, ident], w=[ps_t])
            c.op("dve", lambda e: e.tensor_copy(out=va[:, jb, 0:64], in_=ps_t[:, :]), r=[ps_t], w=[va])
        for qi in range(NQ):
            q0 = qi * 512
            po = ps_o[qi % 2]
            nkb = 4 * qi + 4
            for jb in range(nkb):
                pss = ps_s[it % 3]
                p = P[it % 3]
                it += 1
                c.op("pe", lambda e: e.matmul(pss[:, :], lhsT=k[:, jb * 128:(jb + 1) * 128], rhs=q[:, q0:q0 + 512], start=True, stop=True), r=[k, q], w=[pss])
                c.op("act", lambda e: e.activation(out=p[:], in_=pss[:], func=AF.Exp, scale=scale), r=[pss], w=[p])
                dl = jb - 4 * qi
                if dl >= 0:
                    c.op("pool", lambda e: e.tensor_tensor(out=p[:], in0=p[:], in1=masks[dl][:], op=ALU.mult), r=[p, masks[dl]], w=[p])
                c.op("pe", lambda e: e.matmul(po[:, :], lhsT=va[:, jb, :], rhs=p[:], start=(jb == 0), stop=(jb == nkb - 1)), r=[va, p], w=[po])
            r_ = rec[qi % 2]
            o_ = ob[qi % 2]
            c.op("dve", lambda e: e.reciprocal(out=r_[:], in_=po[64:128, :]), r=[po], w=[r_])
            c.op("dve", lambda e: e.tensor_tensor(out=o_[:], in0=po[0:64, :], in1=r_[:], op=ALU.mult), r=[po, r_], w=[o_])
            c.dma("sp", od[h, :, q0:q0 + 512], o_[:], r=[o_], w=[od])
    c.finish()
    return nc, st


def make_tri(c, kind):
    f = c.sb([128, 128], F32, "trif")
    c.op("pool", lambda e: e.memset(f[:], -1.0), w=[f])
    if kind == "U":
        c.op("pool", lambda e: e.affine_select(out=f[:], in_=f[:], pattern=[[-1, 128]], compare_op=ALU.is_gt, fill=0.0, base=0, channel_multiplier=1), r=[f], w=[f])
    else:
        c.op("pool", lambda e: e.affine_select(out=f[:], in_=f[:], pattern=[[1, 128]], compare_op=ALU.is_ge, fill=0.0, base=0, channel_multiplier=-1), r=[f], w=[f])
    b = c.sb([128, 128], BF16, "trib")
    c.op("pool", lambda e: e.tensor_copy(out=b[:], in_=f[:]), r=[f], w=[b])
    return b


def emit_sb(c, qd, kd, vd, od, S, NH, ident, row0=0):
    NB = S // 128
    NQ = S // 512
    scale = 0.125
    masks = make_masks(c, strict=True)
    negU = make_tri(c, "U")
    negL = make_tri(c, "L")
    Q = [c.sb([64, S], BF16, "sQ") for _ in range(2)]
    K = [c.sb([64, S], BF16, "sK") for _ in range(2)]
    VT = [c.sb([64, S], BF16, "sVT") for _ in range(2)]
    V = [c.sb([128, NB, 64], BF16, "sV") for _ in range(2)]
    ps_z = [c.ps([128, 512], F32, "ps_z") for _ in range(2)]
    ps_a = [c.ps([128, 512], F32, "ps_a") for _ in range(2)]
    ps_o = [c.ps([64, 512], F32, "sps_o") for _ in range(2)]
    ps_t = c.ps([128, 64], F32, "sps_t")
    ex = [c.sb([128, 512], F32, "ex") for _ in range(2)]
    sp = [c.sb([128, 512], F32, "sp") for _ in range(2)]
    spb = [c.sb([128, 512], BF16, "spb") for _ in range(3)]
    lsg = [c.sb([128, 512], F32, "lsg") for _ in range(2)]
    A = [c.sb([128, 512], F32, "A") for _ in range(2)]
    wb = [c.sb([128, 512], BF16, "wb") for _ in range(2)]
    ob = [c.sb([64, 512], F32, "sob") for _ in range(2)]
    it = 0
    for h in range(NH):
        q, k, vt, v = Q[h % 2], K[h % 2], VT[h % 2], V[h % 2]
        c.dma("pool", q[:], qd[h], r=[qd], w=[q])
        c.dma("pool", k[:], kd[h], r=[kd], w=[k])
        c.dma("pool", vt[:], vd[h], r=[vd], w=[vt])
        for jb in range(NB):
            c.op("pe", lambda e: e.matmul(ps_t[:, :], lhsT=vt[:, jb * 128:(jb + 1) * 128], rhs=ident[0:64, 0:64], start=True, stop=True), r=[vt, ident], w=[ps_t])
            c.op("dve", lambda e: e.tensor_copy(out=v[:, jb, :], in_=ps_t[:, :]), r=[ps_t], w=[v])
        for qi in range(NQ):
            q0 = qi * 512
            po = ps_o[qi % 2]
            nkb = 4 * qi + 4
            prev_spb = None
            a_prev = None
            for jb in range(nkb - 1, -1, -1):
                pz = ps_z[it % 2]
                pa = ps_a[it % 2]
                e_, s_, sb_, l_, w_ = ex[it % 2], sp[it % 2], spb[it % 3], lsg[it % 2], wb[it % 2]
                a_new = A[it % 2]
                it += 1
                dl = jb - 4 * qi
                c.op("pe", lambda e: e.matmul(pz[:, :], lhsT=k[:, jb * 128:(jb + 1) * 128], rhs=q[:, q0:q0 + 512], start=True, stop=True), r=[k, q], w=[pz])
                c.op("act", lambda e: e.activation(out=e_[:], in_=pz[:], func=AF.Exp, scale=scale), r=[pz], w=[e_])
                c.op("act", lambda e: e.activation(out=s_[:], in_=e_[:], func=AF.Ln, bias=1.0, scale=1.0), r=[e_], w=[s_])
                if dl >= 0:
                    c.op("pool", lambda e: e.tensor_tensor(out=sb_[:], in0=s_[:], in1=masks[dl][:], op=ALU.mult), r=[s_, masks[dl]], w=[sb_])
                else:
                    c.op("pool", lambda e: e.tensor_copy(out=sb_[:], in_=s_[:]), r=[s_], w=[sb_])
                first = prev_spb is None
                c.op("pe", lambda e: e.matmul(pa[:, :], lhsT=negU[:], rhs=sb_[:], start=True, stop=first), r=[negU, sb_], w=[pa])
                if not first:
                    c.op("pe", lambda e: e.matmul(pa[:, :], lhsT=negL[:], rhs=prev_spb[:], start=False, stop=True), r=[negL, prev_spb], w=[pa])
                c.op("dve", lambda e: e.scalar_tensor_tensor(out=l_[:], in0=pz[:], scalar=scale, in1=s_[:], op0=ALU.mult, op1=ALU.subtract), r=[pz, s_], w=[l_])
                if first:
                    c.op("dve", lambda e: e.tensor_copy(out=a_new[:], in_=pa[:]), r=[pa], w=[a_new])
                else:
                    c.op("dve", lambda e: e.tensor_tensor(out=a_new[:], in0=pa[:], in1=a_prev[:], op=ALU.add), r=[pa, a_prev], w=[a_new])
                c.op("dve", lambda e: e.tensor_tensor(out=l_[:], in0=l_[:], in1=a_new[:], op=ALU.add), r=[l_, a_new], w=[l_])
                c.op("act", lambda e: e.activation(out=w_[:], in_=l_[:], func=AF.Exp), r=[l_], w=[w_])
                if dl >= 0:
                    c.op("pool", lambda e: e.tensor_tensor(out=w_[:], in0=w_[:], in1=masks[dl][:], op=ALU.mult), r=[w_, masks[dl]], w=[w_])
                c.op("pe", lambda e: e.matmul(po[:, :], lhsT=v[:, jb, :], rhs=w_[:], start=first, stop=(jb == 0)), r=[v, w_], w=[po])
                prev_spb = sb_
                a_prev = a_new
            o_ = ob[qi % 2]
            c.op("act", lambda e: e.activation(out=o_[:], in_=po[:, :], func=AF.Copy), r=[po], w=[o_])
            c.dma("sp", od[row0 + h * 64:row0 + (h + 1) * 64, q0:q0 + 512], o_[:], r=[o_], w=[od])


def build_sb_program(S=8192, NH=2):
    nc = bass.Bass("TRN2", target_bir_lowering=False)
    st = ExitStack()
    c = Ctx(nc, st)
    EI, EO = "ExternalInput", "ExternalOutput"
    qd = c.dram("q", [NH, 64, S], F32, EI)
    kd = c.dram("k", [NH, 64, S], F32, EI)
    vd = c.dram("v", [NH, 64, S], F32, EI)
    od = c.dram("oT", [NH * 64, S], F32, EO)
    ident = make_ident(c, BF16)
    emit_sb(c, qd, kd, vd, od, S, NH, ident)
    c.finish()
    return nc, st


import os
def emit_rwkv(c, d, od, S, identf, row0=0):
    STOP = float(os.environ.get("RW_STOP", "9"))
    TS = 512
    NTL = S // TS
    NCH = TS // 64
    GN_EPS = 64e-5
    vec = c.sb([128, 8], F32, "rvec")
    c.dma("sp", vec[:], d["vecs"][:, :], r=[d["vecs"]], w=[vec])
    W0, A0, KK_, KA, RK, LNW, LNB = (vec[:, i:i + 1] for i in (0, 1, 2, 3, 5, 6, 7))
    omk = c.sb([128, 1], F32, "omk")
    c.op("dve", lambda e: e.tensor_scalar(out=omk[:], in0=vec[:, 3:4], scalar1=-1.0, scalar2=1.0, op0=ALU.mult, op1=ALU.add), r=[vec], w=[omk])
    mu3 = c.sb([128, 3], F32, "mu3")
    mu2 = c.sb([64, 2], F32, "mu2")
    mug = c.sb([128, 2], F32, "mug")
    c.dma("sp", mu3[:], d["mu_rkv"][:, :], r=[d["mu_rkv"]], w=[mu3])
    c.dma("sp", mu2[:], d["mu_wa"][:, :], r=[d["mu_wa"]], w=[mu2])
    c.dma("sp", mug[:], d["mu_gl"][:, :], r=[d["mu_gl"]], w=[mug])
    w2b = c.sb([64, 128], BF16, "w2b")
    a2b = c.sb([64, 128], BF16, "a2b")
    g2b = c.sb([128, 2, 128], BF16, "g2b")
    c.dma("pool", w2b[:], d["w2"][:, :], r=[d["w2"]], w=[w2b])
    c.dma("pool", a2b[:], d["a2"][:, :], r=[d["a2"]], w=[a2b])
    c.dma("pool", g2b[:], d["g2"].h.rearrange("k p n -> p k n"), r=[d["g2"]], w=[g2b])
    bones = c.sb([128, 128], F32, "bones")
    bavg = c.sb([128, 128], F32, "bavg")
    for (t_, val) in ((bones, 1.0), (bavg, 1.0 / 64)):
        c.op("pool", lambda e: e.memset(t_[:], 0.0), w=[t_])
        c.op("pool", lambda e: e.memset(t_[0:64, 0:64], val), w=[t_])
        c.op("pool", lambda e: e.memset(t_[64:128, 64:128], val), w=[t_])
    maskT = c.sb([64, 128], F32, "maskT")
    c.op("pool", lambda e: e.memset(maskT[:], 1.0), w=[maskT])
    c.op("pool", lambda e: e.affine_select(out=maskT[:, 0:64], in_=maskT[:, 0:64], pattern=[[1, 64]], compare_op=ALU.is_gt, fill=0.0, base=0, channel_multiplier=-1), r=[maskT], w=[maskT])
    c.op("pool", lambda e: e.affine_select(out=maskT[:, 64:128], in_=maskT[:, 64:128], pattern=[[1, 64]], compare_op=ALU.is_ge, fill=0.0, base=0, channel_multiplier=-1), r=[maskT], w=[maskT])
    maskSL = c.sb([64, 64], F32, "maskSL")
    c.op("pool", lambda e: e.memset(maskSL[:], 1.0), w=[maskSL])
    c.op("pool", lambda e: e.affine_select(out=maskSL[:], in_=maskSL[:], pattern=[[-1, 64]], compare_op=ALU.is_gt, fill=0.0, base=0, channel_multiplier=1), r=[maskSL], w=[maskSL])
    if STOP <= 1:
        return
    def T2(name, n=2, shape=(128, TS), dt=F32):
        return [c.sb(list(shape), dt, name) for _ in range(n)]
    X3 = T2("X3", 2, (128, 3, TS + 1))
    WA = T2("WA", 2, (64, 2, TS + 1))
    GL = T2("GL", 2, (128, 2, TS + 1))
    dtmp = T2("dtmp", 2)
    rm, km, vm = T2("rm"), T2("km"), T2("vm")
    wlm = T2("wlm", 1, (64, TS))[0]
    th = T2("th", 1, (64, TS), BF16)[0]
    alm = T2("alm", 1, (64, TS), BF16)[0]
    glm = T2("glm", 1, (128, 2, TS))[0]
    sgl = T2("sgl", 1, (128, 2, TS), BF16)[0]
    logw, av_, gg = T2("logw", 1)[0], T2("a", 1)[0], T2("gg")
    kk, kk2, rn, kkn, k2, bv, tmpk = (T2(n_, 1)[0] for n_ in ("kk", "kk2", "rn", "kkn", "k2", "bv", "tmpk"))
    cA, cB = T2("cA", 1)[0], T2("cB", 1)[0]
    Winc, Wprev, Einv = T2("Winc"), T2("Wprev", 1)[0], T2("Einv", 1)[0]
    AR = T2("AR", 2, (128, NCH, 128))
    BT, KT = T2("BT"), T2("KT")
    yb = T2("yb")
    ps_big = [c.ps([128, TS], F32, "rps_big") for _ in range(2)]
    big_i = [0]

    def nbig():
        p = ps_big[big_i[0] % 2]
        big_i[0] += 1
        return p
    banks = [c.ps([128, 512], F32, f"rps_bank{i_}") for i_ in range(4)]
    sps = {}
    sps["tr0"] = SubTk(banks[0], banks[0].h[0:64, 0:128], "tr0")
    sps["tr1"] = SubTk(banks[1], banks[1].h[0:64, 0:128], "tr1")
    sps["g1"] = SubTk(banks[2], banks[2].h[0:64, 0:128], "g1")
    sps["g2"] = SubTk(banks[2], banks[2].h[0:64, 128:256], "g2")
    for i_, n_ in enumerate(("n", "p", "pt", "m")):
        sps[n_] = SubTk(banks[2], banks[2].h[0:64, 256 + 64 * i_:256 + 64 * i_ + 64], n_)
    for i_, n_ in enumerate(("x", "u", "y", "x2", "y2")):
        sps[n_] = SubTk(banks[3], banks[3].h[0:64, 64 * i_:64 * i_ + 64], n_)
    sps["zn"] = SubTk(banks[3], banks[3].h[:, 320:384], "zn")
    Z = [c.sb([128, 64], F32, "Z") for _ in range(2)]
    c.op("dve", lambda e: e.memset(Z[0][:], 0.0), w=[Z[0]])
    c.op("dve", lambda e: e.memset(Z[1][:], 0.0), w=[Z[1]])
    btok = [[c.sb([64, 128], F32, "btok") for _ in range(2)] for _ in range(2)]
    ktok = [[c.sb([64, 128], F32, "ktok") for _ in range(2)] for _ in range(2)]
    for lst in (btok, ktok):
        for pr in lst:
            for t_ in pr:
                c.op("pool", lambda e: e.memset(t_[:], 0.0), w=[t_])
    vtok = [c.sb([64, 128], F32, "vtok") for _ in range(2)]
    Gb = [c.sb([64, 128], F32, "Gb") for _ in range(2)]
    Gk = [c.sb([64, 128], F32, "Gk") for _ in range(2)]
    Pm = [[c.sb([64, 64], F32, "Pm") for _ in range(2)] for _ in range(2)]
    PTm = [[c.sb([64, 64], F32, "PTm") for _ in range(2)] for _ in range(2)]
    MT = [c.sb([64, 64], F32, "MT") for _ in range(2)]
    Xs = [c.sb([64, 64], F32, "Xs") for _ in range(2)]
    Us = [c.sb([64, 64], F32, "Us") for _ in range(2)]
    ytmp = [c.sb([64, 64], F32, "ytmp") for _ in range(2)]
    v3 = lambda ap: ap.rearrange("p (c t) -> p c t", t=64)
    zi = 0
    for tl in range(NTL):
        t0 = tl * TS
        pb = tl % 2
        x3, wa, gl = X3[pb], WA[pb], GL[pb]
        for (buf, src, P_) in ((x3, d["rkv"].h.rearrange("k p s -> p k s"), 128), (wa, d["wa"].h.rearrange("k p s -> p k s"), 64), (gl, d["gl"].h.rearrange("k p s -> p k s"), 128)):
            srct = {id(x3): d["rkv"], id(wa): d["wa"], id(gl): d["gl"]}[id(buf)]
            if t0 == 0:
                c.op("pool", lambda e: e.memset(buf[:, :, 0:1], 0.0), w=[buf])
                c.dma("sp", buf[:, :, 1:TS + 1], src[:, :, 0:TS], r=[srct], w=[buf])
            else:
                c.dma("sp", buf[:, :, :], src[:, :, t0 - 1:t0 + TS], r=[srct], w=[buf])
        def lerp(buf, k, P_, mu_ap, mu_tk, out_ap, out_tk, dt_):
            c.op("pool", lambda e: e.tensor_tensor(out=dt_[:P_, :], in0=buf[:P_, k, 0:TS], in1=buf[:P_, k, 1:TS + 1], op=ALU.subtract), r=[buf], w=[dt_])
            c.op("dve", lambda e: e.scalar_tensor_tensor(out=out_ap, in0=dt_[:P_, :], scalar=mu_ap, in1=buf[:P_, k, 1:TS + 1], op0=ALU.mult, op1=ALU.add), r=[dt_, buf, mu_tk], w=[out_tk])
        r_, k_, v_ = rm[pb], km[pb], vm[pb]
        lerp(x3, 0, 128, mu3[:, 0:1], mu3, r_[:, :], r_, dtmp[0])
        lerp(x3, 1, 128, mu3[:, 1:2], mu3, k_[:, :], k_, dtmp[1])
        lerp(x3, 2, 128, mu3[:, 2:3], mu3, v_[:, :], v_, dtmp[0])
        lerp(wa, 0, 64, mu2[:, 0:1], mu2, wlm[:, :], wlm, dtmp[1])
        lerp(wa, 1, 64, mu2[:, 1:2], mu2, alm[:, :], alm, dtmp[0])
        lerp(gl, 0, 128, mug[:, 0:1], mug, glm[:, 0, :], glm, dtmp[1])
        lerp(gl, 1, 128, mug[:, 1:2], mug, glm[:, 1, :], glm, dtmp[0])
        if STOP <= 2:
            continue
        c.op("act", lambda e: e.activation(out=th[:], in_=wlm[:], func=AF.Tanh), r=[wlm], w=[th])
        c.op("act", lambda e: e.activation(out=sgl[:], in_=glm[:], func=AF.Sigmoid), r=[glm], w=[sgl])
        pw = nbig()
        c.op("pe", lambda e: e.matmul(pw[:, :], lhsT=w2b[:], rhs=th[:], start=True, stop=True), r=[w2b, th], w=[pw])
        c.op("act", lambda e: e.activation(out=logw[:], in_=pw[:], func=AF.Sigmoid, bias=W0, scale=1.0), r=[pw, vec], w=[logw])
        c.op("dve", lambda e: e.tensor_scalar(out=logw[:], in0=logw[:], scalar1=-float(np.exp(-0.5)), scalar2=None, op0=ALU.mult), r=[logw], w=[logw])
        pa = nbig()
        c.op("pe", lambda e: e.matmul(pa[:, :], lhsT=a2b[:], rhs=alm[:], start=True, stop=True), r=[a2b, alm], w=[pa])
        c.op("act", lambda e: e.activation(out=av_[:], in_=pa[:], func=AF.Sigmoid, bias=A0, scale=1.0), r=[pa, vec], w=[av_])
        pg = nbig()
        for kx in range(2):
            c.op("pe", lambda e: e.matmul(pg[:, :], lhsT=g2b[:, kx, :], rhs=sgl[:, kx, :], start=(kx == 0), stop=(kx == 1)), r=[g2b, sgl], w=[pg])
        g_ = gg[pb]
        c.op("act", lambda e: e.activation(out=g_[:], in_=pg[:], func=AF.Copy), r=[pg], w=[g_])
        if STOP <= 2.1:
            continue
        c.op("dve", lambda e: e.tensor_scalar(out=kk[:], in0=k_[:], scalar1=KK_, scalar2=None, op0=ALU.mult), r=[k_, vec], w=[kk])
        c.op("act", lambda e: e.activation(out=kk2[:], in_=kk[:], func=AF.Square), r=[kk], w=[kk2])
        pss = nbig()
        c.op("pe", lambda e: e.matmul(pss[:, :], lhsT=bones[:], rhs=kk2[:], start=True, stop=True), r=[bones, kk2], w=[pss])
        c.op("act", lambda e: e.activation(out=rn[:], in_=pss[:], func=AF.Ln, bias=1e-24, scale=1.0), r=[pss], w=[rn])
        c.op("act", lambda e: e.activation(out=rn[:], in_=rn[:], func=AF.Exp, scale=-0.5), r=[rn], w=[rn])
        c.op("dve", lambda e: e.tensor_tensor(out=kkn[:], in0=kk[:], in1=rn[:], op=ALU.mult), r=[kk, rn], w=[kkn])
        c.op("dve", lambda e: e.tensor_scalar(out=tmpk[:], in0=av_[:], scalar1=KA, scalar2=omk[:, 0:1], op0=ALU.mult, op1=ALU.add), r=[av_, vec, omk], w=[tmpk])
        c.op("dve", lambda e: e.tensor_tensor(out=k2[:], in0=k_[:], in1=tmpk[:], op=ALU.mult), r=[k_, tmpk], w=[k2])
        c.op("dve", lambda e: e.tensor_tensor(out=bv[:], in0=kkn[:], in1=av_[:], op=ALU.mult), r=[kkn, av_], w=[bv])
        if STOP <= 2.2:
            continue
        src = logw
        for si, s_ in enumerate((1, 2, 4, 8, 16, 32)):
            dst = cA if si % 2 == 0 else cB
            c.op("pool", lambda e: e.tensor_tensor(out=v3(dst[:, :])[:, :, s_:], in0=v3(src[:, :])[:, :, s_:], in1=v3(src[:, :])[:, :, :64 - s_], op=ALU.add), r=[src], w=[dst])
            c.op("pool", lambda e: e.tensor_copy(out=v3(dst[:, :])[:, :, :s_], in_=v3(src[:, :])[:, :, :s_]), r=[src], w=[dst])
            src = dst
        cum = src
        if STOP <= 2.3:
            continue
        wi = Winc[pb]
        c.op("act", lambda e: e.activation(out=wi[:], in_=cum[:], func=AF.Exp), r=[cum], w=[wi])
        c.op("act", lambda e: e.activation(out=Einv[:], in_=cum[:], func=AF.Exp, scale=-1.0), r=[cum], w=[Einv])
        c.op("pool", lambda e: e.tensor_tensor(out=cA[:], in0=cum[:], in1=logw[:], op=ALU.subtract), r=[cum, logw], w=[cA])
        c.op("act", lambda e: e.activation(out=Wprev[:], in_=cA[:], func=AF.Exp), r=[cA], w=[Wprev])
        ar, bt, kt = AR[pb], BT[pb], KT[pb]
        if STOP <= 2.4:
            continue
        c.op("dve", lambda e: e.tensor_tensor(out=ar[:, :, 64:128], in0=v3(r_[:, :]), in1=v3(wi[:, :]), op=ALU.mult), r=[r_, wi], w=[ar])
        c.op("dve", lambda e: e.scalar_tensor_tensor(out=ar[:, :, 0:64], in0=v3(kkn[:, :]), scalar=-1.0, in1=v3(Wprev[:, :]), op0=ALU.mult, op1=ALU.mult), r=[kkn, Wprev], w=[ar])
        if STOP <= 2.5:
            continue
        c.op("dve", lambda e: e.tensor_tensor(out=bt[:], in0=bv[:], in1=Einv[:], op=ALU.mult), r=[bv, Einv], w=[bt])
        c.op("dve", lambda e: e.tensor_tensor(out=kt[:], in0=k2[:], in1=Einv[:], op=ALU.mult), r=[k2, Einv], w=[kt])
        if STOP <= 3:
            continue
        y_ = yb[pb]
        for ch in range(NCH):
            cs = slice(ch * 64, ch * 64 + 64)
            cp = ch % 2
            zo, zn = Z[zi % 2], Z[(zi + 1) % 2]
            zi += 1
            c.op("pe", lambda e: e.matmul(sps["tr0"][:, :], lhsT=bt[:, cs], rhs=identf[:, :], start=True, stop=True), r=[bt, identf], w=[sps["tr0"]])
            for h in range(2):
                c.op("act", lambda e: e.activation(out=btok[cp][h][:, 64 * h:64 * h + 64], in_=sps["tr0"][:, 64 * h:64 * h + 64], func=AF.Copy), r=[sps["tr0"]], w=[btok[cp][h]])
            if STOP <= 3.05:
                continue
            c.op("pe", lambda e: e.matmul(sps["tr1"][:, :], lhsT=kt[:, cs], rhs=identf[:, :], start=True, stop=True), r=[kt, identf], w=[sps["tr1"]])
            for h in range(2):
                c.op("act", lambda e: e.activation(out=ktok[cp][h][:, 64 * h:64 * h + 64], in_=sps["tr1"][:, 64 * h:64 * h + 64], func=AF.Copy), r=[sps["tr1"]], w=[ktok[cp][h]])
            vt_ = vtok[cp]
            if STOP <= 3.07:
                continue
            c.op("pe", lambda e: e.matmul(sps["tr0"][:, :], lhsT=v_[:, cs], rhs=identf[:, :], start=True, stop=True), r=[v_, identf], w=[sps["tr0"]])
            c.op("act", lambda e: e.activation(out=vt_[:], in_=sps["tr0"][:, :], func=AF.Copy), r=[sps["tr0"]], w=[vt_])
            if STOP <= 3.1:
                continue
            for h in range(2):
                hp = slice(64 * h, 64 * h + 64)
                gb, gk, mt, xs_, us_ = Gb[h], Gk[h], MT[h], Xs[h], Us[h]
                c.op("pe", lambda e: e.matmul(sps["g1"][:, :], lhsT=bt[hp, cs], rhs=ar[hp, ch, :], start=True, stop=True), r=[bt, ar], w=[sps["g1"]])
                c.op("dve", lambda e: e.tensor_tensor(out=gb[:], in0=sps["g1"][:, :], in1=maskT[:], op=ALU.mult), r=[sps["g1"], maskT], w=[gb])
                c.op("pe", lambda e: e.matmul(sps["g2"][:, :], lhsT=kt[hp, cs], rhs=ar[hp, ch, :], start=True, stop=True), r=[kt, ar], w=[sps["g2"]])
                c.op("dve", lambda e: e.tensor_tensor(out=gk[:], in0=sps["g2"][:, :], in1=maskT[:], op=ALU.mult), r=[sps["g2"], maskT], w=[gk])
                c.op("pe", lambda e: e.matmul(sps["n"][:, :], lhsT=ar[hp, ch, 0:64], rhs=bt[hp, cs], start=True, stop=True), r=[ar, bt], w=[sps["n"]])
                if STOP <= 3.2:
                    continue
                P_, PT_ = Pm[h][0], PTm[h][0]
                c.op("dve", lambda e: e.tensor_tensor(out=P_[:], in0=sps["n"][:, :], in1=maskSL[:], op=ALU.mult), r=[sps["n"], maskSL], w=[P_])
                c.op("act", lambda e: e.activation(out=PT_[:], in_=gb[:, 0:64], func=AF.Copy), r=[gb], w=[PT_])
                c.op("dve", lambda e: e.tensor_tensor(out=mt[:], in0=gb[:, 0:64], in1=identf[0:64, 0:64], op=ALU.add), r=[gb, identf], w=[mt])
                for kx in range(5):
                    Pn, PTn = Pm[h][(kx + 1) % 2], PTm[h][(kx + 1) % 2]
                    c.op("pe", lambda e: e.matmul(sps["p"][:, :], lhsT=PT_[:], rhs=P_[:], start=True, stop=True), r=[PT_, P_], w=[sps["p"]])
                    c.op("act", lambda e: e.activation(out=Pn[:], in_=sps["p"][:, :], func=AF.Copy), r=[sps["p"]], w=[Pn])
                    if kx < 4:
                        c.op("pe", lambda e: e.matmul(sps["pt"][:, :], lhsT=P_[:], rhs=PT_[:], start=True, stop=True), r=[PT_, P_], w=[sps["pt"]])
                        c.op("act", lambda e: e.activation(out=PTn[:], in_=sps["pt"][:, :], func=AF.Copy), r=[sps["pt"]], w=[PTn])
                    c.op("pe", lambda e: e.matmul(sps["m"][:, :], lhsT=Pn[:], rhs=mt[:], start=True, stop=True), r=[Pn, mt], w=[sps["m"]])
                    c.op("dve", lambda e: e.tensor_tensor(out=mt[:], in0=mt[:], in1=sps["m"][:, :], op=ALU.add), r=[mt, sps["m"]], w=[mt])
                    P_, PT_ = Pn, PTn
                if STOP <= 3.3:
                    continue
                VAR = os.environ.get("RW_VAR", "")
                if VAR != "no1":
                    c.op("pe", lambda e: e.matmul(sps["x"][:, :], lhsT=ar[hp, ch, 0:64], rhs=zo[hp, :], start=True, stop=True), r=[ar, zo], w=[sps["x"]])
                    c.op("act", lambda e: e.activation(out=xs_[:], in_=sps["x"][:, :], func=AF.Copy), r=[sps["x"]], w=[xs_])
                if VAR != "no2":
                    c.op("pe", lambda e: e.matmul(sps["x2"][:, :], lhsT=gk[:, 0:64], rhs=vt_[:, hp], start=True, stop=True), r=[gk, vt_], w=[sps["x2"]])
                    c.op("dve", lambda e: e.tensor_tensor(out=xs_[:], in0=xs_[:], in1=sps["x2"][:, :], op=ALU.add), r=[xs_, sps["x2"]], w=[xs_])
                if STOP <= 3.35:
                    continue
                c.op("pe", lambda e: e.matmul(sps["u"][:, :], lhsT=mt[:], rhs=xs_[:], start=True, stop=True), r=[mt, xs_], w=[sps["u"]])
                c.op("act", lambda e: e.activation(out=us_[:], in_=sps["u"][:, :], func=AF.Copy), r=[sps["u"]], w=[us_])
                if STOP <= 3.36:
                    continue
                yt_ = ytmp[h]
                c.op("pe", lambda e: e.matmul(sps["y"][:, :], lhsT=zo[hp, :], rhs=ar[hp, ch, 64:128], start=True, stop=True), r=[zo, ar], w=[sps["y"]])
                c.op("act", lambda e: e.activation(out=yt_[:], in_=sps["y"][:, :], func=AF.Copy), r=[sps["y"]], w=[yt_])
                c.op("pe", lambda e: e.matmul(sps["y2"][:, :], lhsT=us_[:], rhs=gb[:, 64:128], start=True, stop=False), r=[us_, gb], w=[sps["y2"]])
                c.op("pe", lambda e: e.matmul(sps["y2"][:, :], lhsT=vt_[:, hp], rhs=gk[:, 64:128], start=False, stop=True), r=[vt_, gk], w=[sps["y2"]])
                c.op("dve", lambda e: e.tensor_tensor(out=y_[hp, cs], in0=yt_[:], in1=sps["y2"][:, :], op=ALU.add), r=[yt_, sps["y2"]], w=[y_])
            if STOP <= 3.4:
                continue
            for h in range(2):
                c.op("pe", lambda e: e.matmul(sps["zn"][:, :], lhsT=btok[cp][h][:], rhs=Us[h][:], start=(h == 0), stop=False), r=[btok[cp][h], Us[h]], w=[sps["zn"]])
                c.op("pe", lambda e: e.matmul(sps["zn"][:, :], lhsT=ktok[cp][h][:], rhs=vt_[:, 64 * h:64 * h + 64], start=False, stop=(h == 1)), r=[ktok[cp][h], vt_], w=[sps["zn"]])
            c.op("dve", lambda e: e.tensor_tensor(out=zn[:], in0=zo[:], in1=sps["zn"][:, :], op=ALU.add), r=[zo, sps["zn"]], w=[zn])
            c.op("dve", lambda e: e.tensor_scalar(out=zn[:], in0=zn[:], scalar1=wi[:, ch * 64 + 63:ch * 64 + 64], scalar2=None, op0=ALU.mult), r=[zn, wi], w=[zn])
        if STOP <= 4:
            continue
        pm = nbig()
        c.op("pe", lambda e: e.matmul(pm[:, :], lhsT=bavg[:], rhs=y_[:], start=True, stop=True), r=[bavg, y_], w=[pm])
        c.op("dve", lambda e: e.tensor_tensor(out=y_[:], in0=y_[:], in1=pm[:], op=ALU.subtract), r=[y_, pm], w=[y_])
        c.op("act", lambda e: e.activation(out=kk2[:], in_=y_[:], func=AF.Square), r=[y_], w=[kk2])
        pv = nbig()
        c.op("pe", lambda e: e.matmul(pv[:, :], lhsT=bavg[:], rhs=kk2[:], start=True, stop=True), r=[bavg, kk2], w=[pv])
        c.op("act", lambda e: e.activation(out=rn[:], in_=pv[:], func=AF.Ln, bias=GN_EPS, scale=1.0), r=[pv], w=[rn])
        c.op("act", lambda e: e.activation(out=rn[:], in_=rn[:], func=AF.Exp, scale=-0.5), r=[rn], w=[rn])
        c.op("dve", lambda e: e.tensor_tensor(out=y_[:], in0=y_[:], in1=rn[:], op=ALU.mult), r=[y_, rn], w=[y_])
        c.op("dve", lambda e: e.tensor_scalar(out=y_[:], in0=y_[:], scalar1=LNW, scalar2=LNB, op0=ALU.mult, op1=ALU.add), r=[y_, vec], w=[y_])
        c.op("dve", lambda e: e.scalar_tensor_tensor(out=kk[:], in0=r_[:], scalar=RK, in1=k2[:], op0=ALU.mult, op1=ALU.mult), r=[r_, vec, k2], w=[kk])
        pb_ = nbig()
        c.op("pe", lambda e: e.matmul(pb_[:, :], lhsT=bones[:], rhs=kk[:], start=True, stop=True), r=[bones, kk], w=[pb_])
        c.op("dve", lambda e: e.tensor_tensor(out=tmpk[:], in0=pb_[:], in1=v_[:], op=ALU.mult), r=[pb_, v_], w=[tmpk])
        c.op("dve", lambda e: e.tensor_tensor(out=y_[:], in0=y_[:], in1=tmpk[:], op=ALU.add), r=[y_, tmpk], w=[y_])
        c.op("dve", lambda e: e.tensor_tensor(out=y_[:], in0=y_[:], in1=g_[:], op=ALU.mult), r=[y_, g_], w=[y_])
        c.dma("sp", od[row0:row0 + 128, t0:t0 + TS], y_[:], r=[y_], w=[od])


def rwkv_dram(c, S):
    EI = "ExternalInput"
    d = {}
    for n_, sh in (("rkv", [3, 128, S]), ("wa", [2, 64, S]), ("gl", [2, 128, S]), ("vecs", [128, 8]), ("mu_rkv", [128, 3]), ("mu_wa", [64, 2]), ("mu_gl", [128, 2]),
                   ("w2", [64, 128]), ("a2", [64, 128]), ("g2", [2, 128, 128])):
        d[n_] = c.dram(n_, sh, F32, EI)
    return d


def build_rwkv_program(S=8192):
    nc = bass.Bass("TRN2", target_bir_lowering=False)
    st = ExitStack()
    c = Ctx(nc, st)
    d = rwkv_dram(c, S)
    od = c.dram("oT", [128, S], F32, "ExternalOutput")
    identf = make_ident(c, F32)
    emit_rwkv(c, d, od, S, identf)
    c.finish()
    return nc, st


def rwkv_host_inputs(rw_T, hg, mu, w0, w2, a0, a2, g2, k_k, k_a, r_k, ln_w, ln_b):
    S = rw_T.shape[1]
    ch = slice(hg * 128, hg * 128 + 128)
    rkv = np.stack([rw_T[0:512][ch], rw_T[512:1024][ch], rw_T[1024:1536][ch]])
    wa = np.stack([rw_T[1536:1600], rw_T[1600:1664]])
    gl = np.zeros((2, 128, S), np.float32)
    gl[0] = rw_T[1664:1792]
    gl[1, :32] = rw_T[1792:1824]
    vecs = np.zeros((128, 8), np.float32)
    for i, v in ((0, w0), (1, a0), (2, k_k), (3, k_a), (5, r_k.reshape(-1)), (6, ln_w), (7, ln_b)):
        vecs[:, i] = v[ch]
    mu_rkv = np.stack([mu[0:512][ch], mu[512:1024][ch], mu[1024:1536][ch]], axis=1)
    mu_wa = np.stack([mu[1536:1600], mu[1600:1664]], axis=1)
    mu_gl = np.zeros((128, 2), np.float32)
    mu_gl[:, 0] = mu[1664:1792]
    mu_gl[:32, 1] = mu[1792:1824]
    g2p = np.zeros((2, 128, 128), np.float32)
    g2p[0] = g2[0:128, ch]
    g2p[1, :32] = g2[128:160, ch]
    return dict(rkv=np.ascontiguousarray(rkv), wa=np.ascontiguousarray(wa), gl=gl, vecs=vecs, mu_rkv=np.ascontiguousarray(mu_rkv), mu_wa=np.ascontiguousarray(mu_wa),
                mu_gl=mu_gl, w2=np.ascontiguousarray(w2[:, ch]), a2=np.ascontiguousarray(a2[:, ch]), g2=g2p)


_PROGS = {}


def _prog(key, builder):
    if key not in _PROGS:
        _PROGS[key] = builder()
    return _PROGS[key][0]


def _run(nc, in_maps):
    res = run_bass_kernel_spmd(nc, in_maps, core_ids=list(range(8)))
    return res.results


B_, S_, D_ = 2, 8192, 1024
TC = 2048
HALO = 32


def _with_halo(xT, i):
    C = xT.shape[0]
    out = np.zeros((C, HALO + TC), np.float32)
    lo = i * TC - HALO
    if lo >= 0:
        out[:] = xT[:, lo:(i + 1) * TC]
    else:
        out[:, HALO:] = xT[:, 0:TC]
    return out


def kernel(x, p, positions, attn_norm, ffn_norm, ffn_w_in, ffn_conv_w, ffn_conv_b, ffn_w_out,
           ple_w_proj, ple_norm, ple_gate_norm, ple_w_gate,
           hyb_w_in, hyb_w_out, rw_mu, rw_w0, rw_w2, rw_a0, rw_a2, rw_g2, rw_k_k, rw_k_a,
           rw_r_k, rw_ln_w, rw_ln_b,
           mla_w_down, mla_q_norm, mla_kv_norm, mla_w_uq, mla_w_ukv, mla_w_o, final_norm):
    f32 = lambda a: np.ascontiguousarray(np.asarray(a), dtype=np.float32)
    x = f32(x)
    p = f32(p)
    positions = np.asarray(positions).astype(np.int32)
    cores = [(b, i) for b in range(B_) for i in range(4)]
    hT = [np.ascontiguousarray(x[b].T) for b in range(B_)]
    DEPTH = 4

    def next_inputs(layer):
        if layer >= DEPTH:
            return "final", dict(g_fin=chunked_vec(f32(final_norm)))
        j = layer // 2
        if layer % 2 == 0:
            return "even", dict(Whyb=blocked(f32(hyb_w_in[j])), g_attn=chunked_vec(f32(attn_norm[layer])))
        return "odd", dict(Wdn=blocked(perm_down(f32(mla_w_down[j]))), Wuq=blocked(perm_uq(f32(mla_w_uq[j]))), Wukv=blocked(f32(mla_w_ukv[j])),
                           g_attn=chunked_vec(f32(attn_norm[layer])), g_q=chunked_vec(f32(mla_q_norm[j])), g_kv=chunked_vec(f32(mla_kv_norm[j])),
                           invf=INVF)

    def run_token(layer_done, oT):
        nxt_layer = 0 if layer_done is None else layer_done + 1
        nxt, wn = next_inputs(nxt_layer)
        mix_in = layer_done is not None
        nc = _prog(("tok", mix_in, nxt), lambda: build_token_program(mix_in, nxt))
        common = dict(wn)
        if mix_in:
            L = layer_done
            j = L // 2
            w_mo = f32(hyb_w_out[j]) if L % 2 == 0 else f32(mla_w_o[j])
            common.update(Wmo=blocked(w_mo), Wfi=blocked(f32(ffn_w_in[L])), Wfo=blocked(f32(ffn_w_out[L])), Wpp=blocked(f32(ple_w_proj[L])),
                          Wpg=blocked(f32(ple_w_gate[L])), g_ffn=chunked_vec(f32(ffn_norm[L])), g_pe=chunked_vec(f32(ple_norm[L])),
                          g_pg=chunked_vec(f32(ple_gate_norm[L])),
                          convw=np.ascontiguousarray(f32(ffn_conv_w[L]).reshape(3, 22, 128).transpose(2, 0, 1)), convb=chunked_vec(f32(ffn_conv_b[L])))
        in_maps = []
        for (b, i) in cores:
            m = dict(common)
            m["hT"] = _with_halo(hT[b], i).reshape(8, 128, HALO + TC)
            if mix_in:
                m["oT"] = _with_halo(oT[b], i).reshape(8, 128, HALO + TC)
                m["pT"] = _with_halo(np.ascontiguousarray(p[layer_done, b].T), i).reshape(2, 128, HALO + TC)
            if nxt == "odd":
                m["pos"] = np.ascontiguousarray(np.broadcast_to(positions[b, i * TC:(i + 1) * TC][None], (128, TC)))
            in_maps.append(m)
        res = _run(nc, in_maps)
        out = {}
        if mix_in:
            for b in range(B_):
                hT[b] = np.concatenate([res[b * 4 + i]["hT_out"].reshape(1024, TC) for i in range(4)], axis=1)
        for key in ("projT", "qT", "kvT", "krT", "yT"):
            if key in res[0]:
                out[key] = [np.concatenate([res[b * 4 + i][key].reshape(-1, TC) for i in range(4)], axis=1) for b in range(B_)]
        return out

    out = run_token(None, None)
    for L in range(DEPTH):
        j = L // 2
        oT = [np.zeros((1024, S_), np.float32) for _ in range(B_)]
        if L % 2 == 0:
            proj = out["projT"]
            nc_sb = _prog(("sb",), lambda: build_sb_program(S_, 2))
            in_maps = []
            for b in range(B_):
                for g in range(4):
                    sl = lambda base: np.ascontiguousarray(proj[b][base + g * 128: base + (g + 1) * 128].reshape(2, 64, S_))
                    in_maps.append(dict(q=sl(0), k=sl(512), v=sl(1024)))
            res = _run(nc_sb, in_maps)
            for b in range(B_):
                for g in range(4):
                    oT[b][g * 128:(g + 1) * 128] = res[b * 4 + g]["oT"]
            nc_rw = _prog(("rw",), lambda: build_rwkv_program(S_))
            prm = [f32(a[j]) for a in (rw_mu, rw_w0, rw_w2, rw_a0, rw_a2, rw_g2, rw_k_k, rw_k_a, rw_r_k, rw_ln_w, rw_ln_b)]
            in_maps = []
            for b in range(B_):
                rwT = proj[b][1536:3360]
                for g in range(4):
                    in_maps.append(rwkv_host_inputs(rwT, g, *prm))
            res = _run(nc_rw, in_maps)
            for b in range(B_):
                for g in range(4):
                    oT[b][512 + g * 128: 512 + (g + 1) * 128] = res[b * 4 + g]["oT"]
        else:
            qT, kvT, krT = out["qT"], out["kvT"], out["krT"]
            nc_mla = _prog(("mla",), lambda: build_mla_program(S_, 4))
            in_maps = []
            for b in range(B_):
                qn = qT[b][0:1024].reshape(16, 64, S_)
                x1 = qT[b][1024:1280].reshape(16, 16, S_)
                x2 = qT[b][1280:1536].reshape(16, 16, S_)
                kv = kvT[b].reshape(16, 128, S_)
                kr = krT[b]
                for g in range(4):
                    hs = slice(4 * g, 4 * g + 4)
                    q = np.concatenate([qn[hs], x1[hs], x2[hs]], axis=1)
                    k = np.concatenate([kv[hs, :64], np.broadcast_to(kr[None], (4, 32, S_))], axis=1)
                    in_maps.append(dict(q=np.ascontiguousarray(q), k=np.ascontiguousarray(k), v=np.ascontiguousarray(kv[hs, 64:])))
            res = _run(nc_mla, in_maps)
            for b in range(B_):
                for g in range(4):
                    oT[b][g * 256:(g + 1) * 256] = res[b * 4 + g]["oT"].reshape(256, S_)
        out = run_token(L, oT)
    y = np.stack([np.ascontiguousarray(out["yT"][b].T) for b in range(B_)]).astype(np.float32)
    return y
```

```python
import numpy as np
from contextlib import ExitStack
import concourse.bass as bass
import concourse.mybir as mybir
from concourse.bass_utils import run_bass_kernel_spmd

F32, BF16, I32 = mybir.dt.float32, mybir.dt.bfloat16, mybir.dt.int32
AF = mybir.ActivationFunctionType
ALU = mybir.AluOpType
SAME_ENG_SYNC = True


class Tk:
    __slots__ = ("h", "w", "r", "name")

    def __init__(self, h, name=""):
        self.h = h
        self.w = None
        self.r = {}
        self.name = name

    def __getitem__(self, idx):
        return self.h[idx]


class SubTk:
    __slots__ = ("h", "p", "name")

    def __init__(self, parent, ap, name=""):
        self.h = ap
        self.p = parent
        self.name = name

    def __getitem__(self, idx):
        return self.h[idx]

    @property
    def w(self):
        return self.p.w

    @w.setter
    def w(self, v):
        self.p.w = v

    @property
    def r(self):
        return self.p.r

    @r.setter
    def r(self, v):
        self.p.r = v


class Ctx:
    def __init__(self, nc, stack, n_dma_sems=48):
        self.nc = nc
        self.engs = {"pe": nc.tensor, "act": nc.scalar, "dve": nc.vector, "pool": nc.gpsimd, "sp": nc.sync}
        self.esem = {k: stack.enter_context(nc.semaphore("s_" + k)) for k in ("pe", "act", "dve", "pool")}
        self.ecnt = {k: 0 for k in self.esem}
        self.dsem = [stack.enter_context(nc.semaphore(f"d{i}")) for i in range(n_dma_sems)]
        self.dcnt = [0] * n_dma_sems
        self.dnext = 0
        self.known = {k: {} for k in self.engs}
        self.nid = 0

    def _sem(self, k):
        return self.esem[k[1]] if k[0] == "e" else self.dsem[k[1]]

    def _wait(self, eng, events):
        need = {}
        for ev in events:
            if ev is None:
                continue
            k, v = ev
            if k[0] == "e" and k[1] == eng and (eng == "pe" or not SAME_ENG_SYNC):
                continue
            if need.get(k, 0) < v:
                need[k] = v
        kn = self.known[eng]
        for k, v in need.items():
            if kn.get(k, 0) >= v:
                continue
            self.engs[eng].wait_ge(self._sem(k), v)
            kn[k] = v

    @staticmethod
    def _deps(r, w):
        ev = []
        for t in r:
            ev.append(t.w)
        for t in w:
            ev.append(t.w)
            ev.extend(t.r.items())
        return ev

    @staticmethod
    def _mark(ev, r, w):
        k, v = ev
        for t in r:
            if t.r.get(k, 0) < v:
                t.r[k] = v
        for t in w:
            t.w = ev
            t.r = {}

    def op(self, eng, fn, r=(), w=()):
        self._wait(eng, self._deps(r, w))
        ins = fn(self.engs[eng])
        self.ecnt[eng] += 1
        ins.then_inc(self.esem[eng], 1)
        ev = (("e", eng), self.ecnt[eng])
        self._mark(ev, r, w)
        return ev

    def dma(self, eng, out, in_, r=(), w=()):
        i = self.dnext
        self.dnext = (i + 1) % len(self.dsem)
        deps = self._deps(r, w)
        if self.dcnt[i] > 0:
            deps.append((("d", i), self.dcnt[i]))
        self._wait(eng, deps)
        ins = self.engs[eng].dma_start(out=out, in_=in_)
        self.dcnt[i] += 16
        ins.then_inc(self.dsem[i], 16)
        ev = (("d", i), self.dcnt[i])
        self._mark(ev, r, w)
        return ev

    def finish(self, eng="sp"):
        evs = [(("d", i), c) for i, c in enumerate(self.dcnt) if c > 0]
        evs += [(("e", k), c) for k, c in self.ecnt.items() if c > 0]
        self._wait(eng, evs)

    def sb(self, shape, dt, name=None):
        self.nid += 1
        name = f"{name or 't'}_{self.nid}"
        return Tk(self.nc.alloc_sbuf_tensor(name, list(shape), dt), name)

    def ps(self, shape, dt=F32, name=None):
        self.nid += 1
        name = f"{name or 'p'}_{self.nid}"
        return Tk(self.nc.alloc_psum_tensor(name, list(shape), dt), name)

    def dram(self, name, shape, dt, kind="Internal"):
        return Tk(self.nc.dram_tensor(name, list(shape), dt, kind=kind).ap(), name)


def perm_down(w):
    out = np.zeros((w.shape[0], 7 * 128), w.dtype)
    out[:, :640] = w[:, :640]
    out[:, 640:656] = w[:, 640:656]
    out[:, 768:784] = w[:, 656:672]
    return out
def perm_uq(w):
    w3 = w.reshape(w.shape[0], 16, 96)
    return np.concatenate([w3[:, :, :64].reshape(-1, 1024), w3[:, :, 64:80].reshape(-1, 256), w3[:, :, 80:96].reshape(-1, 256)], axis=1)
_f = (10000.0 ** (-np.arange(16, dtype=np.float32) / 16)).astype(np.float32)
INVF = np.tile(_f, 8).reshape(128, 1).astype(np.float32)


NORM_EPS = 1e-6
D_MODEL = 1024
FFN = 2816
NFC = FFN // 128


def blocked(w):
    K, N = w.shape
    nb = (N + 127) // 128
    if N % 128:
        w = np.concatenate([w, np.zeros((K, nb * 128 - N), w.dtype)], axis=1)
    kc = K // 128
    return np.ascontiguousarray(w.reshape(kc, 128, nb, 128).transpose(2, 1, 0, 3))


def chunked_vec(v):
    return np.ascontiguousarray(v.reshape(-1, 128).T)


class TokEnv:
    def __init__(self, ctx, T):
        self.c = ctx
        self.T = T
        c = ctx
        self.ones = c.sb([128, 128], BF16, "ones")
        c.op("dve", lambda e: e.memset(self.ones[:], 1.0), w=[self.ones])
        self.ps_ss = c.ps([128, 512], F32, "ps_ss")
        self.ps_mm = [c.ps([128, 512], F32, "ps_mm") for _ in range(4)]
        self.mm_i = 0
        self.wbuf = {}
        self.sq = [c.sb([128, 512], BF16, "sq") for _ in range(2)]
        self.sq_i = 0
        self.rs = c.sb([128, 512], F32, "rs")

    def next_ps(self):
        p = self.ps_mm[self.mm_i % 4]
        self.mm_i += 1
        return p

    def wtile(self, KC):
        if KC not in self.wbuf:
            self.wbuf[KC] = [[self.c.sb([128, KC, 128], BF16, f"wb{KC}") for _ in range(3)], 0]
        ent = self.wbuf[KC]
        t = ent[0][ent[1] % 3]
        ent[1] += 1
        return t

    def load_w(self, Wd, n, KC):
        wt = self.wtile(KC)
        self.c.dma("pool", wt[:], Wd[n], r=[Wd], w=[wt])
        return wt

    def rstd(self, xs, D, T):
        c = self.c
        KC = len(xs)
        for k, (xt, xap) in enumerate(xs):
            sq = self.sq[self.sq_i % 2]
            self.sq_i += 1
            c.op("act", lambda e, sq=sq, xap=xap: e.activation(out=sq[:, :T], in_=xap, func=AF.Square), r=[xt], w=[sq])
            c.op("pe", lambda e, sq=sq, k=k: e.matmul(self.ps_ss[:, :T], lhsT=self.ones[:], rhs=sq[:, :T], start=(k == 0), stop=(k == KC - 1)),
                 r=[sq, self.ones], w=[self.ps_ss])
        c.op("act", lambda e: e.activation(out=self.rs[:, :T], in_=self.ps_ss[:, :T], func=AF.Ln, bias=float(D * NORM_EPS), scale=1.0), r=[self.ps_ss], w=[self.rs])
        c.op("act", lambda e: e.activation(out=self.rs[:, :T], in_=self.rs[:, :T], func=AF.Exp, scale=-0.5), r=[self.rs], w=[self.rs])
        return self.rs

    def normed(self, xs, D, g, xn, T):
        c = self.c
        rs = self.rstd(xs, D, T)
        for k, (xt, xap) in enumerate(xs):
            c.op("dve", lambda e, k=k, xap=xap: e.scalar_tensor_tensor(out=xn[:, k, :T], in0=xap, scalar=g[:, k:k + 1], in1=rs[:, :T], op0=ALU.mult, op1=ALU.mult),
                 r=[xt, g, rs], w=[xn])

    def linear(self, Wd, KC, xn, T, nblocks, consume, xn_r=None):
        c = self.c
        for n in nblocks:
            wt = self.load_w(Wd, n, KC)
            ps = self.next_ps()
            for k in range(KC):
                c.op("pe", lambda e, k=k, wt=wt, ps=ps: e.matmul(ps[:, :T], lhsT=wt[:, k, :], rhs=xn[:, k, :T], start=(k == 0), stop=(k == KC - 1)),
                     r=[wt, xn], w=[ps])
            consume(n, ps)


def prep_gain(c, gd, KC, D, name):
    g = c.sb([128, KC], F32, name)
    c.dma("sp", g[:], gd[:, :], r=[gd], w=[g])
    c.op("dve", lambda e: e.tensor_scalar(out=g[:], in0=g[:], scalar1=float(np.sqrt(D)), scalar2=None, op0=ALU.mult), r=[g], w=[g])
    return g


def build_token_program(mix_in, nxt, NT=4, T=512, parts=("halo", "mo", "ffn", "ple")):
    nc = bass.Bass("TRN2", target_bir_lowering=False)
    st = ExitStack()
    c = Ctx(nc, st)
    HALO = 32
    TT = HALO + NT * T
    NR = NT * T
    EI, EO = "ExternalInput", "ExternalOutput"
    d = {}
    d["hT"] = c.dram("hT", [8, 128, TT], F32, EI)
    hTv = d["hT"].h.rearrange("k p t -> p k t")
    env = TokEnv(c, T)
    h = c.sb([128, 8, T], F32, "h")
    xn = c.sb([128, 8, T], BF16, "xn")
    stage = [c.sb([128, T], F32, "stage") for _ in range(3)]
    stage_i = [0]

    def next_stage():
        s = stage[stage_i[0] % 3]
        stage_i[0] += 1
        return s

    hch = lambda TT_: [(h, h[:, k, :TT_]) for k in range(8)]

    if mix_in:
        d["oT"] = c.dram("oT", [8, 128, TT], F32, EI)
        oTv = d["oT"].h.rearrange("k p t -> p k t")
        d["pT"] = c.dram("pT", [2, 128, TT], F32, EI)
        pTv = d["pT"].h.rearrange("k p t -> p k t")
        d["Wmo"] = c.dram("Wmo", [8, 128, 8, 128], F32, EI)
        d["Wfi"] = c.dram("Wfi", [44, 128, 8, 128], F32, EI)
        d["Wfo"] = c.dram("Wfo", [8, 128, 22, 128], F32, EI)
        d["Wpp"] = c.dram("Wpp", [8, 128, 2, 128], F32, EI)
        d["Wpg"] = c.dram("Wpg", [8, 128, 8, 128], F32, EI)
        d["g_ffn"] = c.dram("g_ffn", [128, 8], F32, EI)
        d["g_pe"] = c.dram("g_pe", [128, 8], F32, EI)
        d["g_pg"] = c.dram("g_pg", [128, 8], F32, EI)
        d["convw"] = c.dram("convw", [128, 3, NFC], F32, EI)
        d["convb"] = c.dram("convb", [128, NFC], F32, EI)
        d["hT_out"] = c.dram("hT_out", [8, 128, NR], F32, EO)
        hOv = d["hT_out"].h.rearrange("k p t -> p k t")
        g_ffn = prep_gain(c, d["g_ffn"], 8, 1024, "g_ffn")
        g_pe = prep_gain(c, d["g_pe"], 8, 1024, "g_pe")
        g_pg = prep_gain(c, d["g_pg"], 8, 1024, "g_pg")
        convw = c.sb([128, 3, NFC], F32, "convw")
        convb = c.sb([128, NFC], F32, "convb")
        c.dma("sp", convw[:], d["convw"][:, :, :], r=[d["convw"]], w=[convw])
        c.dma("sp", convb[:], d["convb"][:, :], r=[d["convb"]], w=[convb])
        o_bf = c.sb([128, 8, T], BF16, "o_bf")
        p_bf = c.sb([128, 2, T], BF16, "p_bf")
        act = c.sb([128, NFC, T], BF16, "act")
        e_sb = c.sb([128, 8, T], F32, "e_sb")
        rs_e = c.sb([128, T], F32, "rs_e")
        carry = c.sb([128, NFC, 2], F32, "carry")
        gate_sb = [c.sb([128, T + 2], F32, "gate_sb") for _ in range(2)]
        ctmp = [c.sb([128, T], F32, "ctmp") for _ in range(2)]
        ge = [c.sb([128, T], F32, "ge") for _ in range(2)]
        sg = [c.sb([128, T], F32, "sg") for _ in range(2)]
        t1 = [c.sb([128, T], F32, "t1") for _ in range(2)]

    if nxt == "even":
        d["Whyb"] = c.dram("Whyb", [27, 128, 8, 128], F32, EI)
        d["g_attn"] = c.dram("g_attn", [128, 8], F32, EI)
        d["projT"] = c.dram("projT", [27, 128, NR], F32, EO)
        g_attn = prep_gain(c, d["g_attn"], 8, 1024, "g_attn")
    elif nxt == "odd":
        d["Wdn"] = c.dram("Wdn", [7, 128, 8, 128], F32, EI)
        d["Wuq"] = c.dram("Wuq", [12, 128, 3, 128], F32, EI)
        d["Wukv"] = c.dram("Wukv", [16, 128, 2, 128], F32, EI)
        d["g_attn"] = c.dram("g_attn", [128, 8], F32, EI)
        d["g_q"] = c.dram("g_q", [128, 3], F32, EI)
        d["g_kv"] = c.dram("g_kv", [128, 2], F32, EI)
        d["pos"] = c.dram("pos", [128, NR], I32, EI)
        d["invf"] = c.dram("invf", [128, 1], F32, EI)
        d["qT"] = c.dram("qT", [12, 128, NR], F32, EO)
        d["kvT"] = c.dram("kvT", [16, 128, NR], F32, EO)
        d["krT"] = c.dram("krT", [2, 16, NR], F32, EO)
        g_attn = prep_gain(c, d["g_attn"], 8, 1024, "g_attn")
        g_q = prep_gain(c, d["g_q"], 3, 384, "g_q")
        g_kv = prep_gain(c, d["g_kv"], 2, 256, "g_kv")
        invf = c.sb([128, 1], F32, "invf")
        c.dma("sp", invf[:], d["invf"][:, :], r=[d["invf"]], w=[invf])
        c_sb = c.sb([128, 7, T], F32, "c_sb")
        cn = c.sb([128, 3, T], BF16, "cn")
        pos_i = c.sb([128, T], I32, "pos_i")
        ang = c.sb([128, T], F32, "ang")
        tmpr = c.sb([128, T], F32, "tmpr")
        cosT = c.sb([128, T], F32, "cosT")
        sinT = c.sb([128, T], F32, "sinT")
        qr = c.sb([128, 4, T], F32, "qr")
        negpi = c.sb([128, 1], F32, "negpi")
        c.op("dve", lambda e: e.memset(negpi[:], -float(np.pi)), w=[negpi])
    elif nxt == "final":
        d["g_fin"] = c.dram("g_fin", [128, 8], F32, EI)
        d["yT"] = c.dram("yT", [8, 128, NR], F32, EO)
        g_fin = prep_gain(c, d["g_fin"], 8, 1024, "g_fin")

    def add_into_h(TT_):
        def f(n, ps):
            c.op("dve", lambda e: e.tensor_tensor(out=h[:, n, :TT_], in0=h[:, n, :TT_], in1=ps[:, :TT_], op=ALU.add), r=[ps, h], w=[h])
        return f

    def mixer_out(col0, TT_):
        c.dma("pool", o_bf[:, :, :TT_], oTv[:, :, col0:col0 + TT_], r=[d["oT"]], w=[o_bf])
        env.linear(d["Wmo"], 8, o_bf, TT_, range(8), add_into_h(TT_))

    def ffn_in(TT_, halo):
        env.normed(hch(TT_), 1024, g_ffn, xn, TT_)
        for fc in range(NFC):
            wg = env.load_w(d["Wfi"], NFC + fc, 8)
            ps_g = env.next_ps()
            for k in range(8):
                c.op("pe", lambda e: e.matmul(ps_g[:, :TT_], lhsT=wg[:, k, :], rhs=xn[:, k, :TT_], start=(k == 0), stop=(k == 7)), r=[wg, xn], w=[ps_g])
            if halo:
                c.op("act", lambda e: e.activation(out=carry[:, fc, :], in_=ps_g[:, TT_ - 2:TT_], func=AF.Copy), r=[ps_g], w=[carry])
                continue
            wu = env.load_w(d["Wfi"], fc, 8)
            ps_u = env.next_ps()
            for k in range(8):
                c.op("pe", lambda e: e.matmul(ps_u[:, :TT_], lhsT=wu[:, k, :], rhs=xn[:, k, :TT_], start=(k == 0), stop=(k == 7)), r=[wu, xn], w=[ps_u])
            gs = gate_sb[fc % 2]
            ct = ctmp[fc % 2]
            gg = ge[fc % 2]
            c.op("act", lambda e: e.activation(out=gs[:, 0:2], in_=carry[:, fc, :], func=AF.Copy), r=[carry], w=[gs])
            c.op("act", lambda e: e.activation(out=gs[:, 2:2 + TT_], in_=ps_g[:, :TT_], func=AF.Copy), r=[ps_g], w=[gs])
            c.op("act", lambda e: e.activation(out=carry[:, fc, :], in_=gs[:, TT_:TT_ + 2], func=AF.Copy), r=[gs], w=[carry])
            c.op("dve", lambda e: e.tensor_scalar(out=ct[:, :TT_], in0=gs[:, 2:2 + TT_], scalar1=convw[:, 2, fc:fc + 1], scalar2=convb[:, fc:fc + 1], op0=ALU.mult, op1=ALU.add),
                 r=[gs, convw, convb], w=[ct])
            c.op("dve", lambda e: e.scalar_tensor_tensor(out=ct[:, :TT_], in0=gs[:, 1:1 + TT_], scalar=convw[:, 1, fc:fc + 1], in1=ct[:, :TT_], op0=ALU.mult, op1=ALU.add),
                 r=[gs, convw, ct], w=[ct])
            c.op("dve", lambda e: e.scalar_tensor_tensor(out=ct[:, :TT_], in0=gs[:, 0:TT_], scalar=convw[:, 0, fc:fc + 1], in1=ct[:, :TT_], op0=ALU.mult, op1=ALU.add),
                 r=[gs, convw, ct], w=[ct])
            c.op("act", lambda e: e.activation(out=gg[:, :TT_], in_=ct[:, :TT_], func=AF.Gelu), r=[ct], w=[gg])
            c.op("dve", lambda e: e.tensor_tensor(out=act[:, fc, :TT_], in0=gg[:, :TT_], in1=ps_u[:, :TT_], op=ALU.mult), r=[gg, ps_u], w=[act])

    def ple(col0, TT_):
        c.dma("pool", p_bf[:, :, :TT_], pTv[:, :, col0:col0 + TT_], r=[d["pT"]], w=[p_bf])

        def ev_e(n, ps):
            c.op("act", lambda e: e.activation(out=e_sb[:, n, :TT_], in_=ps[:, :TT_], func=AF.Copy), r=[ps], w=[e_sb])
        env.linear(d["Wpp"], 2, p_bf, TT_, range(8), ev_e)
        rs = env.rstd([(e_sb, e_sb[:, k, :TT_]) for k in range(8)], 1024, TT_)
        c.op("act", lambda e: e.activation(out=rs_e[:, :TT_], in_=rs[:, :TT_], func=AF.Copy), r=[rs], w=[rs_e])
        env.normed(hch(TT_), 1024, g_pg, xn, TT_)

        def ev_g(n, ps):
            s_ = sg[n % 2]
            t_ = t1[n % 2]
            c.op("act", lambda e: e.activation(out=s_[:, :TT_], in_=ps[:, :TT_], func=AF.Sigmoid), r=[ps], w=[s_])
            c.op("dve", lambda e: e.scalar_tensor_tensor(out=t_[:, :TT_], in0=e_sb[:, n, :TT_], scalar=g_pe[:, n:n + 1], in1=rs_e[:, :TT_], op0=ALU.mult, op1=ALU.mult),
                 r=[e_sb, g_pe, rs_e], w=[t_])
            c.op("dve", lambda e: e.tensor_tensor(out=t_[:, :TT_], in0=t_[:, :TT_], in1=s_[:, :TT_], op=ALU.mult), r=[t_, s_], w=[t_])
            c.op("dve", lambda e: e.tensor_tensor(out=h[:, n, :TT_], in0=h[:, n, :TT_], in1=t_[:, :TT_], op=ALU.add), r=[t_, h], w=[h])
        env.linear(d["Wpg"], 8, xn, TT_, range(8), ev_g)

    def store_rows(dst_tk, dst_ap, n_part=128):
        def f(n, ps):
            s = next_stage()
            c.op("act", lambda e: e.activation(out=s[:n_part, :T], in_=ps[:n_part, :T], func=AF.Copy), r=[ps], w=[s])
            c.dma("sp", dst_ap(n), s[:n_part, :T], r=[s], w=[dst_tk])
        return f

    def rope_tables(r0):
        c.dma("sp", pos_i[:], d["pos"][:, r0:r0 + T], r=[d["pos"]], w=[pos_i])
        c.op("dve", lambda e: e.tensor_copy(out=ang[:], in_=pos_i[:]), r=[pos_i], w=[ang])
        c.op("dve", lambda e: e.tensor_scalar(out=ang[:], in0=ang[:], scalar1=invf[:, 0:1], scalar2=None, op0=ALU.mult), r=[ang, invf], w=[ang])
        for (off, dst) in ((0.0, sinT), (float(np.pi / 2), cosT)):
            if off:
                c.op("dve", lambda e: e.tensor_scalar(out=ang[:], in0=ang[:], scalar1=off, scalar2=None, op0=ALU.add), r=[ang], w=[ang])
            c.op("dve", lambda e: e.tensor_scalar(out=pos_i[:], in0=ang[:], scalar1=float(1.0 / (2 * np.pi)), scalar2=None, op0=ALU.mult), r=[ang], w=[pos_i])
            c.op("dve", lambda e: e.tensor_copy(out=tmpr[:], in_=pos_i[:]), r=[pos_i], w=[tmpr])
            c.op("dve", lambda e: e.scalar_tensor_tensor(out=dst[:], in0=tmpr[:], scalar=-6.28125, in1=ang[:], op0=ALU.mult, op1=ALU.add), r=[tmpr, ang], w=[dst])
            c.op("dve", lambda e: e.scalar_tensor_tensor(out=dst[:], in0=tmpr[:], scalar=-0.0019353071795864769, in1=dst[:], op0=ALU.mult, op1=ALU.add), r=[tmpr, dst], w=[dst])
            c.op("act", lambda e: e.activation(out=dst[:], in_=dst[:], func=AF.Sin), r=[dst], w=[dst])

    def rope_apply(x1_tk, x1, x2, P, out1_ap, out2_ap, dst_tk):
        s1 = next_stage()
        s2 = next_stage()
        tm = next_stage()
        c.op("dve", lambda e: e.tensor_tensor(out=s1[:P, :], in0=x1, in1=cosT[:P, :], op=ALU.mult), r=[x1_tk, cosT], w=[s1])
        c.op("dve", lambda e: e.tensor_tensor(out=tm[:P, :], in0=x2, in1=sinT[:P, :], op=ALU.mult), r=[x1_tk, sinT], w=[tm])
        c.op("dve", lambda e: e.tensor_tensor(out=s1[:P, :], in0=s1[:P, :], in1=tm[:P, :], op=ALU.subtract), r=[s1, tm], w=[s1])
        c.op("dve", lambda e: e.tensor_tensor(out=s2[:P, :], in0=x1, in1=sinT[:P, :], op=ALU.mult), r=[x1_tk, sinT], w=[s2])
        c.op("dve", lambda e: e.tensor_tensor(out=tm[:P, :], in0=x2, in1=cosT[:P, :], op=ALU.mult), r=[x1_tk, cosT, s1], w=[tm])
        c.op("dve", lambda e: e.tensor_tensor(out=s2[:P, :], in0=s2[:P, :], in1=tm[:P, :], op=ALU.add), r=[s2, tm], w=[s2])
        c.dma("sp", out1_ap, s1[:P, :], r=[s1], w=[dst_tk])
        c.dma("sp", out2_ap, s2[:P, :], r=[s2], w=[dst_tk])

    if mix_in:
        if "halo" in parts:
            c.dma("sp", h[:, :, :HALO], hTv[:, :, 0:HALO], r=[d["hT"]], w=[h])
            mixer_out(0, HALO)
            ffn_in(HALO, True)
        else:
            c.op("dve", lambda e: e.memset(carry[:], 0.0), w=[carry])
    for it in range(NT):
        col0 = HALO + it * T
        r0 = it * T
        c.dma("sp", h[:, :, :T], hTv[:, :, col0:col0 + T], r=[d["hT"]], w=[h])
        if mix_in:
            if "mo" in parts:
                mixer_out(col0, T)
            if "ffn" in parts:
                ffn_in(T, False)
                env.linear(d["Wfo"], NFC, act, T, range(8), add_into_h(T))
            if "ple" in parts:
                ple(col0, T)
            c.dma("sp", hOv[:, :, r0:r0 + T], h[:, :, :T], r=[h], w=[d["hT_out"]])
        if nxt == "even":
            env.normed(hch(T), 1024, g_attn, xn, T)
            env.linear(d["Whyb"], 8, xn, T, range(27), store_rows(d["projT"], lambda n: d["projT"][n, :, r0:r0 + T]))
        elif nxt == "odd":
            env.normed(hch(T), 1024, g_attn, xn, T)

            def ev_c(n, ps):
                c.op("act", lambda e: e.activation(out=c_sb[:, n, :], in_=ps[:, :T], func=AF.Copy), r=[ps], w=[c_sb])
            env.linear(d["Wdn"], 8, xn, T, range(7), ev_c)
            rope_tables(r0)
            rope_apply(c_sb, c_sb[0:16, 5, :], c_sb[0:16, 6, :], 16, d["krT"][0, :, r0:r0 + T], d["krT"][1, :, r0:r0 + T], d["krT"])
            env.normed([(c_sb, c_sb[:, k, :]) for k in range(3)], 384, g_q, cn, T)

            def ev_q(n, ps):
                if n < 8:
                    store_rows(d["qT"], lambda n_: d["qT"][n_, :, r0:r0 + T])(n, ps)
                else:
                    c.op("act", lambda e: e.activation(out=qr[:, n - 8, :], in_=ps[:, :T], func=AF.Copy), r=[ps], w=[qr])
            env.linear(d["Wuq"], 3, cn, T, range(12), ev_q)
            for j in range(2):
                rope_apply(qr, qr[:, j, :], qr[:, 2 + j, :], 128, d["qT"][8 + j, :, r0:r0 + T], d["qT"][10 + j, :, r0:r0 + T], d["qT"])
            env.normed([(c_sb, c_sb[:, 3 + k, :]) for k in range(2)], 256, g_kv, cn, T)
            env.linear(d["Wukv"], 2, cn, T, range(16), store_rows(d["kvT"], lambda n: d["kvT"][n, :, r0:r0 + T]))
        elif nxt == "final":
            rs = env.rstd(hch(T), 1024, T)
            for k in range(8):
                s = next_stage()
                c.op("dve", lambda e: e.scalar_tensor_tensor(out=s[:, :T], in0=h[:, k, :T], scalar=g_fin[:, k:k + 1], in1=rs[:, :T], op0=ALU.mult, op1=ALU.mult),
                     r=[h, g_fin, rs], w=[s])
                c.dma("sp", d["yT"][k, :, r0:r0 + T], s[:, :T], r=[s], w=[d["yT"]])
    c.finish()
    return nc, st


def make_masks(c, strict):
    masks = []
    for dl in range(4):
        mf = c.sb([128, 512], F32, "maskf")
        c.op("pool", lambda e: e.memset(mf[:], 1.0), w=[mf])
        c.op("pool", lambda e: e.affine_select(out=mf[:], in_=mf[:], pattern=[[1, 512]], compare_op=(ALU.is_gt if strict else ALU.is_ge),
                                                fill=0.0, base=-128 * dl, channel_multiplier=-1), r=[mf], w=[mf])
        mb = c.sb([128, 512], BF16, "maskb")
        c.op("pool", lambda e: e.tensor_copy(out=mb[:], in_=mf[:]), r=[mf], w=[mb])
        masks.append(mb)
    return masks


def make_ident(c, dt=BF16, n=128):
    f = c.sb([128, n], F32, "identf")
    c.op("pool", lambda e: e.memset(f[:], 1.0), w=[f])
    c.op("pool", lambda e: e.affine_select(out=f[:], in_=f[:], pattern=[[-1, n]], compare_op=ALU.is_equal, fill=0.0, base=0, channel_multiplier=1), r=[f], w=[f])
    if dt == F32:
        return f
    b = c.sb([128, n], dt, "identb")
    c.op("pool", lambda e: e.tensor_copy(out=b[:], in_=f[:]), r=[f], w=[b])
    return b


def build_mla_program(S=8192, NH=4):
    nc = bass.Bass("TRN2", target_bir_lowering=False)
    st = ExitStack()
    c = Ctx(nc, st)
    EI, EO = "ExternalInput", "ExternalOutput"
    qd = c.dram("q", [NH, 96, S], F32, EI)
    kd = c.dram("k", [NH, 96, S], F32, EI)
    vd = c.dram("v", [NH, 64, S], F32, EI)
    od = c.dram("oT", [NH, 64, S], F32, EO)
    NB = S // 128
    NQ = S // 512
    scale = 1.0 / float(np.sqrt(96.0))
    masks = make_masks(c, strict=False)
    ident = make_ident(c, BF16)
    Q = [c.sb([96, S], BF16, "Q") for _ in range(2)]
    K = [c.sb([96, S], BF16, "K") for _ in range(2)]
    VT = [c.sb([64, S], BF16, "VT") for _ in range(2)]
    Va = [c.sb([128, NB, 128], BF16, "Va") for _ in range(2)]
    for va in Va:
        c.op("pool", lambda e: e.memset(va[:], 1.0), w=[va])
    ps_s = [c.ps([128, 512], F32, "ps_s") for _ in range(4)]
    ps_o = [c.ps([128, 512], F32, "ps_o") for _ in range(2)]
    ps_t = c.ps([128, 64], F32, "ps_t")
    P = [c.sb([128, 512], BF16, "P") for _ in range(6)]
    rec = [c.sb([64, 512], F32, "rec") for _ in range(2)]
    ob = [c.sb([64, 512], F32, "ob") for _ in range(2)]
    it = 0
    for h in range(NH):
        q, k, vt, va = Q[h % 2], K[h % 2], VT[h % 2], Va[h % 2]
        c.dma("pool", q[:], qd[h], r=[qd], w=[q])
        c.dma("pool", k[:], kd[h], r=[kd], w=[k])
        c.dma("pool", vt[:], vd[h], r=[vd], w=[vt])
        for jb in range(NB):
            c.op("pe", lambda e: e.matmul(ps_t[:, :], lhsT=vt[:, jb * 128:(jb + 1) * 128], rhs=ident[0:64, 0:64], start=True, stop=True), r=[vt, ident], w=[ps_t])
            c.op("dve", lambda e: e.tensor_copy(out=va[:, jb, 0:64], in_=ps_t[:, :]), r=[ps_t], w=[va])
        blocks = []
        for qi in range(NQ):
            nkb = 4 * qi + 4
            for jb in range(nkb):
                blocks.append(dict(qi=qi, jb=jb, nkb=nkb, pss=ps_s[it % 4], p=P[it % 6]))
                it += 1

        def s1(b_):
            q0 = b_["qi"] * 512
            jb, pss, p = b_["jb"], b_["pss"], b_["p"]
            c.op("pe", lambda e: e.matmul(pss[:, :], lhsT=k[:, jb * 128:(jb + 1) * 128], rhs=q[:, q0:q0 + 512], start=True, stop=True), r=[k, q], w=[pss])
            c.op("act", lambda e: e.activation(out=p[:], in_=pss[:], func=AF.Exp, scale=scale), r=[pss], w=[p])
            dl = jb - 4 * b_["qi"]
            if dl >= 0:
                c.op("pool", lambda e: e.tensor_tensor(out=p[:], in0=p[:], in1=masks[dl][:], op=ALU.mult), r=[p, masks[dl]], w=[p])

        def s2(b_):
            qi, jb, nkb, p = b_["qi"], b_["jb"], b_["nkb"], b_["p"]
            q0 = qi * 512
            po = ps_o[qi % 2]
            c.op("pe", lambda e: e.matmul(po[:, :], lhsT=va[:, jb, :], rhs=p[:], start=(jb == 0), stop=(jb == nkb - 1)), r=[va, p], w=[po])
            if jb == nkb - 1:
                r_ = rec[qi % 2]
                o_ = ob[qi % 2]
                c.op("dve", lambda e: e.reciprocal(out=r_[:], in_=po[64:128, :]), r=[po], w=[r_])
                c.op("dve", lambda e: e.tensor_tensor(out=o_[:], in0=po[0:64, :], in1=r_[:], op=ALU.mult), r=[po, r_], w=[o_])
                c.dma("sp", od[h, :, q0:q0 + 512], o_[:], r=[o_], w=[od])
        SK = 2
        for t in range(len(blocks) + SK):
            if t < len(blocks):
                s1(blocks[t])
            if t - SK >= 0:
                s2(blocks[t - SK])
    c.finish()
    return nc, st


def make_tri(c, kind):
    f = c.sb([128, 128], F32, "trif")
    c.op("pool", lambda e: e.memset(f[:], -1.0), w=[f])
    if kind == "U":
        c.op("pool", lambda e: e.affine_select(out=f[:], in_=f[:], pattern=[[-1, 128]], compare_op=ALU.is_gt, fill=0.0, base=0, channel_multiplier=1), r=[f], w=[f])
    else:
        c.op("pool", lambda e: e.affine_select(out=f[:], in_=f[:], pattern=[[1, 128]], compare_op=ALU.is_ge, fill=0.0, base=0, channel_multiplier=-1), r=[f], w=[f])
    b = c.sb([128, 128], BF16, "trib")
    c.op("pool", lambda e: e.tensor_copy(out=b[:], in_=f[:]), r=[f], w=[b])
    return b


def emit_sb(c, qd, kd, vd, od, S, NH, ident, row0=0):
    NB = S // 128
    NQ = S // 512
    scale = 0.125
    masks = make_masks(c, strict=True)
    negU = make_tri(c, "U")
    negL = make_tri(c, "L")
    Q = [c.sb([64, S], BF16, "sQ") for _ in range(NH)]
    K = [c.sb([64, S], BF16, "sK") for _ in range(NH)]
    VT = [c.sb([64, S], BF16, "sVT") for _ in range(NH)]
    V = [c.sb([128, NB, 64], BF16, "sV") for _ in range(NH)]
    ps_z = [c.ps([128, 512], F32, "ps_z") for _ in range(3)]
    ps_a = [c.ps([128, 512], F32, "ps_a") for _ in range(3)]
    ps_o = [c.ps([64, 512], F32, "sps_o") for _ in range(NH)]
    ps_t = SubTk(ps_a[0], ps_a[0].h[:, 0:64], "sps_t")
    ND = 4
    ex = [c.sb([128, 512], F32, "ex") for _ in range(ND)]
    sp = [c.sb([128, 512], F32, "sp") for _ in range(ND)]
    spb = [c.sb([128, 512], BF16, "spb") for _ in range(2 * ND)]
    lsg = [c.sb([128, 512], F32, "lsg") for _ in range(ND)]
    A = [c.sb([128, 512], F32, "A") for _ in range(2 * ND)]
    wb = [c.sb([128, 512], BF16, "wb") for _ in range(ND)]
    ob = [c.sb([64, 512], F32, "sob") for _ in range(2)]
    for h in range(NH):
        c.dma("pool", Q[h][:], qd[h], r=[qd], w=[Q[h]])
        c.dma("pool", K[h][:], kd[h], r=[kd], w=[K[h]])
        c.dma("pool", VT[h][:], vd[h], r=[vd], w=[VT[h]])
    for h in range(NH):
        for jb in range(NB):
            c.op("pe", lambda e: e.matmul(ps_t[:, :], lhsT=VT[h][:, jb * 128:(jb + 1) * 128], rhs=ident[0:64, 0:64], start=True, stop=True), r=[VT[h], ident], w=[ps_t])
            c.op("dve", lambda e: e.tensor_copy(out=V[h][:, jb, :], in_=ps_t[:, :]), r=[ps_t], w=[V[h]])
    it = 0
    blocks = []
    for qi in range(NQ):
        nkb = 4 * qi + 4
        prev = [None] * NH
        for jb in range(nkb - 1, -1, -1):
            for h in range(NH):
                b_ = dict(qi=qi, jb=jb, h=h, nkb=nkb, pz=ps_z[it % 3], pa=ps_a[it % 3], e=ex[it % ND], s=sp[it % ND], l=lsg[it % ND], w=wb[it % ND],
                          sb=spb[it % (2 * ND)], a=A[it % (2 * ND)], prev=prev[h])
                it += 1
                prev[h] = b_
                blocks.append(b_)
    oi = [0]

    def s1(b_):
        qi, jb, h = b_["qi"], b_["jb"], b_["h"]
        q0 = qi * 512
        pz, e_, s_, sb_, l_ = b_["pz"], b_["e"], b_["s"], b_["sb"], b_["l"]
        dl = jb - 4 * qi
        c.op("pe", lambda e: e.matmul(pz[:, :], lhsT=K[h][:, jb * 128:(jb + 1) * 128], rhs=Q[h][:, q0:q0 + 512], start=True, stop=True), r=[K[h], Q[h]], w=[pz])
        c.op("act", lambda e: e.activation(out=e_[:], in_=pz[:], func=AF.Exp, scale=scale), r=[pz], w=[e_])
        c.op("act", lambda e: e.activation(out=s_[:], in_=e_[:], func=AF.Ln, bias=1.0, scale=1.0), r=[e_], w=[s_])
        if dl >= 0:
            c.op("pool", lambda e: e.tensor_tensor(out=sb_[:], in0=s_[:], in1=masks[dl][:], op=ALU.mult), r=[s_, masks[dl]], w=[sb_])
        else:
            c.op("pool", lambda e: e.tensor_copy(out=sb_[:], in_=s_[:]), r=[s_], w=[sb_])
        c.op("dve", lambda e: e.scalar_tensor_tensor(out=l_[:], in0=pz[:], scalar=scale, in1=s_[:], op0=ALU.mult, op1=ALU.subtract), r=[pz, s_], w=[l_])

    def s2(b_):
        qi, jb = b_["qi"], b_["jb"]
        pa, sb_, l_, w_, a_new, pv = b_["pa"], b_["sb"], b_["l"], b_["w"], b_["a"], b_["prev"]
        dl = jb - 4 * qi
        first = pv is None
        c.op("pe", lambda e: e.matmul(pa[:, :], lhsT=negU[:], rhs=sb_[:], start=True, stop=first), r=[negU, sb_], w=[pa])
        if not first:
            c.op("pe", lambda e: e.matmul(pa[:, :], lhsT=negL[:], rhs=pv["sb"][:], start=False, stop=True), r=[negL, pv["sb"]], w=[pa])
            c.op("dve", lambda e: e.tensor_tensor(out=a_new[:], in0=pa[:], in1=pv["a"][:], op=ALU.add), r=[pa, pv["a"]], w=[a_new])
        else:
            c.op("dve", lambda e: e.tensor_copy(out=a_new[:], in_=pa[:]), r=[pa], w=[a_new])
        c.op("dve", lambda e: e.tensor_tensor(out=l_[:], in0=l_[:], in1=a_new[:], op=ALU.add), r=[l_, a_new], w=[l_])
        c.op("act", lambda e: e.activation(out=w_[:], in_=l_[:], func=AF.Exp), r=[l_], w=[w_])
        if dl >= 0:
            c.op("pool", lambda e: e.tensor_tensor(out=w_[:], in0=w_[:], in1=masks[dl][:], op=ALU.mult), r=[w_, masks[dl]], w=[w_])

    def s3(b_):
        qi, jb, h, w_ = b_["qi"], b_["jb"], b_["h"], b_["w"]
        q0 = qi * 512
        po = ps_o[h]
        c.op("pe", lambda e: e.matmul(po[:, :], lhsT=V[h][:, jb, :], rhs=w_[:], start=(b_["prev"] is None), stop=(jb == 0)), r=[V[h], w_], w=[po])
        if jb == 0:
            o_ = ob[oi[0] % 2]
            oi[0] += 1
            c.op("act", lambda e: e.activation(out=o_[:], in_=po[:, :], func=AF.Copy), r=[po], w=[o_])
            c.dma("sp", od[row0 + h * 64:row0 + (h + 1) * 64, q0:q0 + 512], o_[:], r=[o_], w=[od])
    n = len(blocks)
    for t in range(n + 2):
        if t < n:
            s1(blocks[t])
        if 0 <= t - 1 < n:
            s2(blocks[t - 1])
        if 0 <= t - 2 < n:
            s3(blocks[t - 2])


def build_sb_program(S=8192, NH=2):
    nc = bass.Bass("TRN2", target_bir_lowering=False)
    st = ExitStack()
    c = Ctx(nc, st)
    EI, EO = "ExternalInput", "ExternalOutput"
    qd = c.dram("q", [NH, 64, S], F32, EI)
    kd = c.dram("k", [NH, 64, S], F32, EI)
    vd = c.dram("v", [NH, 64, S], F32, EI)
    od = c.dram("oT", [NH * 64, S], F32, EO)
    ident = make_ident(c, BF16)
    emit_sb(c, qd, kd, vd, od, S, NH, ident)
    c.finish()
    return nc, st


import os
def emit_rwkv(c, d, od, S, identf, row0=0):
    STOP = float(os.environ.get("RW_STOP", "9"))
    TS = 512
    NTL = S // TS
    NCH = TS // 64
    GN_EPS = 64e-5
    vec = c.sb([128, 8], F32, "rvec")
    c.dma("sp", vec[:], d["vecs"][:, :], r=[d["vecs"]], w=[vec])
    W0, A0, KK_, KA, RK, LNW, LNB = (vec[:, i:i + 1] for i in (0, 1, 2, 3, 5, 6, 7))
    omk = c.sb([128, 1], F32, "omk")
    c.op("dve", lambda e: e.tensor_scalar(out=omk[:], in0=vec[:, 3:4], scalar1=-1.0, scalar2=1.0, op0=ALU.mult, op1=ALU.add), r=[vec], w=[omk])
    mu3 = c.sb([128, 3], F32, "mu3")
    mu2 = c.sb([64, 2], F32, "mu2")
    mug = c.sb([128, 2], F32, "mug")
    c.dma("sp", mu3[:], d["mu_rkv"][:, :], r=[d["mu_rkv"]], w=[mu3])
    c.dma("sp", mu2[:], d["mu_wa"][:, :], r=[d["mu_wa"]], w=[mu2])
    c.dma("sp", mug[:], d["mu_gl"][:, :], r=[d["mu_gl"]], w=[mug])
    w2b = c.sb([64, 128], BF16, "w2b")
    a2b = c.sb([64, 128], BF16, "a2b")
    g2b = c.sb([128, 2, 128], BF16, "g2b")
    c.dma("pool", w2b[:], d["w2"][:, :], r=[d["w2"]], w=[w2b])
    c.dma("pool", a2b[:], d["a2"][:, :], r=[d["a2"]], w=[a2b])
    c.dma("pool", g2b[:], d["g2"].h.rearrange("k p n -> p k n"), r=[d["g2"]], w=[g2b])
    bones = c.sb([128, 128], F32, "bones")
    bavg = c.sb([128, 128], F32, "bavg")
    for (t_, val) in ((bones, 1.0), (bavg, 1.0 / 64)):
        c.op("pool", lambda e: e.memset(t_[:], 0.0), w=[t_])
        c.op("pool", lambda e: e.memset(t_[0:64, 0:64], val), w=[t_])
        c.op("pool", lambda e: e.memset(t_[64:128, 64:128], val), w=[t_])
    maskT = c.sb([64, 128], F32, "maskT")
    c.op("pool", lambda e: e.memset(maskT[:], 1.0), w=[maskT])
    c.op("pool", lambda e: e.affine_select(out=maskT[:, 0:64], in_=maskT[:, 0:64], pattern=[[1, 64]], compare_op=ALU.is_gt, fill=0.0, base=0, channel_multiplier=-1), r=[maskT], w=[maskT])
    c.op("pool", lambda e: e.affine_select(out=maskT[:, 64:128], in_=maskT[:, 64:128], pattern=[[1, 64]], compare_op=ALU.is_ge, fill=0.0, base=0, channel_multiplier=-1), r=[maskT], w=[maskT])
    maskSL = c.sb([64, 64], F32, "maskSL")
    c.op("pool", lambda e: e.memset(maskSL[:], 1.0), w=[maskSL])
    c.op("pool", lambda e: e.affine_select(out=maskSL[:], in_=maskSL[:], pattern=[[-1, 64]], compare_op=ALU.is_gt, fill=0.0, base=0, channel_multiplier=1), r=[maskSL], w=[maskSL])
    if STOP <= 1:
        return
    def T2(name, n=2, shape=(128, TS), dt=F32):
        return [c.sb(list(shape), dt, name) for _ in range(n)]
    X3 = T2("X3", 2, (128, 3, TS + 1))
    WA = T2("WA", 2, (64, 2, TS + 1))
    GL = T2("GL", 2, (128, 2, TS + 1))
    dtmp = T2("dtmp", 2)
    rm, km, vm = T2("rm"), T2("km"), T2("vm")
    wlm = T2("wlm", 1, (64, TS))[0]
    th = T2("th", 1, (64, TS), BF16)[0]
    alm = T2("alm", 1, (64, TS), BF16)[0]
    glm = T2("glm", 1, (128, 2, TS))[0]
    sgl = T2("sgl", 1, (128, 2, TS), BF16)[0]
    logw, av_, gg = T2("logw", 1)[0], T2("a", 1)[0], T2("gg")
    kk, kk2, rn, kkn, k2, bv, tmpk = (T2(n_, 1)[0] for n_ in ("kk", "kk2", "rn", "kkn", "k2", "bv", "tmpk"))
    cA, cB = T2("cA", 1)[0], T2("cB", 1)[0]
    Winc, Wprev, Einv = T2("Winc"), T2("Wprev", 1)[0], T2("Einv", 1)[0]
    AR = T2("AR", 2, (128, NCH, 128))
    BT, KT = T2("BT"), T2("KT")
    yb = T2("yb")
    ps_big = [c.ps([128, TS], F32, "rps_big") for _ in range(2)]
    big_i = [0]

    def nbig():
        p = ps_big[big_i[0] % 2]
        big_i[0] += 1
        return p
    banks = [c.ps([128, 512], F32, f"rps_bank{i_}") for i_ in range(4)]
    sps = {}
    sps["tr0"] = SubTk(banks[0], banks[0].h[0:64, 0:128], "tr0")
    sps["tr1"] = SubTk(banks[1], banks[1].h[0:64, 0:128], "tr1")
    sps["g1"] = SubTk(banks[2], banks[2].h[0:64, 0:128], "g1")
    sps["g2"] = SubTk(banks[2], banks[2].h[0:64, 128:256], "g2")
    for i_, n_ in enumerate(("n", "p", "pt", "m")):
        sps[n_] = SubTk(banks[2], banks[2].h[0:64, 256 + 64 * i_:256 + 64 * i_ + 64], n_)
    for i_, n_ in enumerate(("x", "u", "y", "x2", "y2")):
        sps[n_] = SubTk(banks[3], banks[3].h[0:64, 64 * i_:64 * i_ + 64], n_)
    sps["zn"] = SubTk(banks[3], banks[3].h[:, 320:384], "zn")
    Z = [c.sb([128, 64], F32, "Z") for _ in range(2)]
    c.op("dve", lambda e: e.memset(Z[0][:], 0.0), w=[Z[0]])
    c.op("dve", lambda e: e.memset(Z[1][:], 0.0), w=[Z[1]])
    btok = [[c.sb([64, 128], F32, "btok") for _ in range(2)] for _ in range(2)]
    ktok = [[c.sb([64, 128], F32, "ktok") for _ in range(2)] for _ in range(2)]
    for lst in (btok, ktok):
        for pr in lst:
            for t_ in pr:
                c.op("pool", lambda e: e.memset(t_[:], 0.0), w=[t_])
    vtok = [c.sb([64, 128], F32, "vtok") for _ in range(2)]
    Gb = [c.sb([64, 128], F32, "Gb") for _ in range(2)]
    Gk = [c.sb([64, 128], F32, "Gk") for _ in range(2)]
    Pm = [[c.sb([64, 64], F32, "Pm") for _ in range(2)] for _ in range(2)]
    PTm = [[c.sb([64, 64], F32, "PTm") for _ in range(2)] for _ in range(2)]
    MT = [c.sb([64, 64], F32, "MT") for _ in range(2)]
    Xs = [c.sb([64, 64], F32, "Xs") for _ in range(2)]
    Us = [c.sb([64, 64], F32, "Us") for _ in range(2)]
    ytmp = [c.sb([64, 64], F32, "ytmp") for _ in range(2)]
    v3 = lambda ap: ap.rearrange("p (c t) -> p c t", t=64)
    zi = 0
    for tl in range(NTL):
        t0 = tl * TS
        pb = tl % 2
        x3, wa, gl = X3[pb], WA[pb], GL[pb]
        for (buf, src, P_) in ((x3, d["rkv"].h.rearrange("k p s -> p k s"), 128), (wa, d["wa"].h.rearrange("k p s -> p k s"), 64), (gl, d["gl"].h.rearrange("k p s -> p k s"), 128)):
            srct = {id(x3): d["rkv"], id(wa): d["wa"], id(gl): d["gl"]}[id(buf)]
            if t0 == 0:
                c.op("pool", lambda e: e.memset(buf[:, :, 0:1], 0.0), w=[buf])
                c.dma("sp", buf[:, :, 1:TS + 1], src[:, :, 0:TS], r=[srct], w=[buf])
            else:
                c.dma("sp", buf[:, :, :], src[:, :, t0 - 1:t0 + TS], r=[srct], w=[buf])
        def lerp(buf, k, P_, mu_ap, mu_tk, out_ap, out_tk, dt_):
            c.op("pool", lambda e: e.tensor_tensor(out=dt_[:P_, :], in0=buf[:P_, k, 0:TS], in1=buf[:P_, k, 1:TS + 1], op=ALU.subtract), r=[buf], w=[dt_])
            c.op("dve", lambda e: e.scalar_tensor_tensor(out=out_ap, in0=dt_[:P_, :], scalar=mu_ap, in1=buf[:P_, k, 1:TS + 1], op0=ALU.mult, op1=ALU.add), r=[dt_, buf, mu_tk], w=[out_tk])
        r_, k_, v_ = rm[pb], km[pb], vm[pb]
        lerp(x3, 0, 128, mu3[:, 0:1], mu3, r_[:, :], r_, dtmp[0])
        lerp(x3, 1, 128, mu3[:, 1:2], mu3, k_[:, :], k_, dtmp[1])
        lerp(x3, 2, 128, mu3[:, 2:3], mu3, v_[:, :], v_, dtmp[0])
        lerp(wa, 0, 64, mu2[:, 0:1], mu2, wlm[:, :], wlm, dtmp[1])
        lerp(wa, 1, 64, mu2[:, 1:2], mu2, alm[:, :], alm, dtmp[0])
        lerp(gl, 0, 128, mug[:, 0:1], mug, glm[:, 0, :], glm, dtmp[1])
        lerp(gl, 1, 128, mug[:, 1:2], mug, glm[:, 1, :], glm, dtmp[0])
        if STOP <= 2:
            continue
        c.op("act", lambda e: e.activation(out=th[:], in_=wlm[:], func=AF.Tanh), r=[wlm], w=[th])
        c.op("act", lambda e: e.activation(out=sgl[:], in_=glm[:], func=AF.Sigmoid), r=[glm], w=[sgl])
        pw = nbig()
        c.op("pe", lambda e: e.matmul(pw[:, :], lhsT=w2b[:], rhs=th[:], start=True, stop=True), r=[w2b, th], w=[pw])
        c.op("act", lambda e: e.activation(out=logw[:], in_=pw[:], func=AF.Sigmoid, bias=W0, scale=1.0), r=[pw, vec], w=[logw])
        c.op("dve", lambda e: e.tensor_scalar(out=logw[:], in0=logw[:], scalar1=-float(np.exp(-0.5)), scalar2=None, op0=ALU.mult), r=[logw], w=[logw])
        pa = nbig()
        c.op("pe", lambda e: e.matmul(pa[:, :], lhsT=a2b[:], rhs=alm[:], start=True, stop=True), r=[a2b, alm], w=[pa])
        c.op("act", lambda e: e.activation(out=av_[:], in_=pa[:], func=AF.Sigmoid, bias=A0, scale=1.0), r=[pa, vec], w=[av_])
        pg = nbig()
        for kx in range(2):
            c.op("pe", lambda e: e.matmul(pg[:, :], lhsT=g2b[:, kx, :], rhs=sgl[:, kx, :], start=(kx == 0), stop=(kx == 1)), r=[g2b, sgl], w=[pg])
        g_ = gg[pb]
        c.op("act", lambda e: e.activation(out=g_[:], in_=pg[:], func=AF.Copy), r=[pg], w=[g_])
        if STOP <= 2.1:
            continue
        c.op("dve", lambda e: e.tensor_scalar(out=kk[:], in0=k_[:], scalar1=KK_, scalar2=None, op0=ALU.mult), r=[k_, vec], w=[kk])
        c.op("act", lambda e: e.activation(out=kk2[:], in_=kk[:], func=AF.Square), r=[kk], w=[kk2])
        pss = nbig()
        c.op("pe", lambda e: e.matmul(pss[:, :], lhsT=bones[:], rhs=kk2[:], start=True, stop=True), r=[bones, kk2], w=[pss])
        c.op("act", lambda e: e.activation(out=rn[:], in_=pss[:], func=AF.Ln, bias=1e-24, scale=1.0), r=[pss], w=[rn])
        c.op("act", lambda e: e.activation(out=rn[:], in_=rn[:], func=AF.Exp, scale=-0.5), r=[rn], w=[rn])
        c.op("dve", lambda e: e.tensor_tensor(out=kkn[:], in0=kk[:], in1=rn[:], op=ALU.mult), r=[kk, rn], w=[kkn])
        c.op("dve", lambda e: e.tensor_scalar(out=tmpk[:], in0=av_[:], scalar1=KA, scalar2=omk[:, 0:1], op0=ALU.mult, op1=ALU.add), r=[av_, vec, omk], w=[tmpk])
        c.op("dve", lambda e: e.tensor_tensor(out=k2[:], in0=k_[:], in1=tmpk[:], op=ALU.mult), r=[k_, tmpk], w=[k2])
        c.op("dve", lambda e: e.tensor_tensor(out=bv[:], in0=kkn[:], in1=av_[:], op=ALU.mult), r=[kkn, av_], w=[bv])
        if STOP <= 2.2:
            continue
        src = logw
        for si, s_ in enumerate((1, 2, 4, 8, 16, 32)):
            dst = cA if si % 2 == 0 else cB
            c.op("pool", lambda e: e.tensor_tensor(out=v3(dst[:, :])[:, :, s_:], in0=v3(src[:, :])[:, :, s_:], in1=v3(src[:, :])[:, :, :64 - s_], op=ALU.add), r=[src], w=[dst])
            c.op("pool", lambda e: e.tensor_copy(out=v3(dst[:, :])[:, :, :s_], in_=v3(src[:, :])[:, :, :s_]), r=[src], w=[dst])
            src = dst
        cum = src
        if STOP <= 2.3:
            continue
        wi = Winc[pb]
        c.op("act", lambda e: e.activation(out=wi[:], in_=cum[:], func=AF.Exp), r=[cum], w=[wi])
        c.op("act", lambda e: e.activation(out=Einv[:], in_=cum[:], func=AF.Exp, scale=-1.0), r=[cum], w=[Einv])
        c.op("pool", lambda e: e.tensor_tensor(out=cA[:], in0=cum[:], in1=logw[:], op=ALU.subtract), r=[cum, logw], w=[cA])
        c.op("act", lambda e: e.activation(out=Wprev[:], in_=cA[:], func=AF.Exp), r=[cA], w=[Wprev])
        ar, bt, kt = AR[pb], BT[pb], KT[pb]
        if STOP <= 2.4:
            continue
        c.op("dve", lambda e: e.tensor_tensor(out=ar[:, :, 64:128], in0=v3(r_[:, :]), in1=v3(wi[:, :]), op=ALU.mult), r=[r_, wi], w=[ar])
        c.op("dve", lambda e: e.scalar_tensor_tensor(out=ar[:, :, 0:64], in0=v3(kkn[:, :]), scalar=-1.0, in1=v3(Wprev[:, :]), op0=ALU.mult, op1=ALU.mult), r=[kkn, Wprev], w=[ar])
        if STOP <= 2.5:
            continue
        c.op("dve", lambda e: e.tensor_tensor(out=bt[:], in0=bv[:], in1=Einv[:], op=ALU.mult), r=[bv, Einv], w=[bt])
        c.op("dve", lambda e: e.tensor_tensor(out=kt[:], in0=k2[:], in1=Einv[:], op=ALU.mult), r=[k2, Einv], w=[kt])
        if STOP <= 3:
            continue
        y_ = yb[pb]
        for ch in range(NCH):
            cs = slice(ch * 64, ch * 64 + 64)
            cp = ch % 2
            zo, zn = Z[zi % 2], Z[(zi + 1) % 2]
            zi += 1
            c.op("pe", lambda e: e.matmul(sps["tr0"][:, :], lhsT=bt[:, cs], rhs=identf[:, :], start=True, stop=True), r=[bt, identf], w=[sps["tr0"]])
            for h in range(2):
                c.op("act", lambda e: e.activation(out=btok[cp][h][:, 64 * h:64 * h + 64], in_=sps["tr0"][:, 64 * h:64 * h + 64], func=AF.Copy), r=[sps["tr0"]], w=[btok[cp][h]])
            if STOP <= 3.05:
                continue
            c.op("pe", lambda e: e.matmul(sps["tr1"][:, :], lhsT=kt[:, cs], rhs=identf[:, :], start=True, stop=True), r=[kt, identf], w=[sps["tr1"]])
            for h in range(2):
                c.op("act", lambda e: e.activation(out=ktok[cp][h][:, 64 * h:64 * h + 64], in_=sps["tr1"][:, 64 * h:64 * h + 64], func=AF.Copy), r=[sps["tr1"]], w=[ktok[cp][h]])
            vt_ = vtok[cp]
            if STOP <= 3.07:
                continue
            c.op("pe", lambda e: e.matmul(sps["tr0"][:, :], lhsT=v_[:, cs], rhs=identf[:, :], start=True, stop=True), r=[v_, identf], w=[sps["tr0"]])
            c.op("act", lambda e: e.activation(out=vt_[:], in_=sps["tr0"][:, :], func=AF.Copy), r=[sps["tr0"]], w=[vt_])
            if STOP <= 3.1:
                continue
            for h in range(2):
                hp = slice(64 * h, 64 * h + 64)
                gb, gk, mt, xs_, us_ = Gb[h], Gk[h], MT[h], Xs[h], Us[h]
                c.op("pe", lambda e: e.matmul(sps["g1"][:, :], lhsT=bt[hp, cs], rhs=ar[hp, ch, :], start=True, stop=True), r=[bt, ar], w=[sps["g1"]])
                c.op("dve", lambda e: e.tensor_tensor(out=gb[:], in0=sps["g1"][:, :], in1=maskT[:], op=ALU.mult), r=[sps["g1"], maskT], w=[gb])
                c.op("pe", lambda e: e.matmul(sps["g2"][:, :], lhsT=kt[hp, cs], rhs=ar[hp, ch, :], start=True, stop=True), r=[kt, ar], w=[sps["g2"]])
                c.op("dve", lambda e: e.tensor_tensor(out=gk[:], in0=sps["g2"][:, :], in1=maskT[:], op=ALU.mult), r=[sps["g2"], maskT], w=[gk])
                c.op("pe", lambda e: e.matmul(sps["n"][:, :], lhsT=ar[hp, ch, 0:64], rhs=bt[hp, cs], start=True, stop=True), r=[ar, bt], w=[sps["n"]])
                if STOP <= 3.2:
                    continue
                P_, PT_ = Pm[h][0], PTm[h][0]
                c.op("dve", lambda e: e.tensor_tensor(out=P_[:], in0=sps["n"][:, :], in1=maskSL[:], op=ALU.mult), r=[sps["n"], maskSL], w=[P_])
                c.op("act", lambda e: e.activation(out=PT_[:], in_=gb[:, 0:64], func=AF.Copy), r=[gb], w=[PT_])
                c.op("dve", lambda e: e.tensor_tensor(out=mt[:], in0=gb[:, 0:64], in1=identf[0:64, 0:64], op=ALU.add), r=[gb, identf], w=[mt])
                for kx in range(5):
                    Pn, PTn = Pm[h][(kx + 1) % 2], PTm[h][(kx + 1) % 2]
                    c.op("pe", lambda e: e.matmul(sps["p"][:, :], lhsT=PT_[:], rhs=P_[:], start=True, stop=True), r=[PT_, P_], w=[sps["p"]])
                    c.op("act", lambda e: e.activation(out=Pn[:], in_=sps["p"][:, :], func=AF.Copy), r=[sps["p"]], w=[Pn])
                    if kx < 4:
                        c.op("pe", lambda e: e.matmul(sps["pt"][:, :], lhsT=P_[:], rhs=PT_[:], start=True, stop=True), r=[PT_, P_], w=[sps["pt"]])
                        c.op("act", lambda e: e.activation(out=PTn[:], in_=sps["pt"][:, :], func=AF.Copy), r=[sps["pt"]], w=[PTn])
                    c.op("pe", lambda e: e.matmul(sps["m"][:, :], lhsT=Pn[:], rhs=mt[:], start=True, stop=True), r=[Pn, mt], w=[sps["m"]])
                    c.op("dve", lambda e: e.tensor_tensor(out=mt[:], in0=mt[:], in1=sps["m"][:, :], op=ALU.add), r=[mt, sps["m"]], w=[mt])
                    P_, PT_ = Pn, PTn
                if STOP <= 3.3:
                    continue
                VAR = os.environ.get("RW_VAR", "")
                if VAR != "no1":
                    c.op("pe", lambda e: e.matmul(sps["x"][:, :], lhsT=ar[hp, ch, 0:64], rhs=zo[hp, :], start=True, stop=True), r=[ar, zo], w=[sps["x"]])
                    c.op("act", lambda e: e.activation(out=xs_[:], in_=sps["x"][:, :], func=AF.Copy), r=[sps["x"]], w=[xs_])
                if VAR != "no2":
                    c.op("pe", lambda e: e.matmul(sps["x2"][:, :], lhsT=gk[:, 0:64], rhs=vt_[:, hp], start=True, stop=True), r=[gk, vt_], w=[sps["x2"]])
                    c.op("dve", lambda e: e.tensor_tensor(out=xs_[:], in0=xs_[:], in1=sps["x2"][:, :], op=ALU.add), r=[xs_, sps["x2"]], w=[xs_])
                if STOP <= 3.35:
                    continue
                c.op("pe", lambda e: e.matmul(sps["u"][:, :], lhsT=mt[:], rhs=xs_[:], start=True, stop=True), r=[mt, xs_], w=[sps["u"]])
                c.op("act", lambda e: e.activation(out=us_[:], in_=sps["u"][:, :], func=AF.Copy), r=[sps["u"]], w=[us_])
                if STOP <= 3.36:
                    continue
                yt_ = ytmp[h]
                c.op("pe", lambda e: e.matmul(sps["y"][:, :], lhsT=zo[hp, :], rhs=ar[hp, ch, 64:128], start=True, stop=True), r=[zo, ar], w=[sps["y"]])
                c.op("act", lambda e: e.activation(out=yt_[:], in_=sps["y"][:, :], func=AF.Copy), r=[sps["y"]], w=[yt_])
                c.op("pe", lambda e: e.matmul(sps["y2"][:, :], lhsT=us_[:], rhs=gb[:, 64:128], start=True, stop=False), r=[us_, gb], w=[sps["y2"]])
                c.op("pe", lambda e: e.matmul(sps["y2"][:, :], lhsT=vt_[:, hp], rhs=gk[:, 64:128], start=False, stop=True), r=[vt_, gk], w=[sps["y2"]])
                c.op("dve", lambda e: e.tensor_tensor(out=y_[hp, cs], in0=yt_[:], in1=sps["y2"][:, :], op=ALU.add), r=[yt_, sps["y2"]], w=[y_])
            if STOP <= 3.4:
                continue
            for h in range(2):
                c.op("pe", lambda e: e.matmul(sps["zn"][:, :], lhsT=btok[cp][h][:], rhs=Us[h][:], start=(h == 0), stop=False), r=[btok[cp][h], Us[h]], w=[sps["zn"]])
                c.op("pe", lambda e: e.matmul(sps["zn"][:, :], lhsT=ktok[cp][h][:], rhs=vt_[:, 64 * h:64 * h + 64], start=False, stop=(h == 1)), r=[ktok[cp][h], vt_], w=[sps["zn"]])
            c.op("dve", lambda e: e.tensor_tensor(out=zn[:], in0=zo[:], in1=sps["zn"][:, :], op=ALU.add), r=[zo, sps["zn"]], w=[zn])
            c.op("dve", lambda e: e.tensor_scalar(out=zn[:], in0=zn[:], scalar1=wi[:, ch * 64 + 63:ch * 64 + 64], scalar2=None, op0=ALU.mult), r=[zn, wi], w=[zn])
        if STOP <= 4:
            continue
        pm = nbig()
        c.op("pe", lambda e: e.matmul(pm[:, :], lhsT=bavg[:], rhs=y_[:], start=True, stop=True), r=[bavg, y_], w=[pm])
        c.op("dve", lambda e: e.tensor_tensor(out=y_[:], in0=y_[:], in1=pm[:], op=ALU.subtract), r=[y_, pm], w=[y_])
        c.op("act", lambda e: e.activation(out=kk2[:], in_=y_[:], func=AF.Square), r=[y_], w=[kk2])
        pv = nbig()
        c.op("pe", lambda e: e.matmul(pv[:, :], lhsT=bavg[:], rhs=kk2[:], start=True, stop=True), r=[bavg, kk2], w=[pv])
        c.op("act", lambda e: e.activation(out=rn[:], in_=pv[:], func=AF.Ln, bias=GN_EPS, scale=1.0), r=[pv], w=[rn])
        c.op("act", lambda e: e.activation(out=rn[:], in_=rn[:], func=AF.Exp, scale=-0.5), r=[rn], w=[rn])
        c.op("dve", lambda e: e.tensor_tensor(out=y_[:], in0=y_[:], in1=rn[:], op=ALU.mult), r=[y_, rn], w=[y_])
        c.op("dve", lambda e: e.tensor_scalar(out=y_[:], in0=y_[:], scalar1=LNW, scalar2=LNB, op0=ALU.mult, op1=ALU.add), r=[y_, vec], w=[y_])
        c.op("dve", lambda e: e.scalar_tensor_tensor(out=kk[:], in0=r_[:], scalar=RK, in1=k2[:], op0=ALU.mult, op1=ALU.mult), r=[r_, vec, k2], w=[kk])
        pb_ = nbig()
        c.op("pe", lambda e: e.matmul(pb_[:, :], lhsT=bones[:], rhs=kk[:], start=True, stop=True), r=[bones, kk], w=[pb_])
        c.op("dve", lambda e: e.tensor_tensor(out=tmpk[:], in0=pb_[:], in1=v_[:], op=ALU.mult), r=[pb_, v_], w=[tmpk])
        c.op("dve", lambda e: e.tensor_tensor(out=y_[:], in0=y_[:], in1=tmpk[:], op=ALU.add), r=[y_, tmpk], w=[y_])
        c.op("dve", lambda e: e.tensor_tensor(out=y_[:], in0=y_[:], in1=g_[:], op=ALU.mult), r=[y_, g_], w=[y_])
        c.dma("sp", od[row0:row0 + 128, t0:t0 + TS], y_[:], r=[y_], w=[od])


def rwkv_dram(c, S):
    EI = "ExternalInput"
    d = {}
    for n_, sh in (("rkv", [3, 128, S]), ("wa", [2, 64, S]), ("gl", [2, 128, S]), ("vecs", [128, 8]), ("mu_rkv", [128, 3]), ("mu_wa", [64, 2]), ("mu_gl", [128, 2]),
                   ("w2", [64, 128]), ("a2", [64, 128]), ("g2", [2, 128, 128])):
        d[n_] = c.dram(n_, sh, F32, EI)
    return d


def build_rwkv_program(S=8192):
    nc = bass.Bass("TRN2", target_bir_lowering=False)
    st = ExitStack()
    c = Ctx(nc, st)
    d = rwkv_dram(c, S)
    od = c.dram("oT", [128, S], F32, "ExternalOutput")
    identf = make_ident(c, F32)
    emit_rwkv(c, d, od, S, identf)
    c.finish()
    return nc, st


def rwkv_host_inputs(rw_T, hg, mu, w0, w2, a0, a2, g2, k_k, k_a, r_k, ln_w, ln_b):
    S = rw_T.shape[1]
    ch = slice(hg * 128, hg * 128 + 128)
    rkv = np.stack([rw_T[0:512][ch], rw_T[512:1024][ch], rw_T[1024:1536][ch]])
    wa = np.stack([rw_T[1536:1600], rw_T[1600:1664]])
    gl = np.zeros((2, 128, S), np.float32)
    gl[0] = rw_T[1664:1792]
    gl[1, :32] = rw_T[1792:1824]
    vecs = np.zeros((128, 8), np.float32)
    for i, v in ((0, w0), (1, a0), (2, k_k), (3, k_a), (5, r_k.reshape(-1)), (6, ln_w), (7, ln_b)):
        vecs[:, i] = v[ch]
    mu_rkv = np.stack([mu[0:512][ch], mu[512:1024][ch], mu[1024:1536][ch]], axis=1)
    mu_wa = np.stack([mu[1536:1600], mu[1600:1664]], axis=1)
    mu_gl = np.zeros((128, 2), np.float32)
    mu_gl[:, 0] = mu[1664:1792]
    mu_gl[:32, 1] = mu[1792:1824]
    g2p = np.zeros((2, 128, 128), np.float32)
    g2p[0] = g2[0:128, ch]
    g2p[1, :32] = g2[128:160, ch]
    return dict(rkv=np.ascontiguousarray(rkv), wa=np.ascontiguousarray(wa), gl=gl, vecs=vecs, mu_rkv=np.ascontiguousarray(mu_rkv), mu_wa=np.ascontiguousarray(mu_wa),
                mu_gl=mu_gl, w2=np.ascontiguousarray(w2[:, ch]), a2=np.ascontiguousarray(a2[:, ch]), g2=g2p)


_PROGS = {}


def _prog(key, builder):
    if key not in _PROGS:
        _PROGS[key] = builder()
    return _PROGS[key][0]


def _run(nc, in_maps):
    res = run_bass_kernel_spmd(nc, in_maps, core_ids=list(range(8)))
    return res.results


B_, S_, D_ = 2, 8192, 1024
TC = 2048
HALO = 32


def _with_halo(xT, i):
    C = xT.shape[0]
    out = np.zeros((C, HALO + TC), np.float32)
    lo = i * TC - HALO
    if lo >= 0:
        out[:] = xT[:, lo:(i + 1) * TC]
    else:
        out[:, HALO:] = xT[:, 0:TC]
    return out


def kernel(x, p, positions, attn_norm, ffn_norm, ffn_w_in, ffn_conv_w, ffn_conv_b, ffn_w_out,
           ple_w_proj, ple_norm, ple_gate_norm, ple_w_gate,
           hyb_w_in, hyb_w_out, rw_mu, rw_w0, rw_w2, rw_a0, rw_a2, rw_g2, rw_k_k, rw_k_a,
           rw_r_k, rw_ln_w, rw_ln_b,
           mla_w_down, mla_q_norm, mla_kv_norm, mla_w_uq, mla_w_ukv, mla_w_o, final_norm):
    f32 = lambda a: np.ascontiguousarray(np.asarray(a), dtype=np.float32)
    x = f32(x)
    p = f32(p)
    positions = np.asarray(positions).astype(np.int32)
    cores = [(b, i) for b in range(B_) for i in range(4)]
    hT = [np.ascontiguousarray(x[b].T) for b in range(B_)]
    DEPTH = 4

    def next_inputs(layer):
        if layer >= DEPTH:
            return "final", dict(g_fin=chunked_vec(f32(final_norm)))
        j = layer // 2
        if layer % 2 == 0:
            return "even", dict(Whyb=blocked(f32(hyb_w_in[j])), g_attn=chunked_vec(f32(attn_norm[layer])))
        return "odd", dict(Wdn=blocked(perm_down(f32(mla_w_down[j]))), Wuq=blocked(perm_uq(f32(mla_w_uq[j]))), Wukv=blocked(f32(mla_w_ukv[j])),
                           g_attn=chunked_vec(f32(attn_norm[layer])), g_q=chunked_vec(f32(mla_q_norm[j])), g_kv=chunked_vec(f32(mla_kv_norm[j])),
                           invf=INVF)

    def run_token(layer_done, oT):
        nxt_layer = 0 if layer_done is None else layer_done + 1
        nxt, wn = next_inputs(nxt_layer)
        mix_in = layer_done is not None
        nc = _prog(("tok", mix_in, nxt), lambda: build_token_program(mix_in, nxt))
        common = dict(wn)
        if mix_in:
            L = layer_done
            j = L // 2
            w_mo = f32(hyb_w_out[j]) if L % 2 == 0 else f32(mla_w_o[j])
            common.update(Wmo=blocked(w_mo), Wfi=blocked(f32(ffn_w_in[L])), Wfo=blocked(f32(ffn_w_out[L])), Wpp=blocked(f32(ple_w_proj[L])),
                          Wpg=blocked(f32(ple_w_gate[L])), g_ffn=chunked_vec(f32(ffn_norm[L])), g_pe=chunked_vec(f32(ple_norm[L])),
                          g_pg=chunked_vec(f32(ple_gate_norm[L])),
                          convw=np.ascontiguousarray(f32(ffn_conv_w[L]).reshape(3, 22, 128).transpose(2, 0, 1)), convb=chunked_vec(f32(ffn_conv_b[L])))
        in_maps = []
        for (b, i) in cores:
            m = dict(common)
            m["hT"] = _with_halo(hT[b], i).reshape(8, 128, HALO + TC)
            if mix_in:
                m["oT"] = _with_halo(oT[b], i).reshape(8, 128, HALO + TC)
                m["pT"] = _with_halo(np.ascontiguousarray(p[layer_done, b].T), i).reshape(2, 128, HALO + TC)
            if nxt == "odd":
                m["pos"] = np.ascontiguousarray(np.broadcast_to(positions[b, i * TC:(i + 1) * TC][None], (128, TC)))
            in_maps.append(m)
        res = _run(nc, in_maps)
        out = {}
        if mix_in:
            for b in range(B_):
                hT[b] = np.concatenate([res[b * 4 + i]["hT_out"].reshape(1024, TC) for i in range(4)], axis=1)
        for key in ("projT", "qT", "kvT", "krT", "yT"):
            if key in res[0]:
                out[key] = [np.concatenate([res[b * 4 + i][key].reshape(-1, TC) for i in range(4)], axis=1) for b in range(B_)]
        return out

    out = run_token(None, None)
    for L in range(DEPTH):
        j = L // 2
        oT = [np.zeros((1024, S_), np.float32) for _ in range(B_)]
        if L % 2 == 0:
            proj = out["projT"]
            nc_sb = _prog(("sb",), lambda: build_sb_program(S_, 2))
            in_maps = []
            for b in range(B_):
                for g in range(4):
                    sl = lambda base: np.ascontiguousarray(proj[b][base + g * 128: base + (g + 1) * 128].reshape(2, 64, S_))
                    in_maps.append(dict(q=sl(0), k=sl(512), v=sl(1024)))
            res = _run(nc_sb, in_maps)
            for b in range(B_):
                for g in range(4):
                    oT[b][g * 128:(g + 1) * 128] = res[b * 4 + g]["oT"]
            nc_rw = _prog(("rw",), lambda: build_rwkv_program(S_))
            prm = [f32(a[j]) for a in (rw_mu, rw_w0, rw_w2, rw_a0, rw_a2, rw_g2, rw_k_k, rw_k_a, rw_r_k, rw_ln_w, rw_ln_b)]
            in_maps = []
            for b in range(B_):
                rwT = proj[b][1536:3360]
                for g in range(4):
                    in_maps.append(rwkv_host_inputs(rwT, g, *prm))
            res = _run(nc_rw, in_maps)
            for b in range(B_):
                for g in range(4):
                    oT[b][512 + g * 128: 512 + (g + 1) * 128] = res[b * 4 + g]["oT"]
        else:
            qT, kvT, krT = out["qT"], out["kvT"], out["krT"]
            nc_mla = _prog(("mla",), lambda: build_mla_program(S_, 4))
            in_maps = []
            for b in range(B_):
                qn = qT[b][0:1024].reshape(16, 64, S_)
                x1 = qT[b][1024:1280].reshape(16, 16, S_)
                x2 = qT[b][1280:1536].reshape(16, 16, S_)
                kv = kvT[b].reshape(16, 128, S_)
                kr = krT[b]
                for g in range(4):
                    hs = slice(4 * g, 4 * g + 4)
                    q = np.concatenate([qn[hs], x1[hs], x2[hs]], axis=1)
                    k = np.concatenate([kv[hs, :64], np.broadcast_to(kr[None], (4, 32, S_))], axis=1)
                    in_maps.append(dict(q=np.ascontiguousarray(q), k=np.ascontiguousarray(k), v=np.ascontiguousarray(kv[hs, 64:])))
            res = _run(nc_mla, in_maps)
            for b in range(B_):
                for g in range(4):
                    oT[b][g * 256:(g + 1) * 256] = res[b * 4 + g]["oT"].reshape(256, S_)
        out = run_token(L, oT)
    y = np.stack([np.ascontiguousarray(out["yT"][b].T) for b in range(B_)]).astype(np.float32)
    return y
```

```python
import numpy as np
from contextlib import ExitStack
import concourse.bass as bass
import concourse.mybir as mybir
from concourse.bass_utils import run_bass_kernel_spmd

F32, BF16, I32 = mybir.dt.float32, mybir.dt.bfloat16, mybir.dt.int32
AF = mybir.ActivationFunctionType
ALU = mybir.AluOpType
SAME_ENG_SYNC = True


class Tk:
    __slots__ = ("h", "w", "r", "name")

    def __init__(self, h, name=""):
        self.h = h
        self.w = None
        self.r = {}
        self.name = name

    def __getitem__(self, idx):
        return self.h[idx]


class SubTk:
    __slots__ = ("h", "p", "name")

    def __init__(self, parent, ap, name=""):
        self.h = ap
        self.p = parent
        self.name = name

    def __getitem__(self, idx):
        return self.h[idx]

    @property
    def w(self):
        return self.p.w

    @w.setter
    def w(self, v):
        self.p.w = v

    @property
    def r(self):
        return self.p.r

    @r.setter
    def r(self, v):
        self.p.r = v


class Ctx:
    def __init__(self, nc, stack, n_dma_sems=48):
        self.nc = nc
        self.engs = {"pe": nc.tensor, "act": nc.scalar, "dve": nc.vector, "pool": nc.gpsimd, "sp": nc.sync}
        self.esem = {k: stack.enter_context(nc.semaphore("s_" + k)) for k in ("pe", "act", "dve", "pool")}
        self.ecnt = {k: 0 for k in self.esem}
        self.dsem = [stack.enter_context(nc.semaphore(f"d{i}")) for i in range(n_dma_sems)]
        self.dcnt = [0] * n_dma_sems
        self.dnext = 0
        self.known = {k: {} for k in self.engs}
        self.nid = 0

    def _sem(self, k):
        return self.esem[k[1]] if k[0] == "e" else self.dsem[k[1]]

    def _wait(self, eng, events):
        need = {}
        for ev in events:
            if ev is None:
                continue
            k, v = ev
            if k[0] == "e" and k[1] == eng and (eng == "pe" or not SAME_ENG_SYNC):
                continue
            if need.get(k, 0) < v:
                need[k] = v
        kn = self.known[eng]
        for k, v in need.items():
            if kn.get(k, 0) >= v:
                continue
            self.engs[eng].wait_ge(self._sem(k), v)
            kn[k] = v

    @staticmethod
    def _deps(r, w):
        ev = []
        for t in r:
            ev.append(t.w)
        for t in w:
            ev.append(t.w)
            ev.extend(t.r.items())
        return ev

    @staticmethod
    def _mark(ev, r, w):
        k, v = ev
        for t in r:
            if t.r.get(k, 0) < v:
                t.r[k] = v
        for t in w:
            t.w = ev
            t.r = {}

    def op(self, eng, fn, r=(), w=()):
        self._wait(eng, self._deps(r, w))
        ins = fn(self.engs[eng])
        self.ecnt[eng] += 1
        ins.then_inc(self.esem[eng], 1)
        ev = (("e", eng), self.ecnt[eng])
        self._mark(ev, r, w)
        return ev

    def dma(self, eng, out, in_, r=(), w=()):
        i = self.dnext
        self.dnext = (i + 1) % len(self.dsem)
        deps = self._deps(r, w)
        if self.dcnt[i] > 0:
            deps.append((("d", i), self.dcnt[i]))
        self._wait(eng, deps)
        ins = self.engs[eng].dma_start(out=out, in_=in_)
        self.dcnt[i] += 16
        ins.then_inc(self.dsem[i], 16)
        ev = (("d", i), self.dcnt[i])
        self._mark(ev, r, w)
        return ev

    def finish(self, eng="sp"):
        evs = [(("d", i), c) for i, c in enumerate(self.dcnt) if c > 0]
        evs += [(("e", k), c) for k, c in self.ecnt.items() if c > 0]
        self._wait(eng, evs)

    def sb(self, shape, dt, name=None):
        self.nid += 1
        name = f"{name or 't'}_{self.nid}"
        return Tk(self.nc.alloc_sbuf_tensor(name, list(shape), dt), name)

    def ps(self, shape, dt=F32, name=None):
        self.nid += 1
        name = f"{name or 'p'}_{self.nid}"
        return Tk(self.nc.alloc_psum_tensor(name, list(shape), dt), name)

    def dram(self, name, shape, dt, kind="Internal"):
        return Tk(self.nc.dram_tensor(name, list(shape), dt, kind=kind).ap(), name)


def perm_down(w):
    out = np.zeros((w.shape[0], 7 * 128), w.dtype)
    out[:, :640] = w[:, :640]
    out[:, 640:656] = w[:, 640:656]
    out[:, 768:784] = w[:, 656:672]
    return out
def perm_uq(w):
    w3 = w.reshape(w.shape[0], 16, 96)
    return np.concatenate([w3[:, :, :64].reshape(-1, 1024), w3[:, :, 64:80].reshape(-1, 256), w3[:, :, 80:96].reshape(-1, 256)], axis=1)
_f = (10000.0 ** (-np.arange(16, dtype=np.float32) / 16)).astype(np.float32)
INVF = np.tile(_f, 8).reshape(128, 1).astype(np.float32)


NORM_EPS = 1e-6
D_MODEL = 1024
FFN = 2816
NFC = FFN // 128


def blocked(w):
    K, N = w.shape
    nb = (N + 127) // 128
    if N % 128:
        w = np.concatenate([w, np.zeros((K, nb * 128 - N), w.dtype)], axis=1)
    kc = K // 128
    return np.ascontiguousarray(w.reshape(kc, 128, nb, 128).transpose(2, 1, 0, 3))


def chunked_vec(v):
    return np.ascontiguousarray(v.reshape(-1, 128).T)


class TokEnv:
    def __init__(self, ctx, T):
        self.c = ctx
        self.T = T
        c = ctx
        self.ones = c.sb([128, 128], BF16, "ones")
        c.op("dve", lambda e: e.memset(self.ones[:], 1.0), w=[self.ones])
        self.ps_ss = c.ps([128, 512], F32, "ps_ss")
        self.ps_mm = [c.ps([128, 512], F32, "ps_mm") for _ in range(4)]
        self.mm_i = 0
        self.wbuf = {}
        self.sq = [c.sb([128, 512], BF16, "sq") for _ in range(2)]
        self.sq_i = 0
        self.rs = c.sb([128, 512], F32, "rs")

    def next_ps(self):
        p = self.ps_mm[self.mm_i % 4]
        self.mm_i += 1
        return p

    def wtile(self, KC):
        if KC not in self.wbuf:
            self.wbuf[KC] = [[self.c.sb([128, KC, 128], BF16, f"wb{KC}") for _ in range(3)], 0]
        ent = self.wbuf[KC]
        t = ent[0][ent[1] % 3]
        ent[1] += 1
        return t

    def load_w(self, Wd, n, KC):
        wt = self.wtile(KC)
        self.c.dma("pool", wt[:], Wd[n], r=[Wd], w=[wt])
        return wt

    def rstd(self, xs, D, T):
        c = self.c
        KC = len(xs)
        for k, (xt, xap) in enumerate(xs):
            sq = self.sq[self.sq_i % 2]
            self.sq_i += 1
            c.op("act", lambda e, sq=sq, xap=xap: e.activation(out=sq[:, :T], in_=xap, func=AF.Square), r=[xt], w=[sq])
            c.op("pe", lambda e, sq=sq, k=k: e.matmul(self.ps_ss[:, :T], lhsT=self.ones[:], rhs=sq[:, :T], start=(k == 0), stop=(k == KC - 1)),
                 r=[sq, self.ones], w=[self.ps_ss])
        c.op("act", lambda e: e.activation(out=self.rs[:, :T], in_=self.ps_ss[:, :T], func=AF.Ln, bias=float(D * NORM_EPS), scale=1.0), r=[self.ps_ss], w=[self.rs])
        c.op("act", lambda e: e.activation(out=self.rs[:, :T], in_=self.rs[:, :T], func=AF.Exp, scale=-0.5), r=[self.rs], w=[self.rs])
        return self.rs

    def normed(self, xs, D, g, xn, T):
        c = self.c
        rs = self.rstd(xs, D, T)
        for k, (xt, xap) in enumerate(xs):
            c.op("dve", lambda e, k=k, xap=xap: e.scalar_tensor_tensor(out=xn[:, k, :T], in0=xap, scalar=g[:, k:k + 1], in1=rs[:, :T], op0=ALU.mult, op1=ALU.mult),
                 r=[xt, g, rs], w=[xn])

    def linear(self, Wd, KC, xn, T, nblocks, consume, xn_r=None):
        c = self.c
        for n in nblocks:
            wt = self.load_w(Wd, n, KC)
            ps = self.next_ps()
            for k in range(KC):
                c.op("pe", lambda e, k=k, wt=wt, ps=ps: e.matmul(ps[:, :T], lhsT=wt[:, k, :], rhs=xn[:, k, :T], start=(k == 0), stop=(k == KC - 1)),
                     r=[wt, xn], w=[ps])
            consume(n, ps)


def prep_gain(c, gd, KC, D, name):
    g = c.sb([128, KC], F32, name)
    c.dma("sp", g[:], gd[:, :], r=[gd], w=[g])
    c.op("dve", lambda e: e.tensor_scalar(out=g[:], in0=g[:], scalar1=float(np.sqrt(D)), scalar2=None, op0=ALU.mult), r=[g], w=[g])
    return g


def build_token_program(mix_in, nxt, NT=4, T=512, parts=("halo", "mo", "ffn", "ple")):
    nc = bass.Bass("TRN2", target_bir_lowering=False)
    st = ExitStack()
    c = Ctx(nc, st)
    HALO = 32
    TT = HALO + NT * T
    NR = NT * T
    EI, EO = "ExternalInput", "ExternalOutput"
    d = {}
    d["hT"] = c.dram("hT", [8, 128, TT], F32, EI)
    hTv = d["hT"].h.rearrange("k p t -> p k t")
    env = TokEnv(c, T)
    h = c.sb([128, 8, T], F32, "h")
    xn = c.sb([128, 8, T], BF16, "xn")
    stage = [c.sb([128, T], F32, "stage") for _ in range(3)]
    stage_i = [0]

    def next_stage():
        s = stage[stage_i[0] % 3]
        stage_i[0] += 1
        return s

    hch = lambda TT_: [(h, h[:, k, :TT_]) for k in range(8)]

    if mix_in:
        d["oT"] = c.dram("oT", [8, 128, TT], F32, EI)
        oTv = d["oT"].h.rearrange("k p t -> p k t")
        d["pT"] = c.dram("pT", [2, 128, TT], F32, EI)
        pTv = d["pT"].h.rearrange("k p t -> p k t")
        d["Wmo"] = c.dram("Wmo", [8, 128, 8, 128], F32, EI)
        d["Wfi"] = c.dram("Wfi", [44, 128, 8, 128], F32, EI)
        d["Wfo"] = c.dram("Wfo", [8, 128, 22, 128], F32, EI)
        d["Wpp"] = c.dram("Wpp", [8, 128, 2, 128], F32, EI)
        d["Wpg"] = c.dram("Wpg", [8, 128, 8, 128], F32, EI)
        d["g_ffn"] = c.dram("g_ffn", [128, 8], F32, EI)
        d["g_pe"] = c.dram("g_pe", [128, 8], F32, EI)
        d["g_pg"] = c.dram("g_pg", [128, 8], F32, EI)
        d["convw"] = c.dram("convw", [128, 3, NFC], F32, EI)
        d["convb"] = c.dram("convb", [128, NFC], F32, EI)
        d["hT_out"] = c.dram("hT_out", [8, 128, NR], F32, EO)
        hOv = d["hT_out"].h.rearrange("k p t -> p k t")
        g_ffn = prep_gain(c, d["g_ffn"], 8, 1024, "g_ffn")
        g_pe = prep_gain(c, d["g_pe"], 8, 1024, "g_pe")
        g_pg = prep_gain(c, d["g_pg"], 8, 1024, "g_pg")
        convw = c.sb([128, 3, NFC], F32, "convw")
        convb = c.sb([128, NFC], F32, "convb")
        c.dma("sp", convw[:], d["convw"][:, :, :], r=[d["convw"]], w=[convw])
        c.dma("sp", convb[:], d["convb"][:, :], r=[d["convb"]], w=[convb])
        o_bf = c.sb([128, 8, T], BF16, "o_bf")
        p_bf = c.sb([128, 2, T], BF16, "p_bf")
        act = c.sb([128, NFC, T], BF16, "act")
        e_sb = c.sb([128, 8, T], F32, "e_sb")
        rs_e = c.sb([128, T], F32, "rs_e")
        carry = c.sb([128, NFC, 2], F32, "carry")
        gate_sb = [c.sb([128, T + 2], F32, "gate_sb") for _ in range(2)]
        ctmp = [c.sb([128, T], F32, "ctmp") for _ in range(2)]
        ge = [c.sb([128, T], F32, "ge") for _ in range(2)]
        sg = [c.sb([128, T], F32, "sg") for _ in range(2)]
        t1 = [c.sb([128, T], F32, "t1") for _ in range(2)]

    if nxt == "even":
        d["Whyb"] = c.dram("Whyb", [27, 128, 8, 128], F32, EI)
        d["g_attn"] = c.dram("g_attn", [128, 8], F32, EI)
        d["projT"] = c.dram("projT", [27, 128, NR], F32, EO)
        g_attn = prep_gain(c, d["g_attn"], 8, 1024, "g_attn")
    elif nxt == "odd":
        d["Wdn"] = c.dram("Wdn", [7, 128, 8, 128], F32, EI)
        d["Wuq"] = c.dram("Wuq", [12, 128, 3, 128], F32, EI)
        d["Wukv"] = c.dram("Wukv", [16, 128, 2, 128], F32, EI)
        d["g_attn"] = c.dram("g_attn", [128, 8], F32, EI)
        d["g_q"] = c.dram("g_q", [128, 3], F32, EI)
        d["g_kv"] = c.dram("g_kv", [128, 2], F32, EI)
        d["pos"] = c.dram("pos", [128, NR], I32, EI)
        d["invf"] = c.dram("invf", [128, 1], F32, EI)
        d["qT"] = c.dram("qT", [12, 128, NR], F32, EO)
        d["kvT"] = c.dram("kvT", [16, 128, NR], F32, EO)
        d["krT"] = c.dram("krT", [2, 16, NR], F32, EO)
        g_attn = prep_gain(c, d["g_attn"], 8, 1024, "g_attn")
        g_q = prep_gain(c, d["g_q"], 3, 384, "g_q")
        g_kv = prep_gain(c, d["g_kv"], 2, 256, "g_kv")
        invf = c.sb([128, 1], F32, "invf")
        c.dma("sp", invf[:], d["invf"][:, :], r=[d["invf"]], w=[invf])
        c_sb = c.sb([128, 7, T], F32, "c_sb")
        cn = c.sb([128, 3, T], BF16, "cn")
        pos_i = c.sb([128, T], I32, "pos_i")
        ang = c.sb([128, T], F32, "ang")
        tmpr = c.sb([128, T], F32, "tmpr")
        cosT = c.sb([128, T], F32, "cosT")
        sinT = c.sb([128, T], F32, "sinT")
        qr = c.sb([128, 4, T], F32, "qr")
        negpi = c.sb([128, 1], F32, "negpi")
        c.op("dve", lambda e: e.memset(negpi[:], -float(np.pi)), w=[negpi])
    elif nxt == "final":
        d["g_fin"] = c.dram("g_fin", [128, 8], F32, EI)
        d["yT"] = c.dram("yT", [8, 128, NR], F32, EO)
        g_fin = prep_gain(c, d["g_fin"], 8, 1024, "g_fin")

    def add_into_h(TT_):
        def f(n, ps):
            c.op("dve", lambda e: e.tensor_tensor(out=h[:, n, :TT_], in0=h[:, n, :TT_], in1=ps[:, :TT_], op=ALU.add), r=[ps, h], w=[h])
        return f

    def mixer_out(col0, TT_):
        c.dma("pool", o_bf[:, :, :TT_], oTv[:, :, col0:col0 + TT_], r=[d["oT"]], w=[o_bf])
        env.linear(d["Wmo"], 8, o_bf, TT_, range(8), add_into_h(TT_))

    def ffn_in(TT_, halo):
        env.normed(hch(TT_), 1024, g_ffn, xn, TT_)
        for fc in range(NFC):
            wg = env.load_w(d["Wfi"], NFC + fc, 8)
            ps_g = env.next_ps()
            for k in range(8):
                c.op("pe", lambda e: e.matmul(ps_g[:, :TT_], lhsT=wg[:, k, :], rhs=xn[:, k, :TT_], start=(k == 0), stop=(k == 7)), r=[wg, xn], w=[ps_g])
            if halo:
                c.op("act", lambda e: e.activation(out=carry[:, fc, :], in_=ps_g[:, TT_ - 2:TT_], func=AF.Copy), r=[ps_g], w=[carry])
                continue
            wu = env.load_w(d["Wfi"], fc, 8)
            ps_u = env.next_ps()
            for k in range(8):
                c.op("pe", lambda e: e.matmul(ps_u[:, :TT_], lhsT=wu[:, k, :], rhs=xn[:, k, :TT_], start=(k == 0), stop=(k == 7)), r=[wu, xn], w=[ps_u])
            gs = gate_sb[fc % 2]
            ct = ctmp[fc % 2]
            gg = ge[fc % 2]
            c.op("act", lambda e: e.activation(out=gs[:, 0:2], in_=carry[:, fc, :], func=AF.Copy), r=[carry], w=[gs])
            c.op("act", lambda e: e.activation(out=gs[:, 2:2 + TT_], in_=ps_g[:, :TT_], func=AF.Copy), r=[ps_g], w=[gs])
            c.op("act", lambda e: e.activation(out=carry[:, fc, :], in_=gs[:, TT_:TT_ + 2], func=AF.Copy), r=[gs], w=[carry])
            c.op("dve", lambda e: e.tensor_scalar(out=ct[:, :TT_], in0=gs[:, 2:2 + TT_], scalar1=convw[:, 2, fc:fc + 1], scalar2=convb[:, fc:fc + 1], op0=ALU.mult, op1=ALU.add),
                 r=[gs, convw, convb], w=[ct])
            c.op("dve", lambda e: e.scalar_tensor_tensor(out=ct[:, :TT_], in0=gs[:, 1:1 + TT_], scalar=convw[:, 1, fc:fc + 1], in1=ct[:, :TT_], op0=ALU.mult, op1=ALU.add),
                 r=[gs, convw, ct], w=[ct])
            c.op("dve", lambda e: e.scalar_tensor_tensor(out=ct[:, :TT_], in0=gs[:, 0:TT_], scalar=convw[:, 0, fc:fc + 1], in1=ct[:, :TT_], op0=ALU.mult, op1=ALU.add),
                 r=[gs, convw, ct], w=[ct])
            c.op("act", lambda e: e.activation(out=gg[:, :TT_], in_=ct[:, :TT_], func=AF.Gelu), r=[ct], w=[gg])
            c.op("dve", lambda e: e.tensor_tensor(out=act[:, fc, :TT_], in0=gg[:, :TT_], in1=ps_u[:, :TT_], op=ALU.mult), r=[gg, ps_u], w=[act])

    def ple(col0, TT_):
        c.dma("pool", p_bf[:, :, :TT_], pTv[:, :, col0:col0 + TT_], r=[d["pT"]], w=[p_bf])

        def ev_e(n, ps):
            c.op("act", lambda e: e.activation(out=e_sb[:, n, :TT_], in_=ps[:, :TT_], func=AF.Copy), r=[ps], w=[e_sb])
        env.linear(d["Wpp"], 2, p_bf, TT_, range(8), ev_e)
        rs = env.rstd([(e_sb, e_sb[:, k, :TT_]) for k in range(8)], 1024, TT_)
        c.op("act", lambda e: e.activation(out=rs_e[:, :TT_], in_=rs[:, :TT_], func=AF.Copy), r=[rs], w=[rs_e])
        env.normed(hch(TT_), 1024, g_pg, xn, TT_)

        def ev_g(n, ps):
            s_ = sg[n % 2]
            t_ = t1[n % 2]
            c.op("act", lambda e: e.activation(out=s_[:, :TT_], in_=ps[:, :TT_], func=AF.Sigmoid), r=[ps], w=[s_])
            c.op("dve", lambda e: e.scalar_tensor_tensor(out=t_[:, :TT_], in0=e_sb[:, n, :TT_], scalar=g_pe[:, n:n + 1], in1=rs_e[:, :TT_], op0=ALU.mult, op1=ALU.mult),
                 r=[e_sb, g_pe, rs_e], w=[t_])
            c.op("dve", lambda e: e.tensor_tensor(out=t_[:, :TT_], in0=t_[:, :TT_], in1=s_[:, :TT_], op=ALU.mult), r=[t_, s_], w=[t_])
            c.op("dve", lambda e: e.tensor_tensor(out=h[:, n, :TT_], in0=h[:, n, :TT_], in1=t_[:, :TT_], op=ALU.add), r=[t_, h], w=[h])
        env.linear(d["Wpg"], 8, xn, TT_, range(8), ev_g)

    def store_rows(dst_tk, dst_ap, n_part=128):
        def f(n, ps):
            s = next_stage()
            c.op("act", lambda e: e.activation(out=s[:n_part, :T], in_=ps[:n_part, :T], func=AF.Copy), r=[ps], w=[s])
            c.dma("sp", dst_ap(n), s[:n_part, :T], r=[s], w=[dst_tk])
        return f

    def rope_tables(r0):
        c.dma("sp", pos_i[:], d["pos"][:, r0:r0 + T], r=[d["pos"]], w=[pos_i])
        c.op("dve", lambda e: e.tensor_copy(out=ang[:], in_=pos_i[:]), r=[pos_i], w=[ang])
        c.op("dve", lambda e: e.tensor_scalar(out=ang[:], in0=ang[:], scalar1=invf[:, 0:1], scalar2=None, op0=ALU.mult), r=[ang, invf], w=[ang])
        for (off, dst) in ((0.0, sinT), (float(np.pi / 2), cosT)):
            if off:
                c.op("dve", lambda e: e.tensor_scalar(out=ang[:], in0=ang[:], scalar1=off, scalar2=None, op0=ALU.add), r=[ang], w=[ang])
            c.op("dve", lambda e: e.tensor_scalar(out=pos_i[:], in0=ang[:], scalar1=float(1.0 / (2 * np.pi)), scalar2=None, op0=ALU.mult), r=[ang], w=[pos_i])
            c.op("dve", lambda e: e.tensor_copy(out=tmpr[:], in_=pos_i[:]), r=[pos_i], w=[tmpr])
            c.op("dve", lambda e: e.scalar_tensor_tensor(out=dst[:], in0=tmpr[:], scalar=-6.28125, in1=ang[:], op0=ALU.mult, op1=ALU.add), r=[tmpr, ang], w=[dst])
            c.op("dve", lambda e: e.scalar_tensor_tensor(out=dst[:], in0=tmpr[:], scalar=-0.0019353071795864769, in1=dst[:], op0=ALU.mult, op1=ALU.add), r=[tmpr, dst], w=[dst])
            c.op("act", lambda e: e.activation(out=dst[:], in_=dst[:], func=AF.Sin), r=[dst], w=[dst])

    def rope_apply(x1_tk, x1, x2, P, out1_ap, out2_ap, dst_tk):
        s1 = next_stage()
        s2 = next_stage()
        tm = next_stage()
        c.op("dve", lambda e: e.tensor_tensor(out=s1[:P, :], in0=x1, in1=cosT[:P, :], op=ALU.mult), r=[x1_tk, cosT], w=[s1])
        c.op("dve", lambda e: e.tensor_tensor(out=tm[:P, :], in0=x2, in1=sinT[:P, :], op=ALU.mult), r=[x1_tk, sinT], w=[tm])
        c.op("dve", lambda e: e.tensor_tensor(out=s1[:P, :], in0=s1[:P, :], in1=tm[:P, :], op=ALU.subtract), r=[s1, tm], w=[s1])
        c.op("dve", lambda e: e.tensor_tensor(out=s2[:P, :], in0=x1, in1=sinT[:P, :], op=ALU.mult), r=[x1_tk, sinT], w=[s2])
        c.op("dve", lambda e: e.tensor_tensor(out=tm[:P, :], in0=x2, in1=cosT[:P, :], op=ALU.mult), r=[x1_tk, cosT, s1], w=[tm])
        c.op("dve", lambda e: e.tensor_tensor(out=s2[:P, :], in0=s2[:P, :], in1=tm[:P, :], op=ALU.add), r=[s2, tm], w=[s2])
        c.dma("sp", out1_ap, s1[:P, :], r=[s1], w=[dst_tk])
        c.dma("sp", out2_ap, s2[:P, :], r=[s2], w=[dst_tk])

    if mix_in:
        if "halo" in parts:
            c.dma("sp", h[:, :, :HALO], hTv[:, :, 0:HALO], r=[d["hT"]], w=[h])
            mixer_out(0, HALO)
            ffn_in(HALO, True)
        else:
            c.op("dve", lambda e: e.memset(carry[:], 0.0), w=[carry])
    for it in range(NT):
        col0 = HALO + it * T
        r0 = it * T
        c.dma("sp", h[:, :, :T], hTv[:, :, col0:col0 + T], r=[d["hT"]], w=[h])
        if mix_in:
            if "mo" in parts:
                mixer_out(col0, T)
            if "ffn" in parts:
                ffn_in(T, False)
                env.linear(d["Wfo"], NFC, act, T, range(8), add_into_h(T))
            if "ple" in parts:
                ple(col0, T)
            c.dma("sp", hOv[:, :, r0:r0 + T], h[:, :, :T], r=[h], w=[d["hT_out"]])
        if nxt == "even":
            env.normed(hch(T), 1024, g_attn, xn, T)
            env.linear(d["Whyb"], 8, xn, T, range(27), store_rows(d["projT"], lambda n: d["projT"][n, :, r0:r0 + T]))
        elif nxt == "odd":
            env.normed(hch(T), 1024, g_attn, xn, T)

            def ev_c(n, ps):
                c.op("act", lambda e: e.activation(out=c_sb[:, n, :], in_=ps[:, :T], func=AF.Copy), r=[ps], w=[c_sb])
            env.linear(d["Wdn"], 8, xn, T, range(7), ev_c)
            rope_tables(r0)
            rope_apply(c_sb, c_sb[0:16, 5, :], c_sb[0:16, 6, :], 16, d["krT"][0, :, r0:r0 + T], d["krT"][1, :, r0:r0 + T], d["krT"])
            env.normed([(c_sb, c_sb[:, k, :]) for k in range(3)], 384, g_q, cn, T)

            def ev_q(n, ps):
                if n < 8:
                    store_rows(d["qT"], lambda n_: d["qT"][n_, :, r0:r0 + T])(n, ps)
                else:
                    c.op("act", lambda e: e.activation(out=qr[:, n - 8, :], in_=ps[:, :T], func=AF.Copy), r=[ps], w=[qr])
            env.linear(d["Wuq"], 3, cn, T, range(12), ev_q)
            for j in range(2):
                rope_apply(qr, qr[:, j, :], qr[:, 2 + j, :], 128, d["qT"][8 + j, :, r0:r0 + T], d["qT"][10 + j, :, r0:r0 + T], d["qT"])
            env.normed([(c_sb, c_sb[:, 3 + k, :]) for k in range(2)], 256, g_kv, cn, T)
            env.linear(d["Wukv"], 2, cn, T, range(16), store_rows(d["kvT"], lambda n: d["kvT"][n, :, r0:r0 + T]))
        elif nxt == "final":
            rs = env.rstd(hch(T), 1024, T)
            for k in range(8):
                s = next_stage()
                c.op("dve", lambda e: e.scalar_tensor_tensor(out=s[:, :T], in0=h[:, k, :T], scalar=g_fin[:, k:k + 1], in1=rs[:, :T], op0=ALU.mult, op1=ALU.mult),
                     r=[h, g_fin, rs], w=[s])
                c.dma("sp", d["yT"][k, :, r0:r0 + T], s[:, :T], r=[s], w=[d["yT"]])
    c.finish()
    return nc, st


def make_masks(c, strict):
    masks = []
    for dl in range(4):
        mf = c.sb([128, 512], F32, "maskf")
        c.op("pool", lambda e: e.memset(mf[:], 1.0), w=[mf])
        c.op("pool", lambda e: e.affine_select(out=mf[:], in_=mf[:], pattern=[[1, 512]], compare_op=(ALU.is_gt if strict else ALU.is_ge),
                                                fill=0.0, base=-128 * dl, channel_multiplier=-1), r=[mf], w=[mf])
        mb = c.sb([128, 512], BF16, "maskb")
        c.op("pool", lambda e: e.tensor_copy(out=mb[:], in_=mf[:]), r=[mf], w=[mb])
        masks.append(mb)
    return masks


def make_ident(c, dt=BF16, n=128):
    f = c.sb([128, n], F32, "identf")
    c.op("pool", lambda e: e.memset(f[:], 1.0), w=[f])
    c.op("pool", lambda e: e.affine_select(out=f[:], in_=f[:], pattern=[[-1, n]], compare_op=ALU.is_equal, fill=0.0, base=0, channel_multiplier=1), r=[f], w=[f])
    if dt == F32:
        return f
    b = c.sb([128, n], dt, "identb")
    c.op("pool", lambda e: e.tensor_copy(out=b[:], in_=f[:]), r=[f], w=[b])
    return b


def build_mla_program(S=8192, NH=4):
    nc = bass.Bass("TRN2", target_bir_lowering=False)
    st = ExitStack()
    c = Ctx(nc, st)
    EI, EO = "ExternalInput", "ExternalOutput"
    qd = c.dram("q", [NH, 96, S], F32, EI)
    kd = c.dram("k", [NH, 96, S], F32, EI)
    vd = c.dram("v", [NH, 64, S], F32, EI)
    od = c.dram("oT", [NH, 64, S], F32, EO)
    NB = S // 128
    NQ = S // 512
    scale = 1.0 / float(np.sqrt(96.0))
    masks = make_masks(c, strict=False)
    ident = make_ident(c, BF16)
    Q = [c.sb([96, S], BF16, "Q") for _ in range(2)]
    K = [c.sb([96, S], BF16, "K") for _ in range(2)]
    VT = [c.sb([64, S], BF16, "VT") for _ in range(2)]
    Va = [c.sb([128, NB, 128], BF16, "Va") for _ in range(2)]
    for va in Va:
        c.op("pool", lambda e: e.memset(va[:], 1.0), w=[va])
    ps_s = [c.ps([128, 512], F32, "ps_s") for _ in range(4)]
    ps_o = [c.ps([128, 512], F32, "ps_o") for _ in range(2)]
    ps_t = c.ps([128, 64], F32, "ps_t")
    P = [c.sb([128, 512], BF16, "P") for _ in range(6)]
    rec = [c.sb([64, 512], F32, "rec") for _ in range(2)]
    ob = [c.sb([64, 512], F32, "ob") for _ in range(2)]
    it = 0
    for h in range(NH):
        q, k, vt, va = Q[h % 2], K[h % 2], VT[h % 2], Va[h % 2]
        c.dma("pool", q[:], qd[h], r=[qd], w=[q])
        c.dma("pool", k[:], kd[h], r=[kd], w=[k])
        c.dma("pool", vt[:], vd[h], r=[vd], w=[vt])
        for jb in range(NB):
            c.op("pe", lambda e: e.matmul(ps_t[:, :], lhsT=vt[:, jb * 128:(jb + 1) * 128], rhs=ident[0:64, 0:64], start=True, stop=True), r=[vt, ident], w=[ps_t])
            c.op("dve", lambda e: e.tensor_copy(out=va[:, jb, 0:64], in_=ps_t[:, :]), r=[ps_t], w=[va])
        blocks = []
        for qi in range(NQ):
            nkb = 4 * qi + 4
            for jb in range(nkb):
                blocks.append(dict(qi=qi, jb=jb, nkb=nkb, pss=ps_s[it % 4], p=P[it % 6]))
                it += 1

        def s1(b_):
            q0 = b_["qi"] * 512
            jb, pss, p = b_["jb"], b_["pss"], b_["p"]
            c.op("pe", lambda e: e.matmul(pss[:, :], lhsT=k[:, jb * 128:(jb + 1) * 128], rhs=q[:, q0:q0 + 512], start=True, stop=True), r=[k, q], w=[pss])
            c.op("act", lambda e: e.activation(out=p[:], in_=pss[:], func=AF.Exp, scale=scale), r=[pss], w=[p])
            dl = jb - 4 * b_["qi"]
            if dl >= 0:
                c.op("pool", lambda e: e.tensor_tensor(out=p[:], in0=p[:], in1=masks[dl][:], op=ALU.mult), r=[p, masks[dl]], w=[p])

        def s2(b_):
            qi, jb, nkb, p = b_["qi"], b_["jb"], b_["nkb"], b_["p"]
            q0 = qi * 512
            po = ps_o[qi % 2]
            c.op("pe", lambda e: e.matmul(po[:, :], lhsT=va[:, jb, :], rhs=p[:], start=(jb == 0), stop=(jb == nkb - 1)), r=[va, p], w=[po])
            if jb == nkb - 1:
                r_ = rec[qi % 2]
                o_ = ob[qi % 2]
                c.op("dve", lambda e: e.reciprocal(out=r_[:], in_=po[64:128, :]), r=[po], w=[r_])
                c.op("dve", lambda e: e.tensor_tensor(out=o_[:], in0=po[0:64, :], in1=r_[:], op=ALU.mult), r=[po, r_], w=[o_])
                c.dma("sp", od[h, :, q0:q0 + 512], o_[:], r=[o_], w=[od])
        SK = 2
        for t in range(len(blocks) + SK):
            if t < len(blocks):
                s1(blocks[t])
            if t - SK >= 0:
                s2(blocks[t - SK])
    c.finish()
    return nc, st


def make_tri(c, kind):
    f = c.sb([128, 128], F32, "trif")
    c.op("pool", lambda e: e.memset(f[:], -1.0), w=[f])
    if kind == "U":
        c.op("pool", lambda e: e.affine_select(out=f[:], in_=f[:], pattern=[[-1, 128]], compare_op=ALU.is_gt, fill=0.0, base=0, channel_multiplier=1), r=[f], w=[f])
    else:
        c.op("pool", lambda e: e.affine_select(out=f[:], in_=f[:], pattern=[[1, 128]], compare_op=ALU.is_ge, fill=0.0, base=0, channel_multiplier=-1), r=[f], w=[f])
    b = c.sb([128, 128], BF16, "trib")
    c.op("pool", lambda e: e.tensor_copy(out=b[:], in_=f[:]), r=[f], w=[b])
    return b


def emit_sb(c, qd, kd, vd, od, S, NH, ident, row0=0):
    NB = S // 128
    NQ = S // 512
    scale = 0.125
    masks = make_masks(c, strict=True)
    negU = make_tri(c, "U")
    negL = make_tri(c, "L")
    Q = [c.sb([64, S], BF16, "sQ") for _ in range(NH)]
    K = [c.sb([64, S], BF16, "sK") for _ in range(NH)]
    VT = [c.sb([64, S], BF16, "sVT") for _ in range(NH)]
    V = [c.sb([128, NB, 64], BF16, "sV") for _ in range(NH)]
    ps_z = [c.ps([128, 512], F32, "ps_z") for _ in range(3)]
    ps_a = [c.ps([128, 512], F32, "ps_a") for _ in range(3)]
    ps_o = [c.ps([64, 512], F32, "sps_o") for _ in range(NH)]
    ps_t = SubTk(ps_a[0], ps_a[0].h[:, 0:64], "sps_t")
    ND = 4
    ex = [c.sb([128, 512], F32, "ex") for _ in range(ND)]
    sp = [c.sb([128, 512], F32, "sp") for _ in range(ND)]
    spb = [c.sb([128, 512], BF16, "spb") for _ in range(2 * ND)]
    lsg = [c.sb([128, 512], F32, "lsg") for _ in range(ND)]
    A = [c.sb([128, 512], F32, "A") for _ in range(2 * ND)]
    wb = [c.sb([128, 512], BF16, "wb") for _ in range(ND)]
    ob = [c.sb([64, 512], F32, "sob") for _ in range(2)]
    for h in range(NH):
        c.dma("pool", Q[h][:], qd[h], r=[qd], w=[Q[h]])
        c.dma("pool", K[h][:], kd[h], r=[kd], w=[K[h]])
        c.dma("pool", VT[h][:], vd[h], r=[vd], w=[VT[h]])
    for h in range(NH):
        for jb in range(NB):
            c.op("pe", lambda e: e.matmul(ps_t[:, :], lhsT=VT[h][:, jb * 128:(jb + 1) * 128], rhs=ident[0:64, 0:64], start=True, stop=True), r=[VT[h], ident], w=[ps_t])
            c.op("dve", lambda e: e.tensor_copy(out=V[h][:, jb, :], in_=ps_t[:, :]), r=[ps_t], w=[V[h]])
    it = 0
    blocks = []
    for qi in range(NQ):
        nkb = 4 * qi + 4
        prev = [None] * NH
        for jb in range(nkb - 1, -1, -1):
            for h in range(NH):
                b_ = dict(qi=qi, jb=jb, h=h, nkb=nkb, pz=ps_z[it % 3], pa=ps_a[it % 3], e=ex[it % ND], s=sp[it % ND], l=lsg[it % ND], w=wb[it % ND],
                          sb=spb[it % (2 * ND)], a=A[it % (2 * ND)], prev=prev[h])
                it += 1
                prev[h] = b_
                blocks.append(b_)
    oi = [0]

    def s1(b_):
        qi, jb, h = b_["qi"], b_["jb"], b_["h"]
        q0 = qi * 512
        pz, e_, s_, sb_, l_ = b_["pz"], b_["e"], b_["s"], b_["sb"], b_["l"]
        dl = jb - 4 * qi
        c.op("pe", lambda e: e.matmul(pz[:, :], lhsT=K[h][:, jb * 128:(jb + 1) * 128], rhs=Q[h][:, q0:q0 + 512], start=True, stop=True), r=[K[h], Q[h]], w=[pz])
        c.op("act", lambda e: e.activation(out=e_[:], in_=pz[:], func=AF.Exp, scale=scale), r=[pz], w=[e_])
        c.op("act", lambda e: e.activation(out=s_[:], in_=e_[:], func=AF.Ln, bias=1.0, scale=1.0), r=[e_], w=[s_])
        if dl >= 0:
            c.op("pool", lambda e: e.tensor_tensor(out=sb_[:], in0=s_[:], in1=masks[dl][:], op=ALU.mult), r=[s_, masks[dl]], w=[sb_])
        else:
            c.op("pool", lambda e: e.tensor_copy(out=sb_[:], in_=s_[:]), r=[s_], w=[sb_])
        c.op("dve", lambda e: e.scalar_tensor_tensor(out=l_[:], in0=pz[:], scalar=scale, in1=s_[:], op0=ALU.mult, op1=ALU.subtract), r=[pz, s_], w=[l_])

    def s2(b_):
        qi, jb = b_["qi"], b_["jb"]
        pa, sb_, l_, w_, a_new, pv = b_["pa"], b_["sb"], b_["l"], b_["w"], b_["a"], b_["prev"]
        dl = jb - 4 * qi
        first = pv is None
        c.op("pe", lambda e: e.matmul(pa[:, :], lhsT=negU[:], rhs=sb_[:], start=True, stop=first), r=[negU, sb_], w=[pa])
        if not first:
            c.op("pe", lambda e: e.matmul(pa[:, :], lhsT=negL[:], rhs=pv["sb"][:], start=False, stop=True), r=[negL, pv["sb"]], w=[pa])
            c.op("dve", lambda e: e.tensor_tensor(out=a_new[:], in0=pa[:], in1=pv["a"][:], op=ALU.add), r=[pa, pv["a"]], w=[a_new])
        else:
            c.op("dve", lambda e: e.tensor_copy(out=a_new[:], in_=pa[:]), r=[pa], w=[a_new])
        c.op("dve", lambda e: e.tensor_tensor(out=l_[:], in0=l_[:], in1=a_new[:], op=ALU.add), r=[l_, a_new], w=[l_])
        c.op("act", lambda e: e.activation(out=w_[:], in_=l_[:], func=AF.Exp), r=[l_], w=[w_])
        if dl >= 0:
            c.op("pool", lambda e: e.tensor_tensor(out=w_[:], in0=w_[:], in1=masks[dl][:], op=ALU.mult), r=[w_, masks[dl]], w=[w_])

    def s3(b_):
        qi, jb, h, w_ = b_["qi"], b_["jb"], b_["h"], b_["w"]
        q0 = qi * 512
        po = ps_o[h]
        c.op("pe", lambda e: e.matmul(po[:, :], lhsT=V[h][:, jb, :], rhs=w_[:], start=(b_["prev"] is None), stop=(jb == 0)), r=[V[h], w_], w=[po])
        if jb == 0:
            o_ = ob[oi[0] % 2]
            oi[0] += 1
            c.op("act", lambda e: e.activation(out=o_[:], in_=po[:, :], func=AF.Copy), r=[po], w=[o_])
            c.dma("sp", od[row0 + h * 64:row0 + (h + 1) * 64, q0:q0 + 512], o_[:], r=[o_], w=[od])
    n = len(blocks)
    K2, K3 = 1, 2
    for t in range(n + K3):
        if t < n:
            s1(blocks[t])
        if 0 <= t - K2 < n:
            s2(blocks[t - K2])
        if 0 <= t - K3 < n:
            s3(blocks[t - K3])


def build_sb_program(S=8192, NH=2):
    nc = bass.Bass("TRN2", target_bir_lowering=False)
    st = ExitStack()
    c = Ctx(nc, st)
    EI, EO = "ExternalInput", "ExternalOutput"
    qd = c.dram("q", [NH, 64, S], F32, EI)
    kd = c.dram("k", [NH, 64, S], F32, EI)
    vd = c.dram("v", [NH, 64, S], F32, EI)
    od = c.dram("oT", [NH * 64, S], F32, EO)
    ident = make_ident(c, BF16)
    emit_sb(c, qd, kd, vd, od, S, NH, ident)
    c.finish()
    return nc, st


import os
def emit_rwkv(c, d, od, S, identf, row0=0):
    STOP = float(os.environ.get("RW_STOP", "9"))
    TS = 512
    NTL = S // TS
    NCH = TS // 64
    GN_EPS = 64e-5
    vec = c.sb([128, 8], F32, "rvec")
    c.dma("sp", vec[:], d["vecs"][:, :], r=[d["vecs"]], w=[vec])
    W0, A0, KK_, KA, RK, LNW, LNB = (vec[:, i:i + 1] for i in (0, 1, 2, 3, 5, 6, 7))
    omk = c.sb([128, 1], F32, "omk")
    c.op("dve", lambda e: e.tensor_scalar(out=omk[:], in0=vec[:, 3:4], scalar1=-1.0, scalar2=1.0, op0=ALU.mult, op1=ALU.add), r=[vec], w=[omk])
    mu3 = c.sb([128, 3], F32, "mu3")
    mu2 = c.sb([64, 2], F32, "mu2")
    mug = c.sb([128, 2], F32, "mug")
    c.dma("sp", mu3[:], d["mu_rkv"][:, :], r=[d["mu_rkv"]], w=[mu3])
    c.dma("sp", mu2[:], d["mu_wa"][:, :], r=[d["mu_wa"]], w=[mu2])
    c.dma("sp", mug[:], d["mu_gl"][:, :], r=[d["mu_gl"]], w=[mug])
    w2b = c.sb([64, 128], BF16, "w2b")
    a2b = c.sb([64, 128], BF16, "a2b")
    g2b = c.sb([128, 2, 128], BF16, "g2b")
    c.dma("pool", w2b[:], d["w2"][:, :], r=[d["w2"]], w=[w2b])
    c.dma("pool", a2b[:], d["a2"][:, :], r=[d["a2"]], w=[a2b])
    c.dma("pool", g2b[:], d["g2"].h.rearrange("k p n -> p k n"), r=[d["g2"]], w=[g2b])
    bones = c.sb([128, 128], F32, "bones")
    bavg = c.sb([128, 128], F32, "bavg")
    for (t_, val) in ((bones, 1.0), (bavg, 1.0 / 64)):
        c.op("pool", lambda e: e.memset(t_[:], 0.0), w=[t_])
        c.op("pool", lambda e: e.memset(t_[0:64, 0:64], val), w=[t_])
        c.op("pool", lambda e: e.memset(t_[64:128, 64:128], val), w=[t_])
    maskT = c.sb([64, 128], F32, "maskT")
    c.op("pool", lambda e: e.memset(maskT[:], 1.0), w=[maskT])
    c.op("pool", lambda e: e.affine_select(out=maskT[:, 0:64], in_=maskT[:, 0:64], pattern=[[1, 64]], compare_op=ALU.is_gt, fill=0.0, base=0, channel_multiplier=-1), r=[maskT], w=[maskT])
    c.op("pool", lambda e: e.affine_select(out=maskT[:, 64:128], in_=maskT[:, 64:128], pattern=[[1, 64]], compare_op=ALU.is_ge, fill=0.0, base=0, channel_multiplier=-1), r=[maskT], w=[maskT])
    maskSL = c.sb([64, 64], F32, "maskSL")
    c.op("pool", lambda e: e.memset(maskSL[:], 1.0), w=[maskSL])
    c.op("pool", lambda e: e.affine_select(out=maskSL[:], in_=maskSL[:], pattern=[[-1, 64]], compare_op=ALU.is_gt, fill=0.0, base=0, channel_multiplier=1), r=[maskSL], w=[maskSL])
    if STOP <= 1:
        return
    def T2(name, n=2, shape=(128, TS), dt=F32):
        return [c.sb(list(shape), dt, name) for _ in range(n)]
    X3 = T2("X3", 2, (128, 3, TS + 1))
    WA = T2("WA", 2, (64, 2, TS + 1))
    GL = T2("GL", 2, (128, 2, TS + 1))
    dtmp = T2("dtmp", 2)
    rm, km, vm = T2("rm"), T2("km"), T2("vm")
    wlm = T2("wlm", 1, (64, TS))[0]
    th = T2("th", 1, (64, TS), BF16)[0]
    alm = T2("alm", 1, (64, TS), BF16)[0]
    glm = T2("glm", 1, (128, 2, TS))[0]
    sgl = T2("sgl", 1, (128, 2, TS), BF16)[0]
    logw, av_, gg = T2("logw", 1)[0], T2("a", 1)[0], T2("gg")
    kk, kk2, rn, kkn, bv, tmpk = (T2(n_, 1)[0] for n_ in ("kk", "kk2", "rn", "kkn", "bv", "tmpk"))
    k2s = T2("k2")
    cA, cB = T2("cA", 1)[0], T2("cB", 1)[0]
    Winc, Wprev, Einv = T2("Winc"), T2("Wprev", 1)[0], T2("Einv", 1)[0]
    AR = T2("AR", 2, (128, NCH, 128))
    BT, KT = T2("BT"), T2("KT")
    yb = T2("yb")
    ps_big = [c.ps([128, TS], F32, "rps_big") for _ in range(2)]
    big_i = [0]

    def nbig():
        p = ps_big[big_i[0] % 2]
        big_i[0] += 1
        return p
    banks = [c.ps([128, 512], F32, f"rps_bank{i_}") for i_ in range(5)]
    sps = {}
    sps["tr0"] = SubTk(banks[0], banks[0].h[0:64, 0:128], "tr0")
    sps["tr1"] = SubTk(banks[0], banks[0].h[0:64, 128:256], "tr1")
    sps["zn"] = SubTk(banks[0], banks[0].h[:, 256:320], "zn")
    for h_ in range(2):
        bi, bs = banks[1 + h_], banks[3 + h_]
        sps[("g1", h_)] = SubTk(bi, bi.h[0:64, 0:128], "g1")
        sps[("g2", h_)] = SubTk(bi, bi.h[0:64, 128:256], "g2")
        for i_, n_ in enumerate(("n", "p", "pt", "m")):
            sps[(n_, h_)] = SubTk(bi, bi.h[0:64, 256 + 64 * i_:256 + 64 * i_ + 64], n_)
        for i_, n_ in enumerate(("x", "u", "y", "x2", "y2")):
            sps[(n_, h_)] = SubTk(bs, bs.h[0:64, 64 * i_:64 * i_ + 64], n_)
    Z = [c.sb([128, 64], F32, "Z") for _ in range(2)]
    c.op("dve", lambda e: e.memset(Z[0][:], 0.0), w=[Z[0]])
    c.op("dve", lambda e: e.memset(Z[1][:], 0.0), w=[Z[1]])
    btok = [[c.sb([64, 128], F32, "btok") for _ in range(2)] for _ in range(2)]
    ktok = [[c.sb([64, 128], F32, "ktok") for _ in range(2)] for _ in range(2)]
    for lst in (btok, ktok):
        for pr in lst:
            for t_ in pr:
                c.op("pool", lambda e: e.memset(t_[:], 0.0), w=[t_])
    vtok = [c.sb([64, 128], F32, "vtok") for _ in range(2)]
    Gb = [[c.sb([64, 128], F32, "Gb") for _ in range(2)] for _ in range(2)]
    Gk = [[c.sb([64, 128], F32, "Gk") for _ in range(2)] for _ in range(2)]
    Pm = [[c.sb([64, 64], F32, "Pm") for _ in range(2)] for _ in range(2)]
    PTm = [[c.sb([64, 64], F32, "PTm") for _ in range(2)] for _ in range(2)]
    MT = [[c.sb([64, 64], F32, "MT") for _ in range(2)] for _ in range(2)]
    Xs = [c.sb([64, 64], F32, "Xs") for _ in range(2)]
    Us = [c.sb([64, 64], F32, "Us") for _ in range(2)]
    ytmp = [c.sb([64, 64], F32, "ytmp") for _ in range(2)]
    v3 = lambda ap: ap.rearrange("p (c t) -> p c t", t=64)
    def precompute(tl):
        pb = tl % 2
        k2 = k2s[pb]
        t0 = tl * TS
        pb = tl % 2
        x3, wa, gl = X3[pb], WA[pb], GL[pb]
        for (buf, src, P_) in ((x3, d["rkv"].h.rearrange("k p s -> p k s"), 128), (wa, d["wa"].h.rearrange("k p s -> p k s"), 64), (gl, d["gl"].h.rearrange("k p s -> p k s"), 128)):
            srct = {id(x3): d["rkv"], id(wa): d["wa"], id(gl): d["gl"]}[id(buf)]
            if t0 == 0:
                c.op("pool", lambda e: e.memset(buf[:, :, 0:1], 0.0), w=[buf])
                c.dma("sp", buf[:, :, 1:TS + 1], src[:, :, 0:TS], r=[srct], w=[buf])
            else:
                c.dma("sp", buf[:, :, :], src[:, :, t0 - 1:t0 + TS], r=[srct], w=[buf])
        def lerp(buf, k, P_, mu_ap, mu_tk, out_ap, out_tk, dt_):
            c.op("pool", lambda e: e.tensor_tensor(out=dt_[:P_, :], in0=buf[:P_, k, 0:TS], in1=buf[:P_, k, 1:TS + 1], op=ALU.subtract), r=[buf], w=[dt_])
            c.op("dve", lambda e: e.scalar_tensor_tensor(out=out_ap, in0=dt_[:P_, :], scalar=mu_ap, in1=buf[:P_, k, 1:TS + 1], op0=ALU.mult, op1=ALU.add), r=[dt_, buf, mu_tk], w=[out_tk])
        r_, k_, v_ = rm[pb], km[pb], vm[pb]
        lerp(x3, 0, 128, mu3[:, 0:1], mu3, r_[:, :], r_, dtmp[0])
        lerp(x3, 1, 128, mu3[:, 1:2], mu3, k_[:, :], k_, dtmp[1])
        lerp(x3, 2, 128, mu3[:, 2:3], mu3, v_[:, :], v_, dtmp[0])
        lerp(wa, 0, 64, mu2[:, 0:1], mu2, wlm[:, :], wlm, dtmp[1])
        lerp(wa, 1, 64, mu2[:, 1:2], mu2, alm[:, :], alm, dtmp[0])
        lerp(gl, 0, 128, mug[:, 0:1], mug, glm[:, 0, :], glm, dtmp[1])
        lerp(gl, 1, 128, mug[:, 1:2], mug, glm[:, 1, :], glm, dtmp[0])
        c.op("act", lambda e: e.activation(out=th[:], in_=wlm[:], func=AF.Tanh), r=[wlm], w=[th])
        c.op("act", lambda e: e.activation(out=sgl[:], in_=glm[:], func=AF.Sigmoid), r=[glm], w=[sgl])
        pw = nbig()
        c.op("pe", lambda e: e.matmul(pw[:, :], lhsT=w2b[:], rhs=th[:], start=True, stop=True), r=[w2b, th], w=[pw])
        c.op("act", lambda e: e.activation(out=logw[:], in_=pw[:], func=AF.Sigmoid, bias=W0, scale=1.0), r=[pw, vec], w=[logw])
        c.op("dve", lambda e: e.tensor_scalar(out=logw[:], in0=logw[:], scalar1=-float(np.exp(-0.5)), scalar2=None, op0=ALU.mult), r=[logw], w=[logw])
        pa = nbig()
        c.op("pe", lambda e: e.matmul(pa[:, :], lhsT=a2b[:], rhs=alm[:], start=True, stop=True), r=[a2b, alm], w=[pa])
        c.op("act", lambda e: e.activation(out=av_[:], in_=pa[:], func=AF.Sigmoid, bias=A0, scale=1.0), r=[pa, vec], w=[av_])
        pg = nbig()
        for kx in range(2):
            c.op("pe", lambda e: e.matmul(pg[:, :], lhsT=g2b[:, kx, :], rhs=sgl[:, kx, :], start=(kx == 0), stop=(kx == 1)), r=[g2b, sgl], w=[pg])
        g_ = gg[pb]
        c.op("act", lambda e: e.activation(out=g_[:], in_=pg[:], func=AF.Copy), r=[pg], w=[g_])
        c.op("dve", lambda e: e.tensor_scalar(out=kk[:], in0=k_[:], scalar1=KK_, scalar2=None, op0=ALU.mult), r=[k_, vec], w=[kk])
        c.op("act", lambda e: e.activation(out=kk2[:], in_=kk[:], func=AF.Square), r=[kk], w=[kk2])
        pss = nbig()
        c.op("pe", lambda e: e.matmul(pss[:, :], lhsT=bones[:], rhs=kk2[:], start=True, stop=True), r=[bones, kk2], w=[pss])
        c.op("act", lambda e: e.activation(out=rn[:], in_=pss[:], func=AF.Ln, bias=1e-24, scale=1.0), r=[pss], w=[rn])
        c.op("act", lambda e: e.activation(out=rn[:], in_=rn[:], func=AF.Exp, scale=-0.5), r=[rn], w=[rn])
        c.op("dve", lambda e: e.tensor_tensor(out=kkn[:], in0=kk[:], in1=rn[:], op=ALU.mult), r=[kk, rn], w=[kkn])
        c.op("dve", lambda e: e.tensor_scalar(out=tmpk[:], in0=av_[:], scalar1=KA, scalar2=omk[:, 0:1], op0=ALU.mult, op1=ALU.add), r=[av_, vec, omk], w=[tmpk])
        c.op("dve", lambda e: e.tensor_tensor(out=k2[:], in0=k_[:], in1=tmpk[:], op=ALU.mult), r=[k_, tmpk], w=[k2])
        c.op("dve", lambda e: e.tensor_tensor(out=bv[:], in0=kkn[:], in1=av_[:], op=ALU.mult), r=[kkn, av_], w=[bv])
        src = logw
        for si, s_ in enumerate((1, 2, 4, 8, 16, 32)):
            dst = cA if si % 2 == 0 else cB
            c.op("pool", lambda e: e.tensor_tensor(out=v3(dst[:, :])[:, :, s_:], in0=v3(src[:, :])[:, :, s_:], in1=v3(src[:, :])[:, :, :64 - s_], op=ALU.add), r=[src], w=[dst])
            c.op("pool", lambda e: e.tensor_copy(out=v3(dst[:, :])[:, :, :s_], in_=v3(src[:, :])[:, :, :s_]), r=[src], w=[dst])
            src = dst
        cum = src
        wi = Winc[pb]
        c.op("act", lambda e: e.activation(out=wi[:], in_=cum[:], func=AF.Exp), r=[cum], w=[wi])
        c.op("act", lambda e: e.activation(out=Einv[:], in_=cum[:], func=AF.Exp, scale=-1.0), r=[cum], w=[Einv])
        c.op("pool", lambda e: e.tensor_tensor(out=cA[:], in0=cum[:], in1=logw[:], op=ALU.subtract), r=[cum, logw], w=[cA])
        c.op("act", lambda e: e.activation(out=Wprev[:], in_=cA[:], func=AF.Exp), r=[cA], w=[Wprev])
        ar, bt, kt = AR[pb], BT[pb], KT[pb]
        c.op("dve", lambda e: e.tensor_tensor(out=ar[:, :, 64:128], in0=v3(r_[:, :]), in1=v3(wi[:, :]), op=ALU.mult), r=[r_, wi], w=[ar])
        c.op("dve", lambda e: e.scalar_tensor_tensor(out=ar[:, :, 0:64], in0=v3(kkn[:, :]), scalar=-1.0, in1=v3(Wprev[:, :]), op0=ALU.mult, op1=ALU.mult), r=[kkn, Wprev], w=[ar])
        c.op("dve", lambda e: e.tensor_tensor(out=bt[:], in0=bv[:], in1=Einv[:], op=ALU.mult), r=[bv, Einv], w=[bt])
        c.op("dve", lambda e: e.tensor_tensor(out=kt[:], in0=k2[:], in1=Einv[:], op=ALU.mult), r=[k2, Einv], w=[kt])

    def postproc(tl):
        pb = tl % 2
        t0 = tl * TS
        k2 = k2s[pb]
        r_, v_, g_, y_ = rm[pb], vm[pb], gg[pb], yb[pb]
        pm = nbig()
        c.op("pe", lambda e: e.matmul(pm[:, :], lhsT=bavg[:], rhs=y_[:], start=True, stop=True), r=[bavg, y_], w=[pm])
        c.op("dve", lambda e: e.tensor_tensor(out=y_[:], in0=y_[:], in1=pm[:], op=ALU.subtract), r=[y_, pm], w=[y_])
        c.op("act", lambda e: e.activation(out=kk2[:], in_=y_[:], func=AF.Square), r=[y_], w=[kk2])
        pv = nbig()
        c.op("pe", lambda e: e.matmul(pv[:, :], lhsT=bavg[:], rhs=kk2[:], start=True, stop=True), r=[bavg, kk2], w=[pv])
        c.op("act", lambda e: e.activation(out=rn[:], in_=pv[:], func=AF.Ln, bias=GN_EPS, scale=1.0), r=[pv], w=[rn])
        c.op("act", lambda e: e.activation(out=rn[:], in_=rn[:], func=AF.Exp, scale=-0.5), r=[rn], w=[rn])
        c.op("dve", lambda e: e.tensor_tensor(out=y_[:], in0=y_[:], in1=rn[:], op=ALU.mult), r=[y_, rn], w=[y_])
        c.op("dve", lambda e: e.tensor_scalar(out=y_[:], in0=y_[:], scalar1=LNW, scalar2=LNB, op0=ALU.mult, op1=ALU.add), r=[y_, vec], w=[y_])
        c.op("dve", lambda e: e.scalar_tensor_tensor(out=kk[:], in0=r_[:], scalar=RK, in1=k2[:], op0=ALU.mult, op1=ALU.mult), r=[r_, vec, k2], w=[kk])
        pb_ = nbig()
        c.op("pe", lambda e: e.matmul(pb_[:, :], lhsT=bones[:], rhs=kk[:], start=True, stop=True), r=[bones, kk], w=[pb_])
        c.op("dve", lambda e: e.tensor_tensor(out=tmpk[:], in0=pb_[:], in1=v_[:], op=ALU.mult), r=[pb_, v_], w=[tmpk])
        c.op("dve", lambda e: e.tensor_tensor(out=y_[:], in0=y_[:], in1=tmpk[:], op=ALU.add), r=[y_, tmpk], w=[y_])
        c.op("dve", lambda e: e.tensor_tensor(out=y_[:], in0=y_[:], in1=g_[:], op=ALU.mult), r=[y_, g_], w=[y_])
        c.dma("sp", od[row0:row0 + 128, t0:t0 + TS], y_[:], r=[y_], w=[od])


    def gen_T(tl, ch):
        pb, cp = tl % 2, ch % 2
        bt, kt, v_ = BT[pb], KT[pb], vm[pb]
        cs = slice(ch * 64, ch * 64 + 64)
        c.op("pe", lambda e: e.matmul(sps["tr0"][:, :], lhsT=bt[:, cs], rhs=identf[:, :], start=True, stop=True), r=[bt, identf], w=[sps["tr0"]])
        for h in range(2):
            c.op("act", lambda e: e.activation(out=btok[cp][h][:, 64 * h:64 * h + 64], in_=sps["tr0"][:, 64 * h:64 * h + 64], func=AF.Copy), r=[sps["tr0"]], w=[btok[cp][h]])
        yield
        c.op("pe", lambda e: e.matmul(sps["tr1"][:, :], lhsT=kt[:, cs], rhs=identf[:, :], start=True, stop=True), r=[kt, identf], w=[sps["tr1"]])
        for h in range(2):
            c.op("act", lambda e: e.activation(out=ktok[cp][h][:, 64 * h:64 * h + 64], in_=sps["tr1"][:, 64 * h:64 * h + 64], func=AF.Copy), r=[sps["tr1"]], w=[ktok[cp][h]])
        yield
        c.op("pe", lambda e: e.matmul(sps["tr0"][:, :], lhsT=v_[:, cs], rhs=identf[:, :], start=True, stop=True), r=[v_, identf], w=[sps["tr0"]])
        c.op("act", lambda e: e.activation(out=vtok[cp][:], in_=sps["tr0"][:, :], func=AF.Copy), r=[sps["tr0"]], w=[vtok[cp]])
        yield

    def gen_I(tl, ch, h):
        pb, cp = tl % 2, ch % 2
        ar, bt, kt = AR[pb], BT[pb], KT[pb]
        cs = slice(ch * 64, ch * 64 + 64)
        hp = slice(64 * h, 64 * h + 64)
        gb, gk, mt = Gb[cp][h], Gk[cp][h], MT[cp][h]
        P = lambda n_: sps[(n_, h)]
        c.op("pe", lambda e: e.matmul(P("g1")[:, :], lhsT=bt[hp, cs], rhs=ar[hp, ch, :], start=True, stop=True), r=[bt, ar], w=[P("g1")])
        c.op("dve", lambda e: e.tensor_tensor(out=gb[:], in0=P("g1")[:, :], in1=maskT[:], op=ALU.mult), r=[P("g1"), maskT], w=[gb])
        yield
        c.op("pe", lambda e: e.matmul(P("g2")[:, :], lhsT=kt[hp, cs], rhs=ar[hp, ch, :], start=True, stop=True), r=[kt, ar], w=[P("g2")])
        c.op("dve", lambda e: e.tensor_tensor(out=gk[:], in0=P("g2")[:, :], in1=maskT[:], op=ALU.mult), r=[P("g2"), maskT], w=[gk])
        yield
        c.op("pe", lambda e: e.matmul(P("n")[:, :], lhsT=ar[hp, ch, 0:64], rhs=bt[hp, cs], start=True, stop=True), r=[ar, bt], w=[P("n")])
        P_, PT_ = Pm[h][0], PTm[h][0]
        c.op("dve", lambda e: e.tensor_tensor(out=P_[:], in0=P("n")[:, :], in1=maskSL[:], op=ALU.mult), r=[P("n"), maskSL], w=[P_])
        c.op("act", lambda e: e.activation(out=PT_[:], in_=gb[:, 0:64], func=AF.Copy), r=[gb], w=[PT_])
        c.op("dve", lambda e: e.tensor_tensor(out=mt[:], in0=gb[:, 0:64], in1=identf[0:64, 0:64], op=ALU.add), r=[gb, identf], w=[mt])
        yield
        for kx in range(5):
            Pn, PTn = Pm[h][(kx + 1) % 2], PTm[h][(kx + 1) % 2]
            c.op("pe", lambda e: e.matmul(P("p")[:, :], lhsT=PT_[:], rhs=P_[:], start=True, stop=True), r=[PT_, P_], w=[P("p")])
            c.op("act", lambda e: e.activation(out=Pn[:], in_=P("p")[:, :], func=AF.Copy), r=[P("p")], w=[Pn])
            yield
            if kx < 4:
                c.op("pe", lambda e: e.matmul(P("pt")[:, :], lhsT=P_[:], rhs=PT_[:], start=True, stop=True), r=[PT_, P_], w=[P("pt")])
                c.op("act", lambda e: e.activation(out=PTn[:], in_=P("pt")[:, :], func=AF.Copy), r=[P("pt")], w=[PTn])
                yield
            c.op("pe", lambda e: e.matmul(P("m")[:, :], lhsT=Pn[:], rhs=mt[:], start=True, stop=True), r=[Pn, mt], w=[P("m")])
            c.op("dve", lambda e: e.tensor_tensor(out=mt[:], in0=mt[:], in1=P("m")[:, :], op=ALU.add), r=[mt, P("m")], w=[mt])
            yield
            P_, PT_ = Pn, PTn

    def gen_S(tl, ch, h, zo):
        pb, cp = tl % 2, ch % 2
        ar, y_ = AR[pb], yb[pb]
        cs = slice(ch * 64, ch * 64 + 64)
        hp = slice(64 * h, 64 * h + 64)
        gb, gk, mt, xs_, us_, vt_ = Gb[cp][h], Gk[cp][h], MT[cp][h], Xs[h], Us[h], vtok[cp]
        P = lambda n_: sps[(n_, h)]
        c.op("pe", lambda e: e.matmul(P("x")[:, :], lhsT=ar[hp, ch, 0:64], rhs=zo[hp, :], start=True, stop=True), r=[ar, zo], w=[P("x")])
        c.op("act", lambda e: e.activation(out=xs_[:], in_=P("x")[:, :], func=AF.Copy), r=[P("x")], w=[xs_])
        yield
        c.op("pe", lambda e: e.matmul(P("x2")[:, :], lhsT=gk[:, 0:64], rhs=vt_[:, hp], start=True, stop=True), r=[gk, vt_], w=[P("x2")])
        c.op("dve", lambda e: e.tensor_tensor(out=xs_[:], in0=xs_[:], in1=P("x2")[:, :], op=ALU.add), r=[xs_, P("x2")], w=[xs_])
        yield
        c.op("pe", lambda e: e.matmul(P("u")[:, :], lhsT=mt[:], rhs=xs_[:], start=True, stop=True), r=[mt, xs_], w=[P("u")])
        c.op("act", lambda e: e.activation(out=us_[:], in_=P("u")[:, :], func=AF.Copy), r=[P("u")], w=[us_])
        yield
        yt_ = ytmp[h]
        c.op("pe", lambda e: e.matmul(P("y")[:, :], lhsT=zo[hp, :], rhs=ar[hp, ch, 64:128], start=True, stop=True), r=[zo, ar], w=[P("y")])
        c.op("act", lambda e: e.activation(out=yt_[:], in_=P("y")[:, :], func=AF.Copy), r=[P("y")], w=[yt_])
        yield
        c.op("pe", lambda e: e.matmul(P("y2")[:, :], lhsT=us_[:], rhs=gb[:, 64:128], start=True, stop=False), r=[us_, gb], w=[P("y2")])
        c.op("pe", lambda e: e.matmul(P("y2")[:, :], lhsT=vt_[:, hp], rhs=gk[:, 64:128], start=False, stop=True), r=[vt_, gk], w=[P("y2")])
        c.op("dve", lambda e: e.tensor_tensor(out=y_[hp, cs], in0=yt_[:], in1=P("y2")[:, :], op=ALU.add), r=[yt_, P("y2")], w=[y_])
        yield

    def z_update(tl, ch, zo, zn):
        pb, cp = tl % 2, ch % 2
        wi, vt_ = Winc[pb], vtok[cp]
        for h in range(2):
            c.op("pe", lambda e: e.matmul(sps["zn"][:, :], lhsT=btok[cp][h][:], rhs=Us[h][:], start=(h == 0), stop=False), r=[btok[cp][h], Us[h]], w=[sps["zn"]])
            c.op("pe", lambda e: e.matmul(sps["zn"][:, :], lhsT=ktok[cp][h][:], rhs=vt_[:, 64 * h:64 * h + 64], start=False, stop=(h == 1)), r=[ktok[cp][h], vt_], w=[sps["zn"]])
        c.op("dve", lambda e: e.tensor_tensor(out=zn[:], in0=zo[:], in1=sps["zn"][:, :], op=ALU.add), r=[zo, sps["zn"]], w=[zn])
        c.op("dve", lambda e: e.tensor_scalar(out=zn[:], in0=zn[:], scalar1=wi[:, ch * 64 + 63:ch * 64 + 64], scalar2=None, op0=ALU.mult), r=[zn, wi], w=[zn])

    def interleave(gens):
        gens = list(gens)
        while gens:
            for g_ in list(gens):
                try:
                    next(g_)
                except StopIteration:
                    gens.remove(g_)

    chunks = [(tl, ch) for tl in range(NTL) for ch in range(NCH)]
    precompute(0)
    interleave([gen_T(0, 0), gen_I(0, 0, 0), gen_I(0, 0, 1)])
    for gi, (tl, ch) in enumerate(chunks):
        zo, zn = Z[gi % 2], Z[(gi + 1) % 2]
        gens = [gen_S(tl, ch, 0, zo), gen_S(tl, ch, 1, zo)]
        if gi + 1 < len(chunks):
            ntl, nch = chunks[gi + 1]
            if nch == 0:
                precompute(ntl)
            gens = [gen_T(ntl, nch)] + gens + [gen_I(ntl, nch, 0), gen_I(ntl, nch, 1)]
        interleave(gens)
        z_update(tl, ch, zo, zn)
        if ch == NCH - 1:
            postproc(tl)


def rwkv_dram(c, S):
    EI = "ExternalInput"
    d = {}
    for n_, sh in (("rkv", [3, 128, S]), ("wa", [2, 64, S]), ("gl", [2, 128, S]), ("vecs", [128, 8]), ("mu_rkv", [128, 3]), ("mu_wa", [64, 2]), ("mu_gl", [128, 2]),
                   ("w2", [64, 128]), ("a2", [64, 128]), ("g2", [2, 128, 128])):
        d[n_] = c.dram(n_, sh, F32, EI)
    return d


def build_rwkv_program(S=8192):
    nc = bass.Bass("TRN2", target_bir_lowering=False)
    st = ExitStack()
    c = Ctx(nc, st)
    d = rwkv_dram(c, S)
    od = c.dram("oT", [128, S], F32, "ExternalOutput")
    identf = make_ident(c, F32)
    emit_rwkv(c, d, od, S, identf)
    c.finish()
    return nc, st


def rwkv_host_inputs(rw_T, hg, mu, w0, w2, a0, a2, g2, k_k, k_a, r_k, ln_w, ln_b):
    S = rw_T.shape[1]
    ch = slice(hg * 128, hg * 128 + 128)
    rkv = np.stack([rw_T[0:512][ch], rw_T[512:1024][ch], rw_T[1024:1536][ch]])
    wa = np.stack([rw_T[1536:1600], rw_T[1600:1664]])
    gl = np.zeros((2, 128, S), np.float32)
    gl[0] = rw_T[1664:1792]
    gl[1, :32] = rw_T[1792:1824]
    vecs = np.zeros((128, 8), np.float32)
    for i, v in ((0, w0), (1, a0), (2, k_k), (3, k_a), (5, r_k.reshape(-1)), (6, ln_w), (7, ln_b)):
        vecs[:, i] = v[ch]
    mu_rkv = np.stack([mu[0:512][ch], mu[512:1024][ch], mu[1024:1536][ch]], axis=1)
    mu_wa = np.stack([mu[1536:1600], mu[1600:1664]], axis=1)
    mu_gl = np.zeros((128, 2), np.float32)
    mu_gl[:, 0] = mu[1664:1792]
    mu_gl[:32, 1] = mu[1792:1824]
    g2p = np.zeros((2, 128, 128), np.float32)
    g2p[0] = g2[0:128, ch]
    g2p[1, :32] = g2[128:160, ch]
    return dict(rkv=np.ascontiguousarray(rkv), wa=np.ascontiguousarray(wa), gl=gl, vecs=vecs, mu_rkv=np.ascontiguousarray(mu_rkv), mu_wa=np.ascontiguousarray(mu_wa),
                mu_gl=mu_gl, w2=np.ascontiguousarray(w2[:, ch]), a2=np.ascontiguousarray(a2[:, ch]), g2=g2p)


_PROGS = {}


def _prog(key, builder):
    if key not in _PROGS:
        _PROGS[key] = builder()
    return _PROGS[key][0]


def _run(nc, in_maps):
    res = run_bass_kernel_spmd(nc, in_maps, core_ids=list(range(8)))
    return res.results


B_, S_, D_ = 2, 8192, 1024
TC = 2048
HALO = 32


def _with_halo(xT, i):
    C = xT.shape[0]
    out = np.zeros((C, HALO + TC), np.float32)
    lo = i * TC - HALO
    if lo >= 0:
        out[:] = xT[:, lo:(i + 1) * TC]
    else:
        out[:, HALO:] = xT[:, 0:TC]
    return out


def kernel(x, p, positions, attn_norm, ffn_norm, ffn_w_in, ffn_conv_w, ffn_conv_b, ffn_w_out,
           ple_w_proj, ple_norm, ple_gate_norm, ple_w_gate,
           hyb_w_in, hyb_w_out, rw_mu, rw_w0, rw_w2, rw_a0, rw_a2, rw_g2, rw_k_k, rw_k_a,
           rw_r_k, rw_ln_w, rw_ln_b,
           mla_w_down, mla_q_norm, mla_kv_norm, mla_w_uq, mla_w_ukv, mla_w_o, final_norm):
    f32 = lambda a: np.ascontiguousarray(np.asarray(a), dtype=np.float32)
    x = f32(x)
    p = f32(p)
    positions = np.asarray(positions).astype(np.int32)
    cores = [(b, i) for b in range(B_) for i in range(4)]
    hT = [np.ascontiguousarray(x[b].T) for b in range(B_)]
    DEPTH = 4

    def next_inputs(layer):
        if layer >= DEPTH:
            return "final", dict(g_fin=chunked_vec(f32(final_norm)))
        j = layer // 2
        if layer % 2 == 0:
            return "even", dict(Whyb=blocked(f32(hyb_w_in[j])), g_attn=chunked_vec(f32(attn_norm[layer])))
        return "odd", dict(Wdn=blocked(perm_down(f32(mla_w_down[j]))), Wuq=blocked(perm_uq(f32(mla_w_uq[j]))), Wukv=blocked(f32(mla_w_ukv[j])),
                           g_attn=chunked_vec(f32(attn_norm[layer])), g_q=chunked_vec(f32(mla_q_norm[j])), g_kv=chunked_vec(f32(mla_kv_norm[j])),
                           invf=INVF)

    def run_token(layer_done, oT):
        nxt_layer = 0 if layer_done is None else layer_done + 1
        nxt, wn = next_inputs(nxt_layer)
        mix_in = layer_done is not None
        nc = _prog(("tok", mix_in, nxt), lambda: build_token_program(mix_in, nxt))
        common = dict(wn)
        if mix_in:
            L = layer_done
            j = L // 2
            w_mo = f32(hyb_w_out[j]) if L % 2 == 0 else f32(mla_w_o[j])
            common.update(Wmo=blocked(w_mo), Wfi=blocked(f32(ffn_w_in[L])), Wfo=blocked(f32(ffn_w_out[L])), Wpp=blocked(f32(ple_w_proj[L])),
                          Wpg=blocked(f32(ple_w_gate[L])), g_ffn=chunked_vec(f32(ffn_norm[L])), g_pe=chunked_vec(f32(ple_norm[L])),
                          g_pg=chunked_vec(f32(ple_gate_norm[L])),
                          convw=np.ascontiguousarray(f32(ffn_conv_w[L]).reshape(3, 22, 128).transpose(2, 0, 1)), convb=chunked_vec(f32(ffn_conv_b[L])))
        in_maps = []
        for (b, i) in cores:
            m = dict(common)
            m["hT"] = _with_halo(hT[b], i).reshape(8, 128, HALO + TC)
            if mix_in:
                m["oT"] = _with_halo(oT[b], i).reshape(8, 128, HALO + TC)
                m["pT"] = _with_halo(np.ascontiguousarray(p[layer_done, b].T), i).reshape(2, 128, HALO + TC)
            if nxt == "odd":
                m["pos"] = np.ascontiguousarray(np.broadcast_to(positions[b, i * TC:(i + 1) * TC][None], (128, TC)))
            in_maps.append(m)
        res = _run(nc, in_maps)
        out = {}
        if mix_in:
            for b in range(B_):
                hT[b] = np.concatenate([res[b * 4 + i]["hT_out"].reshape(1024, TC) for i in range(4)], axis=1)
        for key in ("projT", "qT", "kvT", "krT", "yT"):
            if key in res[0]:
                out[key] = [np.concatenate([res[b * 4 + i][key].reshape(-1, TC) for i in range(4)], axis=1) for b in range(B_)]
        return out

    out = run_token(None, None)
    for L in range(DEPTH):
        j = L // 2
        oT = [np.zeros((1024, S_), np.float32) for _ in range(B_)]
        if L % 2 == 0:
            proj = out["projT"]
            nc_sb = _prog(("sb",), lambda: build_sb_program(S_, 2))
            in_maps = []
            for b in range(B_):
                for g in range(4):
                    sl = lambda base: np.ascontiguousarray(proj[b][base + g * 128: base + (g + 1) * 128].reshape(2, 64, S_))
                    in_maps.append(dict(q=sl(0), k=sl(512), v=sl(1024)))
            res = _run(nc_sb, in_maps)
            for b in range(B_):
                for g in range(4):
                    oT[b][g * 128:(g + 1) * 128] = res[b * 4 + g]["oT"]
            nc_rw = _prog(("rw",), lambda: build_rwkv_program(S_))
            prm = [f32(a[j]) for a in (rw_mu, rw_w0, rw_w2, rw_a0, rw_a2, rw_g2, rw_k_k, rw_k_a, rw_r_k, rw_ln_w, rw_ln_b)]
            in_maps = []
            for b in range(B_):
                rwT = proj[b][1536:3360]
                for g in range(4):
                    in_maps.append(rwkv_host_inputs(rwT, g, *prm))
            res = _run(nc_rw, in_maps)
            for b in range(B_):
                for g in range(4):
                    oT[b][512 + g * 128: 512 + (g + 1) * 128] = res[b * 4 + g]["oT"]
        else:
            qT, kvT, krT = out["qT"], out["kvT"], out["krT"]
            nc_mla = _prog(("mla",), lambda: build_mla_program(S_, 4))
            in_maps = []
            for b in range(B_):
                qn = qT[b][0:1024].reshape(16, 64, S_)
                x1 = qT[b][1024:1280].reshape(16, 16, S_)
                x2 = qT[b][1280:1536].reshape(16, 16, S_)
                kv = kvT[b].reshape(16, 128, S_)
                kr = krT[b]
                for g in range(4):
                    hs = slice(4 * g, 4 * g + 4)
                    q = np.concatenate([qn[hs], x1[hs], x2[hs]], axis=1)
                    k = np.concatenate([kv[hs, :64], np.broadcast_to(kr[None], (4, 32, S_))], axis=1)
                    in_maps.append(dict(q=np.ascontiguousarray(q), k=np.ascontiguousarray(k), v=np.ascontiguousarray(kv[hs, 64:])))
            res = _run(nc_mla, in_maps)
            for b in range(B_):
                for g in range(4):
                    oT[b][g * 256:(g + 1) * 256] = res[b * 4 + g]["oT"].reshape(256, S_)
        out = run_token(L, oT)
    y = np.stack([np.ascontiguousarray(out["yT"][b].T) for b in range(B_)]).astype(np.float32)
    return y
```

```python
import numpy as np
from contextlib import ExitStack
import concourse.bass as bass
import concourse.mybir as mybir
from concourse.bass_utils import run_bass_kernel_spmd

F32, BF16, I32 = mybir.dt.float32, mybir.dt.bfloat16, mybir.dt.int32
AF = mybir.ActivationFunctionType
ALU = mybir.AluOpType
SAME_ENG_SYNC = False


class Tk:
    __slots__ = ("h", "w", "r", "name")

    def __init__(self, h, name=""):
        self.h = h
        self.w = None
        self.r = {}
        self.name = name

    def __getitem__(self, idx):
        return self.h[idx]


class SubTk:
    __slots__ = ("h", "p", "name")

    def __init__(self, parent, ap, name=""):
        self.h = ap
        self.p = parent
        self.name = name

    def __getitem__(self, idx):
        return self.h[idx]

    @property
    def w(self):
        return self.p.w

    @w.setter
    def w(self, v):
        self.p.w = v

    @property
    def r(self):
        return self.p.r

    @r.setter
    def r(self, v):
        self.p.r = v


class Ctx:
    def __init__(self, nc, stack, n_dma_sems=48):
        self.nc = nc
        self.engs = {"pe": nc.tensor, "act": nc.scalar, "dve": nc.vector, "pool": nc.gpsimd, "sp": nc.sync}
        self.esem = {k: stack.enter_context(nc.semaphore("s_" + k)) for k in ("pe", "act", "dve", "pool")}
        self.ecnt = {k: 0 for k in self.esem}
        self.dsem = [stack.enter_context(nc.semaphore(f"d{i}")) for i in range(n_dma_sems)]
        self.dcnt = [0] * n_dma_sems
        self.dnext = 0
        self.known = {k: {} for k in self.engs}
        self.nid = 0

    def _sem(self, k):
        return self.esem[k[1]] if k[0] == "e" else self.dsem[k[1]]

    def _wait(self, eng, events):
        need = {}
        for ev in events:
            if ev is None:
                continue
            k, v = ev
            if k[0] == "e" and k[1] == eng and (eng == "pe" or not SAME_ENG_SYNC):
                continue
            if need.get(k, 0) < v:
                need[k] = v
        kn = self.known[eng]
        for k, v in need.items():
            if kn.get(k, 0) >= v:
                continue
            self.engs[eng].wait_ge(self._sem(k), v)
            kn[k] = v

    @staticmethod
    def _deps(r, w):
        ev = []
        for t in r:
            ev.append(t.w)
        for t in w:
            ev.append(t.w)
            ev.extend(t.r.items())
        return ev

    @staticmethod
    def _mark(ev, r, w):
        k, v = ev
        for t in r:
            if t.r.get(k, 0) < v:
                t.r[k] = v
        for t in w:
            t.w = ev
            t.r = {}

    def op(self, eng, fn, r=(), w=()):
        self._wait(eng, self._deps(r, w))
        ins = fn(self.engs[eng])
        self.ecnt[eng] += 1
        ins.then_inc(self.esem[eng], 1)
        ev = (("e", eng), self.ecnt[eng])
        self._mark(ev, r, w)
        return ev

    def dma(self, eng, out, in_, r=(), w=()):
        i = self.dnext
        self.dnext = (i + 1) % len(self.dsem)
        deps = self._deps(r, w)
        if self.dcnt[i] > 0:
            deps.append((("d", i), self.dcnt[i]))
        self._wait(eng, deps)
        ins = self.engs[eng].dma_start(out=out, in_=in_)
        self.dcnt[i] += 16
        ins.then_inc(self.dsem[i], 16)
        ev = (("d", i), self.dcnt[i])
        self._mark(ev, r, w)
        return ev

    def finish(self, eng="sp"):
        evs = [(("d", i), c) for i, c in enumerate(self.dcnt) if c > 0]
        evs += [(("e", k), c) for k, c in self.ecnt.items() if c > 0]
        self._wait(eng, evs)

    def sb(self, shape, dt, name=None):
        self.nid += 1
        name = f"{name or 't'}_{self.nid}"
        return Tk(self.nc.alloc_sbuf_tensor(name, list(shape), dt), name)

    def ps(self, shape, dt=F32, name=None):
        self.nid += 1
        name = f"{name or 'p'}_{self.nid}"
        return Tk(self.nc.alloc_psum_tensor(name, list(shape), dt), name)

    def dram(self, name, shape, dt, kind="Internal"):
        return Tk(self.nc.dram_tensor(name, list(shape), dt, kind=kind).ap(), name)


def perm_down(w):
    out = np.zeros((w.shape[0], 7 * 128), w.dtype)
    out[:, :640] = w[:, :640]
    out[:, 640:656] = w[:, 640:656]
    out[:, 768:784] = w[:, 656:672]
    return out
def perm_uq(w):
    w3 = w.reshape(w.shape[0], 16, 96)
    return np.concatenate([w3[:, :, :64].reshape(-1, 1024), w3[:, :, 64:80].reshape(-1, 256), w3[:, :, 80:96].reshape(-1, 256)], axis=1)
_f = (10000.0 ** (-np.arange(16, dtype=np.float32) / 16)).astype(np.float32)
INVF = np.tile(_f, 8).reshape(128, 1).astype(np.float32)


NORM_EPS = 1e-6
D_MODEL = 1024
FFN = 2816
NFC = FFN // 128


def blocked(w):
    K, N = w.shape
    nb = (N + 127) // 128
    if N % 128:
        w = np.concatenate([w, np.zeros((K, nb * 128 - N), w.dtype)], axis=1)
    kc = K // 128
    return np.ascontiguousarray(w.reshape(kc, 128, nb, 128).transpose(2, 1, 0, 3))


def chunked_vec(v):
    return np.ascontiguousarray(v.reshape(-1, 128).T)


class TokEnv:
    def __init__(self, ctx, T):
        self.c = ctx
        self.T = T
        c = ctx
        self.ones = c.sb([128, 128], BF16, "ones")
        c.op("dve", lambda e: e.memset(self.ones[:], 1.0), w=[self.ones])
        self.ps_ss = c.ps([128, 512], F32, "ps_ss")
        self.ps_mm = [c.ps([128, 512], F32, "ps_mm") for _ in range(4)]
        self.mm_i = 0
        self.wbuf = {}
        self.sq = [c.sb([128, 512], BF16, "sq") for _ in range(2)]
        self.sq_i = 0
        self.rs = c.sb([128, 512], F32, "rs")

    def next_ps(self):
        p = self.ps_mm[self.mm_i % 4]
        self.mm_i += 1
        return p

    def wtile(self, KC):
        if KC not in self.wbuf:
            self.wbuf[KC] = [[self.c.sb([128, KC, 128], BF16, f"wb{KC}") for _ in range(3)], 0]
        ent = self.wbuf[KC]
        t = ent[0][ent[1] % 3]
        ent[1] += 1
        return t

    def load_w(self, Wd, n, KC):
        wt = self.wtile(KC)
        self.c.dma("pool", wt[:], Wd[n], r=[Wd], w=[wt])
        return wt

    def rstd(self, xs, D, T):
        c = self.c
        KC = len(xs)
        for k, (xt, xap) in enumerate(xs):
            sq = self.sq[self.sq_i % 2]
            self.sq_i += 1
            c.op("act", lambda e, sq=sq, xap=xap: e.activation(out=sq[:, :T], in_=xap, func=AF.Square), r=[xt], w=[sq])
            c.op("pe", lambda e, sq=sq, k=k: e.matmul(self.ps_ss[:, :T], lhsT=self.ones[:], rhs=sq[:, :T], start=(k == 0), stop=(k == KC - 1)),
                 r=[sq, self.ones], w=[self.ps_ss])
        c.op("act", lambda e: e.activation(out=self.rs[:, :T], in_=self.ps_ss[:, :T], func=AF.Ln, bias=float(D * NORM_EPS), scale=1.0), r=[self.ps_ss], w=[self.rs])
        c.op("act", lambda e: e.activation(out=self.rs[:, :T], in_=self.rs[:, :T], func=AF.Exp, scale=-0.5), r=[self.rs], w=[self.rs])
        return self.rs

    def normed(self, xs, D, g, xn, T):
        c = self.c
        rs = self.rstd(xs, D, T)
        for k, (xt, xap) in enumerate(xs):
            c.op("dve", lambda e, k=k, xap=xap: e.scalar_tensor_tensor(out=xn[:, k, :T], in0=xap, scalar=g[:, k:k + 1], in1=rs[:, :T], op0=ALU.mult, op1=ALU.mult),
                 r=[xt, g, rs], w=[xn])

    def linear(self, Wd, KC, xn, T, nblocks, consume, xn_r=None):
        c = self.c
        for n in nblocks:
            wt = self.load_w(Wd, n, KC)
            ps = self.next_ps()
            for k in range(KC):
                c.op("pe", lambda e, k=k, wt=wt, ps=ps: e.matmul(ps[:, :T], lhsT=wt[:, k, :], rhs=xn[:, k, :T], start=(k == 0), stop=(k == KC - 1)),
                     r=[wt, xn], w=[ps])
            consume(n, ps)


def prep_gain(c, gd, KC, D, name):
    g = c.sb([128, KC], F32, name)
    c.dma("sp", g[:], gd[:, :], r=[gd], w=[g])
    c.op("dve", lambda e: e.tensor_scalar(out=g[:], in0=g[:], scalar1=float(np.sqrt(D)), scalar2=None, op0=ALU.mult), r=[g], w=[g])
    return g


def build_token_program(mix_in, nxt, NT=4, T=512, parts=("halo", "mo", "ffn", "ple")):
    nc = bass.Bass("TRN2", target_bir_lowering=False)
    st = ExitStack()
    c = Ctx(nc, st)
    HALO = 32
    TT = HALO + NT * T
    NR = NT * T
    EI, EO = "ExternalInput", "ExternalOutput"
    d = {}
    d["hT"] = c.dram("hT", [8, 128, TT], F32, EI)
    hTv = d["hT"].h.rearrange("k p t -> p k t")
    env = TokEnv(c, T)
    h = c.sb([128, 8, T], F32, "h")
    xn = c.sb([128, 8, T], BF16, "xn")
    stage = [c.sb([128, T], F32, "stage") for _ in range(3)]
    stage_i = [0]

    def next_stage():
        s = stage[stage_i[0] % 3]
        stage_i[0] += 1
        return s

    hch = lambda TT_: [(h, h[:, k, :TT_]) for k in range(8)]

    if mix_in:
        d["oT"] = c.dram("oT", [8, 128, TT], F32, EI)
        oTv = d["oT"].h.rearrange("k p t -> p k t")
        d["pT"] = c.dram("pT", [2, 128, TT], F32, EI)
        pTv = d["pT"].h.rearrange("k p t -> p k t")
        d["Wmo"] = c.dram("Wmo", [8, 128, 8, 128], F32, EI)
        d["Wfi"] = c.dram("Wfi", [44, 128, 8, 128], F32, EI)
        d["Wfo"] = c.dram("Wfo", [8, 128, 22, 128], F32, EI)
        d["Wpp"] = c.dram("Wpp", [8, 128, 2, 128], F32, EI)
        d["Wpg"] = c.dram("Wpg", [8, 128, 8, 128], F32, EI)
        d["g_ffn"] = c.dram("g_ffn", [128, 8], F32, EI)
        d["g_pe"] = c.dram("g_pe", [128, 8], F32, EI)
        d["g_pg"] = c.dram("g_pg", [128, 8], F32, EI)
        d["convw"] = c.dram("convw", [128, 3, NFC], F32, EI)
        d["convb"] = c.dram("convb", [128, NFC], F32, EI)
        d["hT_out"] = c.dram("hT_out", [8, 128, NR], F32, EO)
        hOv = d["hT_out"].h.rearrange("k p t -> p k t")
        g_ffn = prep_gain(c, d["g_ffn"], 8, 1024, "g_ffn")
        g_pe = prep_gain(c, d["g_pe"], 8, 1024, "g_pe")
        g_pg = prep_gain(c, d["g_pg"], 8, 1024, "g_pg")
        convw = c.sb([128, 3, NFC], F32, "convw")
        convb = c.sb([128, NFC], F32, "convb")
        c.dma("sp", convw[:], d["convw"][:, :, :], r=[d["convw"]], w=[convw])
        c.dma("sp", convb[:], d["convb"][:, :], r=[d["convb"]], w=[convb])
        o_bf = c.sb([128, 8, T], BF16, "o_bf")
        p_bf = c.sb([128, 2, T], BF16, "p_bf")
        act = c.sb([128, NFC, T], BF16, "act")
        e_sb = c.sb([128, 8, T], F32, "e_sb")
        rs_e = c.sb([128, T], F32, "rs_e")
        carry = c.sb([128, NFC, 2], F32, "carry")
        gate_sb = [c.sb([128, T + 2], F32, "gate_sb") for _ in range(2)]
        ctmp = [c.sb([128, T], F32, "ctmp") for _ in range(2)]
        ge = [c.sb([128, T], F32, "ge") for _ in range(2)]
        sg = [c.sb([128, T], F32, "sg") for _ in range(2)]
        t1 = [c.sb([128, T], F32, "t1") for _ in range(2)]

    if nxt == "even":
        d["Whyb"] = c.dram("Whyb", [27, 128, 8, 128], F32, EI)
        d["g_attn"] = c.dram("g_attn", [128, 8], F32, EI)
        d["projT"] = c.dram("projT", [27, 128, NR], F32, EO)
        g_attn = prep_gain(c, d["g_attn"], 8, 1024, "g_attn")
    elif nxt == "odd":
        d["Wdn"] = c.dram("Wdn", [7, 128, 8, 128], F32, EI)
        d["Wuq"] = c.dram("Wuq", [12, 128, 3, 128], F32, EI)
        d["Wukv"] = c.dram("Wukv", [16, 128, 2, 128], F32, EI)
        d["g_attn"] = c.dram("g_attn", [128, 8], F32, EI)
        d["g_q"] = c.dram("g_q", [128, 3], F32, EI)
        d["g_kv"] = c.dram("g_kv", [128, 2], F32, EI)
        d["pos"] = c.dram("pos", [128, NR], I32, EI)
        d["invf"] = c.dram("invf", [128, 1], F32, EI)
        d["qT"] = c.dram("qT", [12, 128, NR], F32, EO)
        d["kvT"] = c.dram("kvT", [16, 128, NR], F32, EO)
        d["krT"] = c.dram("krT", [2, 16, NR], F32, EO)
        g_attn = prep_gain(c, d["g_attn"], 8, 1024, "g_attn")
        g_q = prep_gain(c, d["g_q"], 3, 384, "g_q")
        g_kv = prep_gain(c, d["g_kv"], 2, 256, "g_kv")
        invf = c.sb([128, 1], F32, "invf")
        c.dma("sp", invf[:], d["invf"][:, :], r=[d["invf"]], w=[invf])
        c_sb = c.sb([128, 7, T], F32, "c_sb")
        cn = c.sb([128, 3, T], BF16, "cn")
        pos_i = c.sb([128, T], I32, "pos_i")
        ang = c.sb([128, T], F32, "ang")
        tmpr = c.sb([128, T], F32, "tmpr")
        cosT = c.sb([128, T], F32, "cosT")
        sinT = c.sb([128, T], F32, "sinT")
        qr = c.sb([128, 4, T], F32, "qr")
        negpi = c.sb([128, 1], F32, "negpi")
        c.op("dve", lambda e: e.memset(negpi[:], -float(np.pi)), w=[negpi])
    elif nxt == "final":
        d["g_fin"] = c.dram("g_fin", [128, 8], F32, EI)
        d["yT"] = c.dram("yT", [8, 128, NR], F32, EO)
        g_fin = prep_gain(c, d["g_fin"], 8, 1024, "g_fin")

    def add_into_h(TT_):
        def f(n, ps):
            c.op("dve", lambda e: e.tensor_tensor(out=h[:, n, :TT_], in0=h[:, n, :TT_], in1=ps[:, :TT_], op=ALU.add), r=[ps, h], w=[h])
        return f

    def mixer_out(col0, TT_):
        c.dma("pool", o_bf[:, :, :TT_], oTv[:, :, col0:col0 + TT_], r=[d["oT"]], w=[o_bf])
        env.linear(d["Wmo"], 8, o_bf, TT_, range(8), add_into_h(TT_))

    def ffn_in(TT_, halo):
        env.normed(hch(TT_), 1024, g_ffn, xn, TT_)
        for fc in range(NFC):
            wg = env.load_w(d["Wfi"], NFC + fc, 8)
            ps_g = env.next_ps()
            for k in range(8):
                c.op("pe", lambda e: e.matmul(ps_g[:, :TT_], lhsT=wg[:, k, :], rhs=xn[:, k, :TT_], start=(k == 0), stop=(k == 7)), r=[wg, xn], w=[ps_g])
            if halo:
                c.op("act", lambda e: e.activation(out=carry[:, fc, :], in_=ps_g[:, TT_ - 2:TT_], func=AF.Copy), r=[ps_g], w=[carry])
                continue
            wu = env.load_w(d["Wfi"], fc, 8)
            ps_u = env.next_ps()
            for k in range(8):
                c.op("pe", lambda e: e.matmul(ps_u[:, :TT_], lhsT=wu[:, k, :], rhs=xn[:, k, :TT_], start=(k == 0), stop=(k == 7)), r=[wu, xn], w=[ps_u])
            gs = gate_sb[fc % 2]
            ct = ctmp[fc % 2]
            gg = ge[fc % 2]
            c.op("act", lambda e: e.activation(out=gs[:, 0:2], in_=carry[:, fc, :], func=AF.Copy), r=[carry], w=[gs])
            c.op("act", lambda e: e.activation(out=gs[:, 2:2 + TT_], in_=ps_g[:, :TT_], func=AF.Copy), r=[ps_g], w=[gs])
            c.op("act", lambda e: e.activation(out=carry[:, fc, :], in_=gs[:, TT_:TT_ + 2], func=AF.Copy), r=[gs], w=[carry])
            c.op("dve", lambda e: e.tensor_scalar(out=ct[:, :TT_], in0=gs[:, 2:2 + TT_], scalar1=convw[:, 2, fc:fc + 1], scalar2=convb[:, fc:fc + 1], op0=ALU.mult, op1=ALU.add),
                 r=[gs, convw, convb], w=[ct])
            c.op("dve", lambda e: e.scalar_tensor_tensor(out=ct[:, :TT_], in0=gs[:, 1:1 + TT_], scalar=convw[:, 1, fc:fc + 1], in1=ct[:, :TT_], op0=ALU.mult, op1=ALU.add),
                 r=[gs, convw, ct], w=[ct])
            c.op("dve", lambda e: e.scalar_tensor_tensor(out=ct[:, :TT_], in0=gs[:, 0:TT_], scalar=convw[:, 0, fc:fc + 1], in1=ct[:, :TT_], op0=ALU.mult, op1=ALU.add),
                 r=[gs, convw, ct], w=[ct])
            c.op("act", lambda e: e.activation(out=gg[:, :TT_], in_=ct[:, :TT_], func=AF.Gelu), r=[ct], w=[gg])
            c.op("dve", lambda e: e.tensor_tensor(out=act[:, fc, :TT_], in0=gg[:, :TT_], in1=ps_u[:, :TT_], op=ALU.mult), r=[gg, ps_u], w=[act])

    def ple(col0, TT_):
        c.dma("pool", p_bf[:, :, :TT_], pTv[:, :, col0:col0 + TT_], r=[d["pT"]], w=[p_bf])

        def ev_e(n, ps):
            c.op("act", lambda e: e.activation(out=e_sb[:, n, :TT_], in_=ps[:, :TT_], func=AF.Copy), r=[ps], w=[e_sb])
        env.linear(d["Wpp"], 2, p_bf, TT_, range(8), ev_e)
        rs = env.rstd([(e_sb, e_sb[:, k, :TT_]) for k in range(8)], 1024, TT_)
        c.op("act", lambda e: e.activation(out=rs_e[:, :TT_], in_=rs[:, :TT_], func=AF.Copy), r=[rs], w=[rs_e])
        env.normed(hch(TT_), 1024, g_pg, xn, TT_)

        def ev_g(n, ps):
            s_ = sg[n % 2]
            t_ = t1[n % 2]
            c.op("act", lambda e: e.activation(out=s_[:, :TT_], in_=ps[:, :TT_], func=AF.Sigmoid), r=[ps], w=[s_])
            c.op("dve", lambda e: e.scalar_tensor_tensor(out=t_[:, :TT_], in0=e_sb[:, n, :TT_], scalar=g_pe[:, n:n + 1], in1=rs_e[:, :TT_], op0=ALU.mult, op1=ALU.mult),
                 r=[e_sb, g_pe, rs_e], w=[t_])
            c.op("dve", lambda e: e.tensor_tensor(out=t_[:, :TT_], in0=t_[:, :TT_], in1=s_[:, :TT_], op=ALU.mult), r=[t_, s_], w=[t_])
            c.op("dve", lambda e: e.tensor_tensor(out=h[:, n, :TT_], in0=h[:, n, :TT_], in1=t_[:, :TT_], op=ALU.add), r=[t_, h], w=[h])
        env.linear(d["Wpg"], 8, xn, TT_, range(8), ev_g)

    def store_rows(dst_tk, dst_ap, n_part=128):
        def f(n, ps):
            s = next_stage()
            c.op("act", lambda e: e.activation(out=s[:n_part, :T], in_=ps[:n_part, :T], func=AF.Copy), r=[ps], w=[s])
            c.dma("sp", dst_ap(n), s[:n_part, :T], r=[s], w=[dst_tk])
        return f

    def rope_tables(r0):
        c.dma("sp", pos_i[:], d["pos"][:, r0:r0 + T], r=[d["pos"]], w=[pos_i])
        c.op("dve", lambda e: e.tensor_copy(out=ang[:], in_=pos_i[:]), r=[pos_i], w=[ang])
        c.op("dve", lambda e: e.tensor_scalar(out=ang[:], in0=ang[:], scalar1=invf[:, 0:1], scalar2=None, op0=ALU.mult), r=[ang, invf], w=[ang])
        for (off, dst) in ((0.0, sinT), (float(np.pi / 2), cosT)):
            if off:
                c.op("dve", lambda e: e.tensor_scalar(out=ang[:], in0=ang[:], scalar1=off, scalar2=None, op0=ALU.add), r=[ang], w=[ang])
            c.op("dve", lambda e: e.tensor_scalar(out=pos_i[:], in0=ang[:], scalar1=float(1.0 / (2 * np.pi)), scalar2=None, op0=ALU.mult), r=[ang], w=[pos_i])
            c.op("dve", lambda e: e.tensor_copy(out=tmpr[:], in_=pos_i[:]), r=[pos_i], w=[tmpr])
            c.op("dve", lambda e: e.scalar_tensor_tensor(out=dst[:], in0=tmpr[:], scalar=-6.28125, in1=ang[:], op0=ALU.mult, op1=ALU.add), r=[tmpr, ang], w=[dst])
            c.op("dve", lambda e: e.scalar_tensor_tensor(out=dst[:], in0=tmpr[:], scalar=-0.0019353071795864769, in1=dst[:], op0=ALU.mult, op1=ALU.add), r=[tmpr, dst], w=[dst])
            c.op("act", lambda e: e.activation(out=dst[:], in_=dst[:], func=AF.Sin), r=[dst], w=[dst])

    def rope_apply(x1_tk, x1, x2, P, out1_ap, out2_ap, dst_tk):
        s1 = next_stage()
        s2 = next_stage()
        tm = next_stage()
        c.op("dve", lambda e: e.tensor_tensor(out=s1[:P, :], in0=x1, in1=cosT[:P, :], op=ALU.mult), r=[x1_tk, cosT], w=[s1])
        c.op("dve", lambda e: e.tensor_tensor(out=tm[:P, :], in0=x2, in1=sinT[:P, :], op=ALU.mult), r=[x1_tk, sinT], w=[tm])
        c.op("dve", lambda e: e.tensor_tensor(out=s1[:P, :], in0=s1[:P, :], in1=tm[:P, :], op=ALU.subtract), r=[s1, tm], w=[s1])
        c.op("dve", lambda e: e.tensor_tensor(out=s2[:P, :], in0=x1, in1=sinT[:P, :], op=ALU.mult), r=[x1_tk, sinT], w=[s2])
        c.op("dve", lambda e: e.tensor_tensor(out=tm[:P, :], in0=x2, in1=cosT[:P, :], op=ALU.mult), r=[x1_tk, cosT, s1], w=[tm])
        c.op("dve", lambda e: e.tensor_tensor(out=s2[:P, :], in0=s2[:P, :], in1=tm[:P, :], op=ALU.add), r=[s2, tm], w=[s2])
        c.dma("sp", out1_ap, s1[:P, :], r=[s1], w=[dst_tk])
        c.dma("sp", out2_ap, s2[:P, :], r=[s2], w=[dst_tk])

    if mix_in:
        if "halo" in parts:
            c.dma("sp", h[:, :, :HALO], hTv[:, :, 0:HALO], r=[d["hT"]], w=[h])
            mixer_out(0, HALO)
            ffn_in(HALO, True)
        else:
            c.op("dve", lambda e: e.memset(carry[:], 0.0), w=[carry])
    for it in range(NT):
        col0 = HALO + it * T
        r0 = it * T
        c.dma("sp", h[:, :, :T], hTv[:, :, col0:col0 + T], r=[d["hT"]], w=[h])
        if mix_in:
            if "mo" in parts:
                mixer_out(col0, T)
            if "ffn" in parts:
                ffn_in(T, False)
                env.linear(d["Wfo"], NFC, act, T, range(8), add_into_h(T))
            if "ple" in parts:
                ple(col0, T)
            c.dma("sp", hOv[:, :, r0:r0 + T], h[:, :, :T], r=[h], w=[d["hT_out"]])
        if nxt == "even":
            env.normed(hch(T), 1024, g_attn, xn, T)
            env.linear(d["Whyb"], 8, xn, T, range(27), store_rows(d["projT"], lambda n: d["projT"][n, :, r0:r0 + T]))
        elif nxt == "odd":
            env.normed(hch(T), 1024, g_attn, xn, T)

            def ev_c(n, ps):
                c.op("act", lambda e: e.activation(out=c_sb[:, n, :], in_=ps[:, :T], func=AF.Copy), r=[ps], w=[c_sb])
            env.linear(d["Wdn"], 8, xn, T, range(7), ev_c)
            rope_tables(r0)
            rope_apply(c_sb, c_sb[0:16, 5, :], c_sb[0:16, 6, :], 16, d["krT"][0, :, r0:r0 + T], d["krT"][1, :, r0:r0 + T], d["krT"])
            env.normed([(c_sb, c_sb[:, k, :]) for k in range(3)], 384, g_q, cn, T)

            def ev_q(n, ps):
                if n < 8:
                    store_rows(d["qT"], lambda n_: d["qT"][n_, :, r0:r0 + T])(n, ps)
                else:
                    c.op("act", lambda e: e.activation(out=qr[:, n - 8, :], in_=ps[:, :T], func=AF.Copy), r=[ps], w=[qr])
            env.linear(d["Wuq"], 3, cn, T, range(12), ev_q)
            for j in range(2):
                rope_apply(qr, qr[:, j, :], qr[:, 2 + j, :], 128, d["qT"][8 + j, :, r0:r0 + T], d["qT"][10 + j, :, r0:r0 + T], d["qT"])
            env.normed([(c_sb, c_sb[:, 3 + k, :]) for k in range(2)], 256, g_kv, cn, T)
            env.linear(d["Wukv"], 2, cn, T, range(16), store_rows(d["kvT"], lambda n: d["kvT"][n, :, r0:r0 + T]))
        elif nxt == "final":
            rs = env.rstd(hch(T), 1024, T)
            for k in range(8):
                s = next_stage()
                c.op("dve", lambda e: e.scalar_tensor_tensor(out=s[:, :T], in0=h[:, k, :T], scalar=g_fin[:, k:k + 1], in1=rs[:, :T], op0=ALU.mult, op1=ALU.mult),
                     r=[h, g_fin, rs], w=[s])
                c.dma("sp", d["yT"][k, :, r0:r0 + T], s[:, :T], r=[s], w=[d["yT"]])
    c.finish()
    return nc, st


def make_masks(c, strict):
    masks = []
    for dl in range(4):
        mf = c.sb([128, 512], F32, "maskf")
        c.op("pool", lambda e: e.memset(mf[:], 1.0), w=[mf])
        c.op("pool", lambda e: e.affine_select(out=mf[:], in_=mf[:], pattern=[[1, 512]], compare_op=(ALU.is_gt if strict else ALU.is_ge),
                                                fill=0.0, base=-128 * dl, channel_multiplier=-1), r=[mf], w=[mf])
        mb = c.sb([128, 512], BF16, "maskb")
        c.op("pool", lambda e: e.tensor_copy(out=mb[:], in_=mf[:]), r=[mf], w=[mb])
        masks.append(mb)
    return masks


def make_ident(c, dt=BF16, n=128):
    f = c.sb([128, n], F32, "identf")
    c.op("pool", lambda e: e.memset(f[:], 1.0), w=[f])
    c.op("pool", lambda e: e.affine_select(out=f[:], in_=f[:], pattern=[[-1, n]], compare_op=ALU.is_equal, fill=0.0, base=0, channel_multiplier=1), r=[f], w=[f])
    if dt == F32:
        return f
    b = c.sb([128, n], dt, "identb")
    c.op("pool", lambda e: e.tensor_copy(out=b[:], in_=f[:]), r=[f], w=[b])
    return b


def build_mla_program(S=8192, NH=4):
    nc = bass.Bass("TRN2", target_bir_lowering=False)
    st = ExitStack()
    c = Ctx(nc, st)
    EI, EO = "ExternalInput", "ExternalOutput"
    qd = c.dram("q", [NH, 96, S], F32, EI)
    kd = c.dram("k", [NH, 96, S], F32, EI)
    vd = c.dram("v", [NH, 64, S], F32, EI)
    od = c.dram("oT", [NH, 64, S], F32, EO)
    NB = S // 128
    NQ = S // 512
    scale = 1.0 / float(np.sqrt(96.0))
    masks = make_masks(c, strict=False)
    ident = make_ident(c, BF16)
    Q = [c.sb([96, S], BF16, "Q") for _ in range(2)]
    K = [c.sb([96, S], BF16, "K") for _ in range(2)]
    VT = [c.sb([64, S], BF16, "VT") for _ in range(2)]
    Va = [c.sb([128, NB, 128], BF16, "Va") for _ in range(2)]
    for va in Va:
        c.op("pool", lambda e: e.memset(va[:], 1.0), w=[va])
    ps_s = [c.ps([128, 512], F32, "ps_s") for _ in range(4)]
    ps_o = [c.ps([128, 512], F32, "ps_o") for _ in range(2)]
    ps_t = c.ps([128, 64], F32, "ps_t")
    P = [c.sb([128, 512], BF16, "P") for _ in range(6)]
    rec = [c.sb([64, 512], F32, "rec") for _ in range(2)]
    ob = [c.sb([64, 512], F32, "ob") for _ in range(2)]
    it = 0
    for h in range(NH):
        q, k, vt, va = Q[h % 2], K[h % 2], VT[h % 2], Va[h % 2]
        c.dma("pool", q[:], qd[h], r=[qd], w=[q])
        c.dma("pool", k[:], kd[h], r=[kd], w=[k])
        c.dma("pool", vt[:], vd[h], r=[vd], w=[vt])
        for jb in range(NB):
            c.op("pe", lambda e: e.matmul(ps_t[:, :], lhsT=vt[:, jb * 128:(jb + 1) * 128], rhs=ident[0:64, 0:64], start=True, stop=True), r=[vt, ident], w=[ps_t])
            c.op("dve", lambda e: e.tensor_copy(out=va[:, jb, 0:64], in_=ps_t[:, :]), r=[ps_t], w=[va])
        blocks = []
        for qi in range(NQ):
            nkb = 4 * qi + 4
            for jb in range(nkb):
                blocks.append(dict(qi=qi, jb=jb, nkb=nkb, pss=ps_s[it % 4], p=P[it % 6]))
                it += 1

        def s1(b_):
            q0 = b_["qi"] * 512
            jb, pss, p = b_["jb"], b_["pss"], b_["p"]
            c.op("pe", lambda e: e.matmul(pss[:, :], lhsT=k[:, jb * 128:(jb + 1) * 128], rhs=q[:, q0:q0 + 512], start=True, stop=True), r=[k, q], w=[pss])
            c.op("act", lambda e: e.activation(out=p[:], in_=pss[:], func=AF.Exp, scale=scale), r=[pss], w=[p])
            dl = jb - 4 * b_["qi"]
            if dl >= 0:
                c.op("pool", lambda e: e.tensor_tensor(out=p[:], in0=p[:], in1=masks[dl][:], op=ALU.mult), r=[p, masks[dl]], w=[p])

        def s2(b_):
            qi, jb, nkb, p = b_["qi"], b_["jb"], b_["nkb"], b_["p"]
            q0 = qi * 512
            po = ps_o[qi % 2]
            c.op("pe", lambda e: e.matmul(po[:, :], lhsT=va[:, jb, :], rhs=p[:], start=(jb == 0), stop=(jb == nkb - 1)), r=[va, p], w=[po])
            if jb == nkb - 1:
                r_ = rec[qi % 2]
                o_ = ob[qi % 2]
                c.op("dve", lambda e: e.reciprocal(out=r_[:], in_=po[64:128, :]), r=[po], w=[r_])
                c.op("dve", lambda e: e.tensor_tensor(out=o_[:], in0=po[0:64, :], in1=r_[:], op=ALU.mult), r=[po, r_], w=[o_])
                c.dma("sp", od[h, :, q0:q0 + 512], o_[:], r=[o_], w=[od])
        SK = 2
        for t in range(len(blocks) + SK):
            if t < len(blocks):
                s1(blocks[t])
            if t - SK >= 0:
                s2(blocks[t - SK])
    c.finish()
    return nc, st


def make_tri(c, kind):
    f = c.sb([128, 128], F32, "trif")
    c.op("pool", lambda e: e.memset(f[:], -1.0), w=[f])
    if kind == "U":
        c.op("pool", lambda e: e.affine_select(out=f[:], in_=f[:], pattern=[[-1, 128]], compare_op=ALU.is_gt, fill=0.0, base=0, channel_multiplier=1), r=[f], w=[f])
    else:
        c.op("pool", lambda e: e.affine_select(out=f[:], in_=f[:], pattern=[[1, 128]], compare_op=ALU.is_ge, fill=0.0, base=0, channel_multiplier=-1), r=[f], w=[f])
    b = c.sb([128, 128], BF16, "trib")
    c.op("pool", lambda e: e.tensor_copy(out=b[:], in_=f[:]), r=[f], w=[b])
    return b


def emit_sb(c, qd, kd, vd, od, S, NH, ident, row0=0):
    NB = S // 128
    NQ = S // 512
    scale = 0.125
    masks = make_masks(c, strict=True)
    negU = make_tri(c, "U")
    negL = make_tri(c, "L")
    Q = [c.sb([64, S], BF16, "sQ") for _ in range(NH)]
    K = [c.sb([64, S], BF16, "sK") for _ in range(NH)]
    VT = [c.sb([64, S], BF16, "sVT") for _ in range(NH)]
    V = [c.sb([128, NB, 64], BF16, "sV") for _ in range(NH)]
    ps_z = [c.ps([128, 512], F32, "ps_z") for _ in range(3)]
    ps_a = [c.ps([128, 512], F32, "ps_a") for _ in range(3)]
    ps_o = [c.ps([64, 512], F32, "sps_o") for _ in range(NH)]
    ps_t = SubTk(ps_a[0], ps_a[0].h[:, 0:64], "sps_t")
    ND = 4
    ex = [c.sb([128, 512], F32, "ex") for _ in range(ND)]
    sp = [c.sb([128, 512], F32, "sp") for _ in range(ND)]
    spb = [c.sb([128, 512], BF16, "spb") for _ in range(2 * ND)]
    lsg = [c.sb([128, 512], F32, "lsg") for _ in range(ND)]
    A = [c.sb([128, 512], F32, "A") for _ in range(2 * ND)]
    wb = [c.sb([128, 512], BF16, "wb") for _ in range(ND)]
    ob = [c.sb([64, 512], F32, "sob") for _ in range(2)]
    for h in range(NH):
        c.dma("pool", Q[h][:], qd[h], r=[qd], w=[Q[h]])
        c.dma("pool", K[h][:], kd[h], r=[kd], w=[K[h]])
        c.dma("pool", VT[h][:], vd[h], r=[vd], w=[VT[h]])
    for h in range(NH):
        for jb in range(NB):
            c.op("pe", lambda e: e.matmul(ps_t[:, :], lhsT=VT[h][:, jb * 128:(jb + 1) * 128], rhs=ident[0:64, 0:64], start=True, stop=True), r=[VT[h], ident], w=[ps_t])
            c.op("dve", lambda e: e.tensor_copy(out=V[h][:, jb, :], in_=ps_t[:, :]), r=[ps_t], w=[V[h]])
    it = 0
    blocks = []
    for qi in range(NQ):
        nkb = 4 * qi + 4
        prev = [None] * NH
        for jb in range(nkb - 1, -1, -1):
            for h in range(NH):
                b_ = dict(qi=qi, jb=jb, h=h, nkb=nkb, pz=ps_z[it % 3], pa=ps_a[it % 3], e=ex[it % ND], s=sp[it % ND], l=lsg[it % ND], w=wb[it % ND],
                          sb=spb[it % (2 * ND)], a=A[it % (2 * ND)], prev=prev[h])
                it += 1
                prev[h] = b_
                blocks.append(b_)
    oi = [0]

    def s1(b_):
        qi, jb, h = b_["qi"], b_["jb"], b_["h"]
        q0 = qi * 512
        pz, e_, s_, sb_, l_ = b_["pz"], b_["e"], b_["s"], b_["sb"], b_["l"]
        dl = jb - 4 * qi
        c.op("pe", lambda e: e.matmul(pz[:, :], lhsT=K[h][:, jb * 128:(jb + 1) * 128], rhs=Q[h][:, q0:q0 + 512], start=True, stop=True), r=[K[h], Q[h]], w=[pz])
        c.op("act", lambda e: e.activation(out=e_[:], in_=pz[:], func=AF.Exp, scale=scale), r=[pz], w=[e_])
        c.op("act", lambda e: e.activation(out=s_[:], in_=e_[:], func=AF.Ln, bias=1.0, scale=1.0), r=[e_], w=[s_])
        if dl >= 0:
            c.op("pool", lambda e: e.tensor_tensor(out=sb_[:], in0=s_[:], in1=masks[dl][:], op=ALU.mult), r=[s_, masks[dl]], w=[sb_])
        else:
            c.op("pool", lambda e: e.tensor_copy(out=sb_[:], in_=s_[:]), r=[s_], w=[sb_])
        c.op("dve", lambda e: e.scalar_tensor_tensor(out=l_[:], in0=pz[:], scalar=scale, in1=s_[:], op0=ALU.mult, op1=ALU.subtract), r=[pz, s_], w=[l_])

    def s2(b_):
        qi, jb = b_["qi"], b_["jb"]
        pa, sb_, l_, w_, a_new, pv = b_["pa"], b_["sb"], b_["l"], b_["w"], b_["a"], b_["prev"]
        dl = jb - 4 * qi
        first = pv is None
        c.op("pe", lambda e: e.matmul(pa[:, :], lhsT=negU[:], rhs=sb_[:], start=True, stop=first), r=[negU, sb_], w=[pa])
        if not first:
            c.op("pe", lambda e: e.matmul(pa[:, :], lhsT=negL[:], rhs=pv["sb"][:], start=False, stop=True), r=[negL, pv["sb"]], w=[pa])
            c.op("dve", lambda e: e.tensor_tensor(out=a_new[:], in0=pa[:], in1=pv["a"][:], op=ALU.add), r=[pa, pv["a"]], w=[a_new])
        else:
            c.op("dve", lambda e: e.tensor_copy(out=a_new[:], in_=pa[:]), r=[pa], w=[a_new])
        c.op("dve", lambda e: e.tensor_tensor(out=l_[:], in0=l_[:], in1=a_new[:], op=ALU.add), r=[l_, a_new], w=[l_])
        c.op("act", lambda e: e.activation(out=w_[:], in_=l_[:], func=AF.Exp), r=[l_], w=[w_])
        if dl >= 0:
            c.op("pool", lambda e: e.tensor_tensor(out=w_[:], in0=w_[:], in1=masks[dl][:], op=ALU.mult), r=[w_, masks[dl]], w=[w_])

    def s3(b_):
        qi, jb, h, w_ = b_["qi"], b_["jb"], b_["h"], b_["w"]
        q0 = qi * 512
        po = ps_o[h]
        c.op("pe", lambda e: e.matmul(po[:, :], lhsT=V[h][:, jb, :], rhs=w_[:], start=(b_["prev"] is None), stop=(jb == 0)), r=[V[h], w_], w=[po])
        if jb == 0:
            o_ = ob[oi[0] % 2]
            oi[0] += 1
            c.op("act", lambda e: e.activation(out=o_[:], in_=po[:, :], func=AF.Copy), r=[po], w=[o_])
            c.dma("sp", od[row0 + h * 64:row0 + (h + 1) * 64, q0:q0 + 512], o_[:], r=[o_], w=[od])
    n = len(blocks)
    K2, K3 = 1, 2
    for t in range(n + K3):
        if t < n:
            s1(blocks[t])
        if 0 <= t - K2 < n:
            s2(blocks[t - K2])
        if 0 <= t - K3 < n:
            s3(blocks[t - K3])


def build_sb_program(S=8192, NH=2):
    nc = bass.Bass("TRN2", target_bir_lowering=False)
    st = ExitStack()
    c = Ctx(nc, st)
    EI, EO = "ExternalInput", "ExternalOutput"
    qd = c.dram("q", [NH, 64, S], F32, EI)
    kd = c.dram("k", [NH, 64, S], F32, EI)
    vd = c.dram("v", [NH, 64, S], F32, EI)
    od = c.dram("oT", [NH * 64, S], F32, EO)
    ident = make_ident(c, BF16)
    emit_sb(c, qd, kd, vd, od, S, NH, ident)
    c.finish()
    return nc, st


import os
def emit_rwkv(c, d, od, S, identf, row0=0):
    STOP = float(os.environ.get("RW_STOP", "9"))
    TS = 512
    NTL = S // TS
    NCH = TS // 64
    GN_EPS = 64e-5
    vec = c.sb([128, 8], F32, "rvec")
    c.dma("sp", vec[:], d["vecs"][:, :], r=[d["vecs"]], w=[vec])
    W0, A0, KK_, KA, RK, LNW, LNB = (vec[:, i:i + 1] for i in (0, 1, 2, 3, 5, 6, 7))
    omk = c.sb([128, 1], F32, "omk")
    c.op("dve", lambda e: e.tensor_scalar(out=omk[:], in0=vec[:, 3:4], scalar1=-1.0, scalar2=1.0, op0=ALU.mult, op1=ALU.add), r=[vec], w=[omk])
    mu3 = c.sb([128, 3], F32, "mu3")
    mu2 = c.sb([64, 2], F32, "mu2")
    mug = c.sb([128, 2], F32, "mug")
    c.dma("sp", mu3[:], d["mu_rkv"][:, :], r=[d["mu_rkv"]], w=[mu3])
    c.dma("sp", mu2[:], d["mu_wa"][:, :], r=[d["mu_wa"]], w=[mu2])
    c.dma("sp", mug[:], d["mu_gl"][:, :], r=[d["mu_gl"]], w=[mug])
    w2b = c.sb([64, 128], BF16, "w2b")
    a2b = c.sb([64, 128], BF16, "a2b")
    g2b = c.sb([128, 2, 128], BF16, "g2b")
    c.dma("pool", w2b[:], d["w2"][:, :], r=[d["w2"]], w=[w2b])
    c.dma("pool", a2b[:], d["a2"][:, :], r=[d["a2"]], w=[a2b])
    c.dma("pool", g2b[:], d["g2"].h.rearrange("k p n -> p k n"), r=[d["g2"]], w=[g2b])
    bones = c.sb([128, 128], F32, "bones")
    bavg = c.sb([128, 128], F32, "bavg")
    for (t_, val) in ((bones, 1.0), (bavg, 1.0 / 64)):
        c.op("pool", lambda e: e.memset(t_[:], 0.0), w=[t_])
        c.op("pool", lambda e: e.memset(t_[0:64, 0:64], val), w=[t_])
        c.op("pool", lambda e: e.memset(t_[64:128, 64:128], val), w=[t_])
    maskT = c.sb([64, 128], F32, "maskT")
    c.op("pool", lambda e: e.memset(maskT[:], 1.0), w=[maskT])
    c.op("pool", lambda e: e.affine_select(out=maskT[:, 0:64], in_=maskT[:, 0:64], pattern=[[1, 64]], compare_op=ALU.is_gt, fill=0.0, base=0, channel_multiplier=-1), r=[maskT], w=[maskT])
    c.op("pool", lambda e: e.affine_select(out=maskT[:, 64:128], in_=maskT[:, 64:128], pattern=[[1, 64]], compare_op=ALU.is_ge, fill=0.0, base=0, channel_multiplier=-1), r=[maskT], w=[maskT])
    maskSL = c.sb([64, 64], F32, "maskSL")
    c.op("pool", lambda e: e.memset(maskSL[:], 1.0), w=[maskSL])
    c.op("pool", lambda e: e.affine_select(out=maskSL[:], in_=maskSL[:], pattern=[[-1, 64]], compare_op=ALU.is_gt, fill=0.0, base=0, channel_multiplier=1), r=[maskSL], w=[maskSL])
    if STOP <= 1:
        return
    def T2(name, n=2, shape=(128, TS), dt=F32):
        return [c.sb(list(shape), dt, name) for _ in range(n)]
    X3 = T2("X3", 2, (128, 3, TS + 1))
    WA = T2("WA", 2, (64, 2, TS + 1))
    GL = T2("GL", 2, (128, 2, TS + 1))
    dtmp = T2("dtmp", 2)
    rm, km, vm = T2("rm"), T2("km"), T2("vm")
    wlm = T2("wlm", 1, (64, TS))[0]
    th = T2("th", 1, (64, TS), BF16)[0]
    alm = T2("alm", 1, (64, TS), BF16)[0]
    glm = T2("glm", 1, (128, 2, TS))[0]
    sgl = T2("sgl", 1, (128, 2, TS), BF16)[0]
    logw, av_, gg = T2("logw", 1)[0], T2("a", 1)[0], T2("gg")
    kk, kk2, rn, kkn, bv, tmpk = (T2(n_, 1)[0] for n_ in ("kk", "kk2", "rn", "kkn", "bv", "tmpk"))
    k2s = T2("k2")
    cA, cB = T2("cA", 1)[0], T2("cB", 1)[0]
    Winc, Wprev, Einv = T2("Winc"), T2("Wprev", 1)[0], T2("Einv", 1)[0]
    AR = T2("AR", 2, (128, NCH, 128))
    BT, KT = T2("BT"), T2("KT")
    yb = T2("yb")
    ps_big = [c.ps([128, TS], F32, "rps_big") for _ in range(2)]
    big_i = [0]

    def nbig():
        p = ps_big[big_i[0] % 2]
        big_i[0] += 1
        return p
    banks = [c.ps([128, 512], F32, f"rps_bank{i_}") for i_ in range(5)]
    sps = {}
    sps["tr0"] = SubTk(banks[0], banks[0].h[0:64, 0:128], "tr0")
    sps["tr1"] = SubTk(banks[0], banks[0].h[0:64, 128:256], "tr1")
    sps["zn"] = SubTk(banks[0], banks[0].h[:, 256:320], "zn")
    for h_ in range(2):
        bi, bs = banks[1 + h_], banks[3 + h_]
        sps[("g1", h_)] = SubTk(bi, bi.h[0:64, 0:128], "g1")
        sps[("g2", h_)] = SubTk(bi, bi.h[0:64, 128:256], "g2")
        for i_, n_ in enumerate(("n", "p", "pt", "m")):
            sps[(n_, h_)] = SubTk(bi, bi.h[0:64, 256 + 64 * i_:256 + 64 * i_ + 64], n_)
        for i_, n_ in enumerate(("x", "u", "y", "x2", "y2")):
            sps[(n_, h_)] = SubTk(bs, bs.h[0:64, 64 * i_:64 * i_ + 64], n_)
    Z = [c.sb([128, 64], F32, "Z") for _ in range(2)]
    c.op("dve", lambda e: e.memset(Z[0][:], 0.0), w=[Z[0]])
    c.op("dve", lambda e: e.memset(Z[1][:], 0.0), w=[Z[1]])
    btok = [[c.sb([64, 128], F32, "btok") for _ in range(2)] for _ in range(2)]
    ktok = [[c.sb([64, 128], F32, "ktok") for _ in range(2)] for _ in range(2)]
    for lst in (btok, ktok):
        for pr in lst:
            for t_ in pr:
                c.op("pool", lambda e: e.memset(t_[:], 0.0), w=[t_])
    vtok = [c.sb([64, 128], F32, "vtok") for _ in range(2)]
    Gb = [[c.sb([64, 128], F32, "Gb") for _ in range(2)] for _ in range(2)]
    Gk = [[c.sb([64, 128], F32, "Gk") for _ in range(2)] for _ in range(2)]
    Pm = [[c.sb([64, 64], F32, "Pm") for _ in range(2)] for _ in range(2)]
    PTm = [[c.sb([64, 64], F32, "PTm") for _ in range(2)] for _ in range(2)]
    MT = [[c.sb([64, 64], F32, "MT") for _ in range(2)] for _ in range(2)]
    Xs = [c.sb([64, 64], F32, "Xs") for _ in range(2)]
    Us = [c.sb([64, 64], F32, "Us") for _ in range(2)]
    ytmp = [c.sb([64, 64], F32, "ytmp") for _ in range(2)]
    v3 = lambda ap: ap.rearrange("p (c t) -> p c t", t=64)
    def precompute(tl):
        pb = tl % 2
        k2 = k2s[pb]
        t0 = tl * TS
        pb = tl % 2
        x3, wa, gl = X3[pb], WA[pb], GL[pb]
        for (buf, src, P_) in ((x3, d["rkv"].h.rearrange("k p s -> p k s"), 128), (wa, d["wa"].h.rearrange("k p s -> p k s"), 64), (gl, d["gl"].h.rearrange("k p s -> p k s"), 128)):
            srct = {id(x3): d["rkv"], id(wa): d["wa"], id(gl): d["gl"]}[id(buf)]
            if t0 == 0:
                c.op("pool", lambda e: e.memset(buf[:, :, 0:1], 0.0), w=[buf])
                c.dma("sp", buf[:, :, 1:TS + 1], src[:, :, 0:TS], r=[srct], w=[buf])
            else:
                c.dma("sp", buf[:, :, :], src[:, :, t0 - 1:t0 + TS], r=[srct], w=[buf])
        def lerp(buf, k, P_, mu_ap, mu_tk, out_ap, out_tk, dt_):
            c.op("pool", lambda e: e.tensor_tensor(out=dt_[:P_, :], in0=buf[:P_, k, 0:TS], in1=buf[:P_, k, 1:TS + 1], op=ALU.subtract), r=[buf], w=[dt_])
            c.op("dve", lambda e: e.scalar_tensor_tensor(out=out_ap, in0=dt_[:P_, :], scalar=mu_ap, in1=buf[:P_, k, 1:TS + 1], op0=ALU.mult, op1=ALU.add), r=[dt_, buf, mu_tk], w=[out_tk])
        r_, k_, v_ = rm[pb], km[pb], vm[pb]
        lerp(x3, 0, 128, mu3[:, 0:1], mu3, r_[:, :], r_, dtmp[0])
        lerp(x3, 1, 128, mu3[:, 1:2], mu3, k_[:, :], k_, dtmp[1])
        lerp(x3, 2, 128, mu3[:, 2:3], mu3, v_[:, :], v_, dtmp[0])
        lerp(wa, 0, 64, mu2[:, 0:1], mu2, wlm[:, :], wlm, dtmp[1])
        lerp(wa, 1, 64, mu2[:, 1:2], mu2, alm[:, :], alm, dtmp[0])
        lerp(gl, 0, 128, mug[:, 0:1], mug, glm[:, 0, :], glm, dtmp[1])
        lerp(gl, 1, 128, mug[:, 1:2], mug, glm[:, 1, :], glm, dtmp[0])
        c.op("act", lambda e: e.activation(out=th[:], in_=wlm[:], func=AF.Tanh), r=[wlm], w=[th])
        c.op("act", lambda e: e.activation(out=sgl[:], in_=glm[:], func=AF.Sigmoid), r=[glm], w=[sgl])
        pw = nbig()
        c.op("pe", lambda e: e.matmul(pw[:, :], lhsT=w2b[:], rhs=th[:], start=True, stop=True), r=[w2b, th], w=[pw])
        c.op("act", lambda e: e.activation(out=logw[:], in_=pw[:], func=AF.Sigmoid, bias=W0, scale=1.0), r=[pw, vec], w=[logw])
        c.op("dve", lambda e: e.tensor_scalar(out=logw[:], in0=logw[:], scalar1=-float(np.exp(-0.5)), scalar2=None, op0=ALU.mult), r=[logw], w=[logw])
        pa = nbig()
        c.op("pe", lambda e: e.matmul(pa[:, :], lhsT=a2b[:], rhs=alm[:], start=True, stop=True), r=[a2b, alm], w=[pa])
        c.op("act", lambda e: e.activation(out=av_[:], in_=pa[:], func=AF.Sigmoid, bias=A0, scale=1.0), r=[pa, vec], w=[av_])
        pg = nbig()
        for kx in range(2):
            c.op("pe", lambda e: e.matmul(pg[:, :], lhsT=g2b[:, kx, :], rhs=sgl[:, kx, :], start=(kx == 0), stop=(kx == 1)), r=[g2b, sgl], w=[pg])
        g_ = gg[pb]
        c.op("act", lambda e: e.activation(out=g_[:], in_=pg[:], func=AF.Copy), r=[pg], w=[g_])
        c.op("dve", lambda e: e.tensor_scalar(out=kk[:], in0=k_[:], scalar1=KK_, scalar2=None, op0=ALU.mult), r=[k_, vec], w=[kk])
        c.op("act", lambda e: e.activation(out=kk2[:], in_=kk[:], func=AF.Square), r=[kk], w=[kk2])
        pss = nbig()
        c.op("pe", lambda e: e.matmul(pss[:, :], lhsT=bones[:], rhs=kk2[:], start=True, stop=True), r=[bones, kk2], w=[pss])
        c.op("act", lambda e: e.activation(out=rn[:], in_=pss[:], func=AF.Ln, bias=1e-24, scale=1.0), r=[pss], w=[rn])
        c.op("act", lambda e: e.activation(out=rn[:], in_=rn[:], func=AF.Exp, scale=-0.5), r=[rn], w=[rn])
        c.op("dve", lambda e: e.tensor_tensor(out=kkn[:], in0=kk[:], in1=rn[:], op=ALU.mult), r=[kk, rn], w=[kkn])
        c.op("dve", lambda e: e.tensor_scalar(out=tmpk[:], in0=av_[:], scalar1=KA, scalar2=omk[:, 0:1], op0=ALU.mult, op1=ALU.add), r=[av_, vec, omk], w=[tmpk])
        c.op("dve", lambda e: e.tensor_tensor(out=k2[:], in0=k_[:], in1=tmpk[:], op=ALU.mult), r=[k_, tmpk], w=[k2])
        c.op("dve", lambda e: e.tensor_tensor(out=bv[:], in0=kkn[:], in1=av_[:], op=ALU.mult), r=[kkn, av_], w=[bv])
        src = logw
        for si, s_ in enumerate((1, 2, 4, 8, 16, 32)):
            dst = cA if si % 2 == 0 else cB
            c.op("pool", lambda e: e.tensor_tensor(out=v3(dst[:, :])[:, :, s_:], in0=v3(src[:, :])[:, :, s_:], in1=v3(src[:, :])[:, :, :64 - s_], op=ALU.add), r=[src], w=[dst])
            c.op("pool", lambda e: e.tensor_copy(out=v3(dst[:, :])[:, :, :s_], in_=v3(src[:, :])[:, :, :s_]), r=[src], w=[dst])
            src = dst
        cum = src
        wi = Winc[pb]
        c.op("act", lambda e: e.activation(out=wi[:], in_=cum[:], func=AF.Exp), r=[cum], w=[wi])
        c.op("act", lambda e: e.activation(out=Einv[:], in_=cum[:], func=AF.Exp, scale=-1.0), r=[cum], w=[Einv])
        c.op("pool", lambda e: e.tensor_tensor(out=cA[:], in0=cum[:], in1=logw[:], op=ALU.subtract), r=[cum, logw], w=[cA])
        c.op("act", lambda e: e.activation(out=Wprev[:], in_=cA[:], func=AF.Exp), r=[cA], w=[Wprev])
        ar, bt, kt = AR[pb], BT[pb], KT[pb]
        c.op("dve", lambda e: e.tensor_tensor(out=ar[:, :, 64:128], in0=v3(r_[:, :]), in1=v3(wi[:, :]), op=ALU.mult), r=[r_, wi], w=[ar])
        c.op("dve", lambda e: e.scalar_tensor_tensor(out=ar[:, :, 0:64], in0=v3(kkn[:, :]), scalar=-1.0, in1=v3(Wprev[:, :]), op0=ALU.mult, op1=ALU.mult), r=[kkn, Wprev], w=[ar])
        c.op("dve", lambda e: e.tensor_tensor(out=bt[:], in0=bv[:], in1=Einv[:], op=ALU.mult), r=[bv, Einv], w=[bt])
        c.op("dve", lambda e: e.tensor_tensor(out=kt[:], in0=k2[:], in1=Einv[:], op=ALU.mult), r=[k2, Einv], w=[kt])

    def postproc(tl):
        pb = tl % 2
        t0 = tl * TS
        k2 = k2s[pb]
        r_, v_, g_, y_ = rm[pb], vm[pb], gg[pb], yb[pb]
        pm = nbig()
        c.op("pe", lambda e: e.matmul(pm[:, :], lhsT=bavg[:], rhs=y_[:], start=True, stop=True), r=[bavg, y_], w=[pm])
        c.op("dve", lambda e: e.tensor_tensor(out=y_[:], in0=y_[:], in1=pm[:], op=ALU.subtract), r=[y_, pm], w=[y_])
        c.op("act", lambda e: e.activation(out=kk2[:], in_=y_[:], func=AF.Square), r=[y_], w=[kk2])
        pv = nbig()
        c.op("pe", lambda e: e.matmul(pv[:, :], lhsT=bavg[:], rhs=kk2[:], start=True, stop=True), r=[bavg, kk2], w=[pv])
        c.op("act", lambda e: e.activation(out=rn[:], in_=pv[:], func=AF.Ln, bias=GN_EPS, scale=1.0), r=[pv], w=[rn])
        c.op("act", lambda e: e.activation(out=rn[:], in_=rn[:], func=AF.Exp, scale=-0.5), r=[rn], w=[rn])
        c.op("dve", lambda e: e.tensor_tensor(out=y_[:], in0=y_[:], in1=rn[:], op=ALU.mult), r=[y_, rn], w=[y_])
        c.op("dve", lambda e: e.tensor_scalar(out=y_[:], in0=y_[:], scalar1=LNW, scalar2=LNB, op0=ALU.mult, op1=ALU.add), r=[y_, vec], w=[y_])
        c.op("dve", lambda e: e.scalar_tensor_tensor(out=kk[:], in0=r_[:], scalar=RK, in1=k2[:], op0=ALU.mult, op1=ALU.mult), r=[r_, vec, k2], w=[kk])
        pb_ = nbig()
        c.op("pe", lambda e: e.matmul(pb_[:, :], lhsT=bones[:], rhs=kk[:], start=True, stop=True), r=[bones, kk], w=[pb_])
        c.op("dve", lambda e: e.tensor_tensor(out=tmpk[:], in0=pb_[:], in1=v_[:], op=ALU.mult), r=[pb_, v_], w=[tmpk])
        c.op("dve", lambda e: e.tensor_tensor(out=y_[:], in0=y_[:], in1=tmpk[:], op=ALU.add), r=[y_, tmpk], w=[y_])
        c.op("dve", lambda e: e.tensor_tensor(out=y_[:], in0=y_[:], in1=g_[:], op=ALU.mult), r=[y_, g_], w=[y_])
        c.dma("sp", od[row0:row0 + 128, t0:t0 + TS], y_[:], r=[y_], w=[od])


    def gen_T(tl, ch):
        pb, cp = tl % 2, ch % 2
        bt, kt, v_ = BT[pb], KT[pb], vm[pb]
        cs = slice(ch * 64, ch * 64 + 64)
        c.op("pe", lambda e: e.matmul(sps["tr0"][:, :], lhsT=bt[:, cs], rhs=identf[:, :], start=True, stop=True), r=[bt, identf], w=[sps["tr0"]])
        for h in range(2):
            c.op("act", lambda e: e.activation(out=btok[cp][h][:, 64 * h:64 * h + 64], in_=sps["tr0"][:, 64 * h:64 * h + 64], func=AF.Copy), r=[sps["tr0"]], w=[btok[cp][h]])
        yield
        c.op("pe", lambda e: e.matmul(sps["tr1"][:, :], lhsT=kt[:, cs], rhs=identf[:, :], start=True, stop=True), r=[kt, identf], w=[sps["tr1"]])
        for h in range(2):
            c.op("act", lambda e: e.activation(out=ktok[cp][h][:, 64 * h:64 * h + 64], in_=sps["tr1"][:, 64 * h:64 * h + 64], func=AF.Copy), r=[sps["tr1"]], w=[ktok[cp][h]])
        yield
        c.op("pe", lambda e: e.matmul(sps["tr0"][:, :], lhsT=v_[:, cs], rhs=identf[:, :], start=True, stop=True), r=[v_, identf], w=[sps["tr0"]])
        c.op("act", lambda e: e.activation(out=vtok[cp][:], in_=sps["tr0"][:, :], func=AF.Copy), r=[sps["tr0"]], w=[vtok[cp]])
        yield

    def gen_I(tl, ch, h):
        pb, cp = tl % 2, ch % 2
        ar, bt, kt = AR[pb], BT[pb], KT[pb]
        cs = slice(ch * 64, ch * 64 + 64)
        hp = slice(64 * h, 64 * h + 64)
        gb, gk, mt = Gb[cp][h], Gk[cp][h], MT[cp][h]
        P = lambda n_: sps[(n_, h)]
        c.op("pe", lambda e: e.matmul(P("g1")[:, :], lhsT=bt[hp, cs], rhs=ar[hp, ch, :], start=True, stop=True), r=[bt, ar], w=[P("g1")])
        c.op("dve", lambda e: e.tensor_tensor(out=gb[:], in0=P("g1")[:, :], in1=maskT[:], op=ALU.mult), r=[P("g1"), maskT], w=[gb])
        yield
        c.op("pe", lambda e: e.matmul(P("g2")[:, :], lhsT=kt[hp, cs], rhs=ar[hp, ch, :], start=True, stop=True), r=[kt, ar], w=[P("g2")])
        c.op("dve", lambda e: e.tensor_tensor(out=gk[:], in0=P("g2")[:, :], in1=maskT[:], op=ALU.mult), r=[P("g2"), maskT], w=[gk])
        yield
        c.op("pe", lambda e: e.matmul(P("n")[:, :], lhsT=ar[hp, ch, 0:64], rhs=bt[hp, cs], start=True, stop=True), r=[ar, bt], w=[P("n")])
        P_, PT_ = Pm[h][0], PTm[h][0]
        c.op("dve", lambda e: e.tensor_tensor(out=P_[:], in0=P("n")[:, :], in1=maskSL[:], op=ALU.mult), r=[P("n"), maskSL], w=[P_])
        c.op("act", lambda e: e.activation(out=PT_[:], in_=gb[:, 0:64], func=AF.Copy), r=[gb], w=[PT_])
        c.op("dve", lambda e: e.tensor_tensor(out=mt[:], in0=gb[:, 0:64], in1=identf[0:64, 0:64], op=ALU.add), r=[gb, identf], w=[mt])
        yield
        for kx in range(5):
            Pn, PTn = Pm[h][(kx + 1) % 2], PTm[h][(kx + 1) % 2]
            c.op("pe", lambda e: e.matmul(P("p")[:, :], lhsT=PT_[:], rhs=P_[:], start=True, stop=True), r=[PT_, P_], w=[P("p")])
            c.op("act", lambda e: e.activation(out=Pn[:], in_=P("p")[:, :], func=AF.Copy), r=[P("p")], w=[Pn])
            yield
            if kx < 4:
                c.op("pe", lambda e: e.matmul(P("pt")[:, :], lhsT=P_[:], rhs=PT_[:], start=True, stop=True), r=[PT_, P_], w=[P("pt")])
                c.op("act", lambda e: e.activation(out=PTn[:], in_=P("pt")[:, :], func=AF.Copy), r=[P("pt")], w=[PTn])
                yield
            c.op("pe", lambda e: e.matmul(P("m")[:, :], lhsT=Pn[:], rhs=mt[:], start=True, stop=True), r=[Pn, mt], w=[P("m")])
            c.op("dve", lambda e: e.tensor_tensor(out=mt[:], in0=mt[:], in1=P("m")[:, :], op=ALU.add), r=[mt, P("m")], w=[mt])
            yield
            P_, PT_ = Pn, PTn

    def gen_S(tl, ch, h, zo):
        pb, cp = tl % 2, ch % 2
        ar, y_ = AR[pb], yb[pb]
        cs = slice(ch * 64, ch * 64 + 64)
        hp = slice(64 * h, 64 * h + 64)
        gb, gk, mt, xs_, us_, vt_ = Gb[cp][h], Gk[cp][h], MT[cp][h], Xs[h], Us[h], vtok[cp]
        P = lambda n_: sps[(n_, h)]
        c.op("pe", lambda e: e.matmul(P("x")[:, :], lhsT=ar[hp, ch, 0:64], rhs=zo[hp, :], start=True, stop=True), r=[ar, zo], w=[P("x")])
        c.op("act", lambda e: e.activation(out=xs_[:], in_=P("x")[:, :], func=AF.Copy), r=[P("x")], w=[xs_])
        yield
        c.op("pe", lambda e: e.matmul(P("x2")[:, :], lhsT=gk[:, 0:64], rhs=vt_[:, hp], start=True, stop=True), r=[gk, vt_], w=[P("x2")])
        c.op("dve", lambda e: e.tensor_tensor(out=xs_[:], in0=xs_[:], in1=P("x2")[:, :], op=ALU.add), r=[xs_, P("x2")], w=[xs_])
        yield
        c.op("pe", lambda e: e.matmul(P("u")[:, :], lhsT=mt[:], rhs=xs_[:], start=True, stop=True), r=[mt, xs_], w=[P("u")])
        c.op("act", lambda e: e.activation(out=us_[:], in_=P("u")[:, :], func=AF.Copy), r=[P("u")], w=[us_])
        yield
        yt_ = ytmp[h]
        c.op("pe", lambda e: e.matmul(P("y")[:, :], lhsT=zo[hp, :], rhs=ar[hp, ch, 64:128], start=True, stop=True), r=[zo, ar], w=[P("y")])
        c.op("act", lambda e: e.activation(out=yt_[:], in_=P("y")[:, :], func=AF.Copy), r=[P("y")], w=[yt_])
        yield
        c.op("pe", lambda e: e.matmul(P("y2")[:, :], lhsT=us_[:], rhs=gb[:, 64:128], start=True, stop=False), r=[us_, gb], w=[P("y2")])
        c.op("pe", lambda e: e.matmul(P("y2")[:, :], lhsT=vt_[:, hp], rhs=gk[:, 64:128], start=False, stop=True), r=[vt_, gk], w=[P("y2")])
        c.op("dve", lambda e: e.tensor_tensor(out=y_[hp, cs], in0=yt_[:], in1=P("y2")[:, :], op=ALU.add), r=[yt_, P("y2")], w=[y_])
        yield

    def z_update(tl, ch, zo, zn):
        pb, cp = tl % 2, ch % 2
        wi, vt_ = Winc[pb], vtok[cp]
        for h in range(2):
            c.op("pe", lambda e: e.matmul(sps["zn"][:, :], lhsT=btok[cp][h][:], rhs=Us[h][:], start=(h == 0), stop=False), r=[btok[cp][h], Us[h]], w=[sps["zn"]])
            c.op("pe", lambda e: e.matmul(sps["zn"][:, :], lhsT=ktok[cp][h][:], rhs=vt_[:, 64 * h:64 * h + 64], start=False, stop=(h == 1)), r=[ktok[cp][h], vt_], w=[sps["zn"]])
        c.op("dve", lambda e: e.tensor_tensor(out=zn[:], in0=zo[:], in1=sps["zn"][:, :], op=ALU.add), r=[zo, sps["zn"]], w=[zn])
        c.op("dve", lambda e: e.tensor_scalar(out=zn[:], in0=zn[:], scalar1=wi[:, ch * 64 + 63:ch * 64 + 64], scalar2=None, op0=ALU.mult), r=[zn, wi], w=[zn])

    def interleave(gens):
        gens = list(gens)
        while gens:
            for g_ in list(gens):
                try:
                    next(g_)
                except StopIteration:
                    gens.remove(g_)

    chunks = [(tl, ch) for tl in range(NTL) for ch in range(NCH)]
    precompute(0)
    interleave([gen_T(0, 0), gen_I(0, 0, 0), gen_I(0, 0, 1)])
    for gi, (tl, ch) in enumerate(chunks):
        zo, zn = Z[gi % 2], Z[(gi + 1) % 2]
        gens = [gen_S(tl, ch, 0, zo), gen_S(tl, ch, 1, zo)]
        if gi + 1 < len(chunks):
            ntl, nch = chunks[gi + 1]
            if nch == 0:
                precompute(ntl)
            gens = [gen_T(ntl, nch)] + gens + [gen_I(ntl, nch, 0), gen_I(ntl, nch, 1)]
        interleave(gens)
        z_update(tl, ch, zo, zn)
        if ch == NCH - 1:
            postproc(tl)


def rwkv_dram(c, S):
    EI = "ExternalInput"
    d = {}
    for n_, sh in (("rkv", [3, 128, S]), ("wa", [2, 64, S]), ("gl", [2, 128, S]), ("vecs", [128, 8]), ("mu_rkv", [128, 3]), ("mu_wa", [64, 2]), ("mu_gl", [128, 2]),
                   ("w2", [64, 128]), ("a2", [64, 128]), ("g2", [2, 128, 128])):
        d[n_] = c.dram(n_, sh, F32, EI)
    return d


def build_rwkv_program(S=8192):
    nc = bass.Bass("TRN2", target_bir_lowering=False)
    st = ExitStack()
    c = Ctx(nc, st)
    d = rwkv_dram(c, S)
    od = c.dram("oT", [128, S], F32, "ExternalOutput")
    identf = make_ident(c, F32)
    emit_rwkv(c, d, od, S, identf)
    c.finish()
    return nc, st


def rwkv_host_inputs(rw_T, hg, mu, w0, w2, a0, a2, g2, k_k, k_a, r_k, ln_w, ln_b):
    S = rw_T.shape[1]
    ch = slice(hg * 128, hg * 128 + 128)
    rkv = np.stack([rw_T[0:512][ch], rw_T[512:1024][ch], rw_T[1024:1536][ch]])
    wa = np.stack([rw_T[1536:1600], rw_T[1600:1664]])
    gl = np.zeros((2, 128, S), np.float32)
    gl[0] = rw_T[1664:1792]
    gl[1, :32] = rw_T[1792:1824]
    vecs = np.zeros((128, 8), np.float32)
    for i, v in ((0, w0), (1, a0), (2, k_k), (3, k_a), (5, r_k.reshape(-1)), (6, ln_w), (7, ln_b)):
        vecs[:, i] = v[ch]
    mu_rkv = np.stack([mu[0:512][ch], mu[512:1024][ch], mu[1024:1536][ch]], axis=1)
    mu_wa = np.stack([mu[1536:1600], mu[1600:1664]], axis=1)
    mu_gl = np.zeros((128, 2), np.float32)
    mu_gl[:, 0] = mu[1664:1792]
    mu_gl[:32, 1] = mu[1792:1824]
    g2p = np.zeros((2, 128, 128), np.float32)
    g2p[0] = g2[0:128, ch]
    g2p[1, :32] = g2[128:160, ch]
    return dict(rkv=np.ascontiguousarray(rkv), wa=np.ascontiguousarray(wa), gl=gl, vecs=vecs, mu_rkv=np.ascontiguousarray(mu_rkv), mu_wa=np.ascontiguousarray(mu_wa),
                mu_gl=mu_gl, w2=np.ascontiguousarray(w2[:, ch]), a2=np.ascontiguousarray(a2[:, ch]), g2=g2p)


_PROGS = {}


def _prog(key, builder):
    if key not in _PROGS:
        _PROGS[key] = builder()
    return _PROGS[key][0]


def _run(nc, in_maps):
    res = run_bass_kernel_spmd(nc, in_maps, core_ids=list(range(8)))
    return res.results


B_, S_, D_ = 2, 8192, 1024
TC = 2048
HALO = 32


def _with_halo(xT, i):
    C = xT.shape[0]
    out = np.zeros((C, HALO + TC), np.float32)
    lo = i * TC - HALO
    if lo >= 0:
        out[:] = xT[:, lo:(i + 1) * TC]
    else:
        out[:, HALO:] = xT[:, 0:TC]
    return out


def kernel(x, p, positions, attn_norm, ffn_norm, ffn_w_in, ffn_conv_w, ffn_conv_b, ffn_w_out,
           ple_w_proj, ple_norm, ple_gate_norm, ple_w_gate,
           hyb_w_in, hyb_w_out, rw_mu, rw_w0, rw_w2, rw_a0, rw_a2, rw_g2, rw_k_k, rw_k_a,
           rw_r_k, rw_ln_w, rw_ln_b,
           mla_w_down, mla_q_norm, mla_kv_norm, mla_w_uq, mla_w_ukv, mla_w_o, final_norm):
    f32 = lambda a: np.ascontiguousarray(np.asarray(a), dtype=np.float32)
    x = f32(x)
    p = f32(p)
    positions = np.asarray(positions).astype(np.int32)
    cores = [(b, i) for b in range(B_) for i in range(4)]
    hT = [np.ascontiguousarray(x[b].T) for b in range(B_)]
    DEPTH = 4

    def next_inputs(layer):
        if layer >= DEPTH:
            return "final", dict(g_fin=chunked_vec(f32(final_norm)))
        j = layer // 2
        if layer % 2 == 0:
            return "even", dict(Whyb=blocked(f32(hyb_w_in[j])), g_attn=chunked_vec(f32(attn_norm[layer])))
        return "odd", dict(Wdn=blocked(perm_down(f32(mla_w_down[j]))), Wuq=blocked(perm_uq(f32(mla_w_uq[j]))), Wukv=blocked(f32(mla_w_ukv[j])),
                           g_attn=chunked_vec(f32(attn_norm[layer])), g_q=chunked_vec(f32(mla_q_norm[j])), g_kv=chunked_vec(f32(mla_kv_norm[j])),
                           invf=INVF)

    def run_token(layer_done, oT):
        nxt_layer = 0 if layer_done is None else layer_done + 1
        nxt, wn = next_inputs(nxt_layer)
        mix_in = layer_done is not None
        nc = _prog(("tok", mix_in, nxt), lambda: build_token_program(mix_in, nxt))
        common = dict(wn)
        if mix_in:
            L = layer_done
            j = L // 2
            w_mo = f32(hyb_w_out[j]) if L % 2 == 0 else f32(mla_w_o[j])
            common.update(Wmo=blocked(w_mo), Wfi=blocked(f32(ffn_w_in[L])), Wfo=blocked(f32(ffn_w_out[L])), Wpp=blocked(f32(ple_w_proj[L])),
                          Wpg=blocked(f32(ple_w_gate[L])), g_ffn=chunked_vec(f32(ffn_norm[L])), g_pe=chunked_vec(f32(ple_norm[L])),
                          g_pg=chunked_vec(f32(ple_gate_norm[L])),
                          convw=np.ascontiguousarray(f32(ffn_conv_w[L]).reshape(3, 22, 128).transpose(2, 0, 1)), convb=chunked_vec(f32(ffn_conv_b[L])))
        in_maps = []
        for (b, i) in cores:
            m = dict(common)
            m["hT"] = _with_halo(hT[b], i).reshape(8, 128, HALO + TC)
            if mix_in:
                m["oT"] = _with_halo(oT[b], i).reshape(8, 128, HALO + TC)
                m["pT"] = _with_halo(np.ascontiguousarray(p[layer_done, b].T), i).reshape(2, 128, HALO + TC)
            if nxt == "odd":
                m["pos"] = np.ascontiguousarray(np.broadcast_to(positions[b, i * TC:(i + 1) * TC][None], (128, TC)))
            in_maps.append(m)
        res = _run(nc, in_maps)
        out = {}
        if mix_in:
            for b in range(B_):
                hT[b] = np.concatenate([res[b * 4 + i]["hT_out"].reshape(1024, TC) for i in range(4)], axis=1)
        for key in ("projT", "qT", "kvT", "krT", "yT"):
            if key in res[0]:
                out[key] = [np.concatenate([res[b * 4 + i][key].reshape(-1, TC) for i in range(4)], axis=1) for b in range(B_)]
        return out

    out = run_token(None, None)
    for L in range(DEPTH):
        j = L // 2
        oT = [np.zeros((1024, S_), np.float32) for _ in range(B_)]
        if L % 2 == 0:
            proj = out["projT"]
            nc_sb = _prog(("sb",), lambda: build_sb_program(S_, 2))
            in_maps = []
            for b in range(B_):
                for g in range(4):
                    sl = lambda base: np.ascontiguousarray(proj[b][base + g * 128: base + (g + 1) * 128].reshape(2, 64, S_))
                    in_maps.append(dict(q=sl(0), k=sl(512), v=sl(1024)))
            res = _run(nc_sb, in_maps)
            for b in range(B_):
                for g in range(4):
                    oT[b][g * 128:(g + 1) * 128] = res[b * 4 + g]["oT"]
            nc_rw = _prog(("rw",), lambda: build_rwkv_program(S_))
            prm = [f32(a[j]) for a in (rw_mu, rw_w0, rw_w2, rw_a0, rw_a2, rw_g2, rw_k_k, rw_k_a, rw_r_k, rw_ln_w, rw_ln_b)]
            in_maps = []
            for b in range(B_):
                rwT = proj[b][1536:3360]
                for g in range(4):
                    in_maps.append(rwkv_host_inputs(rwT, g, *prm))
            res = _run(nc_rw, in_maps)
            for b in range(B_):
                for g in range(4):
                    oT[b][512 + g * 128: 512 + (g + 1) * 128] = res[b * 4 + g]["oT"]
        else:
            qT, kvT, krT = out["qT"], out["kvT"], out["krT"]
            nc_mla = _prog(("mla",), lambda: build_mla_program(S_, 4))
            in_maps = []
            for b in range(B_):
                qn = qT[b][0:1024].reshape(16, 64, S_)
                x1 = qT[b][1024:1280].reshape(16, 16, S_)
                x2 = qT[b][1280:1536].reshape(16, 16, S_)
                kv = kvT[b].reshape(16, 128, S_)
                kr = krT[b]
                for g in range(4):
                    hs = slice(4 * g, 4 * g + 4)
                    q = np.concatenate([qn[hs], x1[hs], x2[hs]], axis=1)
                    k = np.concatenate([kv[hs, :64], np.broadcast_to(kr[None], (4, 32, S_))], axis=1)
                    in_maps.append(dict(q=np.ascontiguousarray(q), k=np.ascontiguousarray(k), v=np.ascontiguousarray(kv[hs, 64:])))
            res = _run(nc_mla, in_maps)
            for b in range(B_):
                for g in range(4):
                    oT[b][g * 256:(g + 1) * 256] = res[b * 4 + g]["oT"].reshape(256, S_)
        out = run_token(L, oT)
    y = np.stack([np.ascontiguousarray(out["yT"][b].T) for b in range(B_)]).astype(np.float32)
    return y
```
